# Optimizing a Trainium2 kernel written in Bass

```python
import jax, jax.numpy as jnp
from jax import lax
import numpy as np

D_MODEL = 1024
BATCH = 8
SEQ = 2048
DEPTH = 2

GRID_W = 64
CTX_LEN = 256
Q_BLOCK = 128
ROPE_THETA = 10000.0
EPS = 1e-6

GQA_HEADS = 8
GQA_KV_HEADS = 2
GQA_GROUP = GQA_HEADS // GQA_KV_HEADS
GQA_HEAD_DIM = 64
GQA_SCALE = GQA_HEAD_DIM ** -0.5
MLA_HEADS = 8
MLA_Q_RANK = 384
MLA_KV_RANK = 256
MLA_NOPE_DIM = 64
MLA_ROPE_DIM = 32
MLA_V_DIM = 64
MLA_SCALE = (MLA_NOPE_DIM + MLA_ROPE_DIM) ** -0.5
MLA_UQ_WIDTH = MLA_HEADS * (MLA_NOPE_DIM + MLA_ROPE_DIM)
MLA_UKV_WIDTH = MLA_HEADS * (MLA_NOPE_DIM + MLA_V_DIM)
FOURIER_GROUPS = 4
FOURIER_GROUP_DIM = 128
FOURIER_WIDTH = FOURIER_GROUPS * FOURIER_GROUP_DIM
N_BRANCH = 3
FFN_HIDDEN = -(-8 * D_MODEL // (3 * 256)) * 256

IN_WIDTHS = (
    GQA_HEADS * GQA_HEAD_DIM,
    GQA_KV_HEADS * GQA_HEAD_DIM,
    GQA_KV_HEADS * GQA_HEAD_DIM,
    MLA_Q_RANK,
    MLA_KV_RANK,
    MLA_ROPE_DIM,
    FOURIER_WIDTH,
    N_BRANCH * D_MODEL,
)
IN_WIDTH = sum(IN_WIDTHS)

kernel_name = "hybrid_gqa_mla_fnet_prefix_dit"


def rms_norm(x, g):
    xf = x.astype(jnp.float32)
    y = xf * lax.rsqrt(jnp.mean(xf * xf, axis=-1, keepdims=True) + EPS)
    return (y * g.astype(jnp.float32)).astype(x.dtype)


def modulate(x, g, shift, scale):
    return rms_norm(x, g) * (1 + scale) + shift


def split_in(p):
    idx, acc = [], 0
    for w in IN_WIDTHS[:-1]:
        acc += w
        idx.append(acc)
    return jnp.split(p, idx, axis=-1)


def axial_rope_tables(rows, dim):
    row_id = jnp.repeat(jnp.arange(rows), GRID_W)
    col_id = jnp.tile(jnp.arange(GRID_W), rows)
    half = dim // 2
    freqs = ROPE_THETA ** (-jnp.arange(0, half, 2, dtype=jnp.float32) / half)

    def axis_angles(pos):
        ang = pos.astype(jnp.float32)[:, None] * freqs[None, :]
        return jnp.concatenate([ang, ang], axis=-1)

    ang = jnp.concatenate([axis_angles(row_id), axis_angles(col_id)], axis=-1)
    return jnp.cos(ang), jnp.sin(ang)


def apply_rope(x, cos, sin):
    dim = x.shape[-1]
    q = dim // 4
    xr = x.reshape(*x.shape[:-1], 2, 2, q)
    rot = jnp.concatenate([-xr[..., 1:2, :], xr[..., 0:1, :]], axis=-2).reshape(x.shape)
    out = x.astype(jnp.float32) * cos[None, :, None, :] + rot.astype(jnp.float32) * sin[None, :, None, :]
    return out.astype(x.dtype)


def attend(q, k, v, scale):
    B, N = q.shape[:2]
    nb = N // Q_BLOCK
    qb = jnp.moveaxis(q.reshape(B, nb, Q_BLOCK, *q.shape[2:]), 1, 0)

    def one_block(qblk):
        s = jnp.einsum('bqkgd,bmkd->bkgqm', qblk, k, preferred_element_type=jnp.float32) * scale
        p = jax.nn.softmax(s, axis=-1).astype(v.dtype)
        return jnp.einsum('bkgqm,bmke->bqkge', p, v)

    o = lax.map(one_block, qb)
    o = jnp.moveaxis(o, 0, 1)
    return o.reshape(B, N, -1)


def gqa_queries(pq, g_q, rope):
    B, N, _ = pq.shape
    q = rms_norm(pq.reshape(B, N, GQA_HEADS, GQA_HEAD_DIM), g_q)
    if rope is not None:
        q = apply_rope(q, *rope)
    return q.reshape(B, N, GQA_KV_HEADS, GQA_GROUP, GQA_HEAD_DIM)


def gqa_keys_values(pk, pv, g_k, rope):
    B, N, _ = pk.shape
    k = rms_norm(pk.reshape(B, N, GQA_KV_HEADS, GQA_HEAD_DIM), g_k)
    if rope is not None:
        k = apply_rope(k, *rope)
    v = pv.reshape(B, N, GQA_KV_HEADS, GQA_HEAD_DIM)
    return k, v


def mla_queries(pcq, g_cq, w_uq, g_qn, g_qr, rope):
    B, N, _ = pcq.shape
    q = (rms_norm(pcq, g_cq) @ w_uq).reshape(B, N, MLA_HEADS, MLA_NOPE_DIM + MLA_ROPE_DIM)
    qn = rms_norm(q[..., :MLA_NOPE_DIM], g_qn)
    qr = rms_norm(q[..., MLA_NOPE_DIM:], g_qr)
    if rope is not None:
        qr = apply_rope(qr, *rope)
    return jnp.concatenate([qn, qr], axis=-1)[:, :, :, None, :]


def mla_keys_values(pckv, pkr, g_ckv, w_ukv, g_kn, g_kr, rope):
    B, N, _ = pckv.shape
    kv = (rms_norm(pckv, g_ckv) @ w_ukv).reshape(B, N, MLA_HEADS, MLA_NOPE_DIM + MLA_V_DIM)
    kn = rms_norm(kv[..., :MLA_NOPE_DIM], g_kn)
    v = kv[..., MLA_NOPE_DIM:]
    kr = rms_norm(pkr, g_kr)[:, :, None, :]
    if rope is not None:
        kr = apply_rope(kr, *rope)
    k = jnp.concatenate([kn, jnp.broadcast_to(kr, (B, N, MLA_HEADS, MLA_ROPE_DIM))], axis=-1)
    return k, v


def fourier_mix(pf):
    B, N, _ = pf.shape
    f = pf.astype(jnp.float32).reshape(B, N, FOURIER_GROUPS, FOURIER_GROUP_DIM)
    f = jnp.fft.fft2(f, axes=(1, 3), norm="ortho").real
    return f.reshape(B, N, FOURIER_WIDTH).astype(pf.dtype)


def merge_branches(ya, yb, yc, pg, b_gate, w_br_a, w_br_b, w_br_c, w_out):
    B, N, _ = pg.shape
    g = jax.nn.sigmoid((pg + b_gate).astype(jnp.float32)).astype(ya.dtype)
    g = g.reshape(B, N, N_BRANCH, D_MODEL)
    m = g[:, :, 0] * (ya @ w_br_a) + g[:, :, 1] * (yb @ w_br_b) + g[:, :, 2] * (yc @ w_br_c)
    return m @ w_out


def swiglu(h, w_in, w_out):
    gate, up = jnp.split(h @ w_in, 2, axis=-1)
    return (jax.nn.silu(gate) * up) @ w_out


def setup_inputs(seed: int = 0) -> dict:
    key = jax.random.key(seed)
    ks = jax.random.split(key, 32)
    L, D = DEPTH, D_MODEL
    f32 = jnp.float32

    def w(k, shape, fan_in, gain=1.0):
        return jax.random.normal(k, shape, f32) * (gain * fan_in ** -0.5)

    def gain(k, shape):
        return 1.0 + 0.1 * jax.random.normal(k, shape, f32)

    def bias(k, shape):
        return 0.01 * jax.random.normal(k, shape, f32)

    return {
        "x": jax.random.normal(ks[0], (BATCH, SEQ, D), f32),
        "c": jax.random.normal(ks[1], (BATCH, D), f32),
        "ctx": jax.random.normal(ks[2], (BATCH, CTX_LEN, D), f32),
        "c_ctx": jax.random.normal(ks[3], (D,), f32),
        "w_mod": w(ks[4], (L, D, 6 * D), D, 0.5),
        "b_mod": bias(ks[5], (L, 6 * D)),
        "g_norm1": gain(ks[6], (L, D)),
        "g_norm2": gain(ks[7], (L, D)),
        "w_in": w(ks[8], (L, D, IN_WIDTH), D),
        "g_q_gqa": gain(ks[9], (L, GQA_HEAD_DIM)),
        "g_k_gqa": gain(ks[10], (L, GQA_HEAD_DIM)),
        "g_cq": gain(ks[11], (L, MLA_Q_RANK)),
        "g_ckv": gain(ks[12], (L, MLA_KV_RANK)),
        "w_uq": w(ks[13], (L, MLA_Q_RANK, MLA_UQ_WIDTH), MLA_Q_RANK),
        "w_ukv": w(ks[14], (L, MLA_KV_RANK, MLA_UKV_WIDTH), MLA_KV_RANK),
        "g_q_nope": gain(ks[15], (L, MLA_NOPE_DIM)),
        "g_k_nope": gain(ks[16], (L, MLA_NOPE_DIM)),
        "g_q_rope": gain(ks[17], (L, MLA_ROPE_DIM)),
        "g_k_rope": gain(ks[18], (L, MLA_ROPE_DIM)),
        "b_gate": bias(ks[19], (L, N_BRANCH * D)),
        "w_br_a": w(ks[20], (L, GQA_HEADS * GQA_HEAD_DIM, D), GQA_HEADS * GQA_HEAD_DIM),
        "w_br_b": w(ks[21], (L, MLA_HEADS * MLA_V_DIM, D), MLA_HEADS * MLA_V_DIM),
        "w_br_c": w(ks[22], (L, FOURIER_WIDTH, D), FOURIER_WIDTH),
        "w_out": w(ks[23], (L, D, D), D),
        "w_ffn_in": w(ks[24], (L, D, 2 * FFN_HIDDEN), D),
        "w_ffn_out": w(ks[25], (L, FFN_HIDDEN, D), FFN_HIDDEN),
    }


def reference(x, c, ctx, c_ctx, w_mod, b_mod, g_norm1, g_norm2, w_in, g_q_gqa, g_k_gqa,
              g_cq, g_ckv, w_uq, w_ukv, g_q_nope, g_k_nope, g_q_rope, g_k_rope, b_gate,
              w_br_a, w_br_b, w_br_c, w_out, w_ffn_in, w_ffn_out):
    B, N, D = x.shape
    ROWS = N // GRID_W
    rope_a = axial_rope_tables(ROWS, GQA_HEAD_DIM)
    rope_b = axial_rope_tables(ROWS, MLA_ROPE_DIM)
    silu_c = jax.nn.silu(c)
    silu_cc = jax.nn.silu(c_ctx)
    xc = ctx

    for l in range(DEPTH):
        last = l == DEPTH - 1
        mod = (silu_c @ w_mod[l] + b_mod[l])[:, None, :]
        mod_c = (silu_cc @ w_mod[l] + b_mod[l])[None, None, :]
        sh1, sc1, gt1, sh2, sc2, gt2 = jnp.split(mod, 6, axis=-1)
        csh1, csc1, cgt1, csh2, csc2, cgt2 = jnp.split(mod_c, 6, axis=-1)

        h = modulate(x, g_norm1[l], sh1, sc1)
        hc = modulate(xc, g_norm1[l], csh1, csc1)
        p_q, p_k, p_v, p_cq, p_ckv, p_kr, p_f, p_g = split_in(h @ w_in[l])
        c_q, c_k, c_v, c_cq, c_ckv, c_kr, c_f, c_g = split_in(hc @ w_in[l])

        kc_a, vc_a = gqa_keys_values(c_k, c_v, g_k_gqa[l], None)
        kc_b, vc_b = mla_keys_values(c_ckv, c_kr, g_ckv[l], w_ukv[l], g_k_nope[l], g_k_rope[l], None)

        q_a = gqa_queries(p_q, g_q_gqa[l], rope_a)
        k_a, v_a = gqa_keys_values(p_k, p_v, g_k_gqa[l], rope_a)
        ya = attend(q_a, jnp.concatenate([kc_a, k_a], axis=1),
                    jnp.concatenate([vc_a, v_a], axis=1), GQA_SCALE)
        q_b = mla_queries(p_cq, g_cq[l], w_uq[l], g_q_nope[l], g_q_rope[l], rope_b)
        k_b, v_b = mla_keys_values(p_ckv, p_kr, g_ckv[l], w_ukv[l], g_k_nope[l], g_k_rope[l], rope_b)
        yb = attend(q_b, jnp.concatenate([kc_b, k_b], axis=1),
                    jnp.concatenate([vc_b, v_b], axis=1), MLA_SCALE)
        yc = fourier_mix(p_f)

        x_new = x + gt1 * merge_branches(ya, yb, yc, p_g, b_gate[l], w_br_a[l], w_br_b[l],
                                         w_br_c[l], w_out[l])
        x_new = x_new + gt2 * swiglu(modulate(x_new, g_norm2[l], sh2, sc2), w_ffn_in[l], w_ffn_out[l])

        if not last:
            qc_a = gqa_queries(c_q, g_q_gqa[l], None)
            yca = attend(qc_a, kc_a, vc_a, GQA_SCALE)
            qc_b = mla_queries(c_cq, g_cq[l], w_uq[l], g_q_nope[l], g_q_rope[l], None)
            ycb = attend(qc_b, kc_b, vc_b, MLA_SCALE)
            ycc = fourier_mix(c_f)
            xc = xc + cgt1 * merge_branches(yca, ycb, ycc, c_g, b_gate[l], w_br_a[l], w_br_b[l],
                                            w_br_c[l], w_out[l])
            xc = xc + cgt2 * swiglu(modulate(xc, g_norm2[l], csh2, csc2), w_ffn_in[l], w_ffn_out[l])

        x = x_new

    return x
```

```python
import contextlib
import numpy as np
import ml_dtypes
import concourse.bass as bass
import concourse.mybir as mybir
from concourse.bass_utils import run_bass_kernel_spmd

F32 = mybir.dt.float32
BF16 = mybir.dt.bfloat16
AF = mybir.ActivationFunctionType
ALU = mybir.AluOpType
AX = mybir.AxisListType

D = 1024
NLAT = 2048
NCTX = 256
NT = 18
DEPTH = 2
EPS = 1e-6
FFN_H = 2816
IN_W = 5024
GQA_SCALE = 64 ** -0.5
MLA_SCALE = 96 ** -0.5
N_DMA_SEMS = 48
ARENA_BYTES = 207 * 1024


class Res:
    __slots__ = ("name", "last_w", "readers")

    def __init__(self, name):
        self.name = name
        self.last_w = None
        self.readers = []


class Op:
    __slots__ = ("id", "eng", "fn", "deps", "dma", "pos", "waits", "signals", "sigval", "dsem", "dval")

    def __init__(self, id, eng, fn, deps, dma):
        self.id = id
        self.eng = eng
        self.fn = fn
        self.deps = deps
        self.dma = dma
        self.pos = -1
        self.waits = []
        self.signals = False
        self.sigval = 0
        self.dsem = -1
        self.dval = 0


class Prog:
    ENGS = ("pe", "act", "dve", "pool", "sp")

    def __init__(self, nc):
        self.nc = nc
        self.ops = []
        self.dma_count = 0
        self.dma_last_on_sem = {}

    def op(self, eng, fn, reads=(), writes=(), dma=False):
        deps = {}
        for w in writes:
            if w.last_w is not None:
                deps[w.last_w] = False
            for rr in w.readers:
                deps[rr] = False
        for r in reads:
            if r.last_w is not None:
                deps[r.last_w] = True
        o = Op(len(self.ops), eng, fn, deps, dma)
        if dma:
            s = self.dma_count % N_DMA_SEMS
            self.dma_count += 1
            prev = self.dma_last_on_sem.get(s)
            o.dsem = s
            if prev is not None:
                o.deps.setdefault(prev.id, False)
                o.dval = prev.dval + 16
            else:
                o.dval = 16
            self.dma_last_on_sem[s] = o
        for r in reads:
            r.readers.append(o.id)
        for w in writes:
            w.last_w = o.id
            w.readers = []
        self.ops.append(o)
        return o

    def schedule(self):
        ops = self.ops
        per_eng = {e: [] for e in self.ENGS}
        for o in ops:
            o.pos = len(per_eng[o.eng])
            per_eng[o.eng].append(o)
        known = {e: {} for e in self.ENGS}
        snap = {}
        for o in ops:
            k = known[o.eng]
            for d in sorted(o.deps):
                dop = ops[d]
                if dop.dma:
                    key = ("d", dop.dsem)
                    need = dop.dval
                else:
                    if dop.eng == o.eng and (o.eng == "pe" or not o.deps[d]):
                        continue
                    key = dop.eng
                    need = dop.pos + 1
                if k.get(key, 0) >= need:
                    continue
                o.waits.append(d)
                dop.signals = True
                ks = snap.get(d)
                if ks is not None:
                    for kk, vv in ks.items():
                        if k.get(kk, 0) < vv:
                            k[kk] = vv
                if k.get(key, 0) < need:
                    k[key] = need
            snap[o.id] = dict(k)
        for e in self.ENGS:
            c = 0
            for o in per_eng[e]:
                if o.dma:
                    continue
                if o.signals:
                    c += 1
                    o.sigval = c
        self.per_eng = per_eng

    def emit(self):
        nc = self.nc
        self.schedule()
        ops = self.ops
        with contextlib.ExitStack() as st:
            tsem = {e: st.enter_context(nc.semaphore("ts_" + e)) for e in ("pe", "act", "dve", "pool")}
            dsems = [st.enter_context(nc.semaphore("dm_%d" % i)) for i in range(N_DMA_SEMS)]
            block = st.enter_context(nc.Block())

            def run(engname):
                def body(eng):
                    for o in self.per_eng[engname]:
                        for d in o.waits:
                            dop = ops[d]
                            if dop.dma:
                                eng.wait_ge(dsems[dop.dsem], dop.dval)
                            else:
                                eng.wait_ge(tsem[dop.eng], dop.sigval)
                        ins = o.fn(eng)
                        if o.dma:
                            ins.then_inc(dsems[o.dsem], 16)
                        elif o.signals:
                            ins.then_inc(tsem[o.eng], 1)
                    for s, lo in self.dma_last_on_sem.items():
                        if lo.eng == engname:
                            eng.wait_ge(dsems[s], lo.dval)
                return body

            block.tensor(run("pe"))
            block.scalar(run("act"))
            block.vector(run("dve"))
            block.gpsimd(run("pool"))
            block.sync(run("sp"))


class Arena:
    def __init__(self, tensor, nbytes):
        self.t = tensor
        self.size = nbytes
        self.live = []
        self.ghosts = []
        self.peak = 0

    def alloc(self, name, nbytes, dtype=BF16, nres=1):
        req = nbytes
        nbytes = (nbytes + 63) // 64 * 64
        off = 0
        for (o, s, _) in sorted(self.live, key=lambda b: b[0]):
            if off + nbytes <= o:
                break
            off = max(off, o + s)
        if off + nbytes > self.size:
            raise RuntimeError("arena full allocating %s (%d B); live=%s" % (
                name, nbytes, [(r[2][0].name, r[1]) for r in self.live]))
        self.peak = max(self.peak, off + nbytes)
        ids = []
        keep = []
        for (go, gs, gids) in self.ghosts:
            if go < off + nbytes and off < go + gs:
                ids.extend(gids)
                if go >= off and go + gs <= off + nbytes:
                    continue
            keep.append((go, gs, gids))
        self.ghosts = keep
        ids = sorted(set(ids))
        res = []
        for i in range(nres):
            r = Res("%s_%d" % (name, i))
            r.readers = list(ids)
            res.append(r)
        self.live.append((off, nbytes, res))
        ap = self.t[:, off // 2:(off + req) // 2]
        if dtype == F32:
            ap = ap.bitcast(F32)
        return Buf(ap, res, off, self)

    def free(self, buf):
        for i, (o, s, res) in enumerate(self.live):
            if o == buf.off and res is buf.res:
                ids = []
                for r in res:
                    ids.extend(r.readers)
                    if r.last_w is not None:
                        ids.append(r.last_w)
                self.ghosts.append((o, s, sorted(set(ids))))
                del self.live[i]
                return
        raise RuntimeError("free of unknown buffer")


class Buf:
    def __init__(self, ap, res, off, arena):
        self.ap = ap
        self.res = res
        self.off = off
        self.arena = arena

    @property
    def r(self):
        return self.res[0]

    def free(self):
        self.arena.free(self)


def _rope_tables(dim):
    half = dim // 2
    freqs = 10000.0 ** (-np.arange(0, half, 2, dtype=np.float32) / half)
    t = np.arange(NLAT)
    row = (t // 64).astype(np.float32)
    col = (t % 64).astype(np.float32)

    def ax(pos):
        a = pos[:, None] * freqs[None, :]
        return np.concatenate([a, a], axis=-1)

    ang = np.concatenate([ax(row), ax(col)], axis=-1).astype(np.float32)
    cos = np.cos(ang).astype(np.float32)
    sin = np.sin(ang).astype(np.float32)
    q = dim // 4
    sgn = np.ones((dim,), np.float32)
    for a in range(2):
        sgn[a * 2 * q: a * 2 * q + q] = -1.0
    sins = sin * sgn[None, :]
    cos_f = np.ones((NT * 128, dim), np.float32)
    sin_f = np.zeros((NT * 128, dim), np.float32)
    cos_f[:NLAT] = cos
    sin_f[:NLAT] = sins
    cos_t = cos_f.reshape(NT, 128, dim).transpose(1, 0, 2)
    sin_t = sin_f.reshape(NT, 128, dim).transpose(1, 0, 2)
    return np.ascontiguousarray(cos_t), np.ascontiguousarray(sin_t)


def _dft_tables():
    bf = ml_dtypes.bfloat16
    n = np.arange(NLAT, dtype=np.int64)
    ph = (np.outer(n, n) % NLAT).astype(np.float64) * (2.0 * np.pi / NLAT)
    cn = (np.cos(ph) / np.sqrt(NLAT)).astype(np.float32)
    sn = (np.sin(ph) / np.sqrt(NLAT)).astype(np.float32)

    def lay(m):
        return np.ascontiguousarray(m.reshape(16, 128, 8, 256).transpose(2, 1, 0, 3)).astype(bf)

    n2 = np.arange(NCTX, dtype=np.int64)
    ph2 = (np.outer(n2, n2) % NCTX).astype(np.float64) * (2.0 * np.pi / NCTX)
    c256 = (np.cos(ph2) / np.sqrt(NCTX)).astype(np.float32)
    s256 = (np.sin(ph2) / np.sqrt(NCTX)).astype(np.float32)

    def lay2(m):
        return np.ascontiguousarray(m.reshape(2, 128, 256).transpose(1, 0, 2)).astype(bf)

    n3 = np.arange(128, dtype=np.int64)
    ph3 = (np.outer(n3, n3) % 128).astype(np.float64) * (2.0 * np.pi / 128)
    c128 = (np.cos(ph3) / np.sqrt(128.0)).astype(np.float32).astype(bf)
    ns128 = (-np.sin(ph3) / np.sqrt(128.0)).astype(np.float32).astype(bf)
    return lay(cn), lay(sn), lay2(c256), lay2(s256), c128, ns128


RV = {}
_o = 0
for _n, _w in (("gqa_q", 64), ("gqa_k", 64), ("q_nope", 64), ("k_nope", 64), ("q_rope", 32), ("k_rope", 32),
               ("b_gate", 3072), ("b_gt1", 1024), ("b_gt2", 1024)):
    RV[_n] = (_o, _w)
    _o += _w
RV_W = _o
FM = {"g1": (0, 8), "g2": (8, 8), "bmod": (16, 48), "gcq": (64, 3), "gckv": (67, 2)}
FM_W = 69

_CACHE = {}


def build_nc(depth=DEPTH, dbg=()):
    nc = bass.Bass("TRN2", target_bir_lowering=False)
    din = lambda n, s, dt=F32: nc.dram_tensor(n, list(s), dt, kind="ExternalInput").ap()
    x_d = din("x", [NLAT, D])
    ctx_d = din("ctx", [NCTX, D])
    c2_d = din("c2", [128, 16])
    w_mod_d = din("w_mod", [DEPTH, D, 6 * D])
    w_in_d = din("w_in", [DEPTH, D, IN_W])
    w_uq_d = din("w_uq", [DEPTH, 384, 768])
    w_ukv_d = din("w_ukv", [DEPTH, 256, 1024])
    w_br_d = [din("w_br_" + s, [DEPTH, 512, D]) for s in "abc"]
    w_out_d = din("w_out", [DEPTH, D, D])
    w_fi_d = din("w_ffn_in", [DEPTH, D, 2 * FFN_H])
    w_fo_d = din("w_ffn_out", [DEPTH, FFN_H, D])
    vrow_d = din("vrow", [DEPTH, RV_W])
    vfm_d = din("vfm", [128, DEPTH * FM_W])
    identf_d = din("identf", [128, 128])
    cosa_d = din("cosa", [128, NT * 64])
    sina_d = din("sina", [128, NT * 64])
    cosb_d = din("cosb", [128, NT * 32])
    sinb_d = din("sinb", [128, NT * 32])
    cn_d = din("cn", [8, 128, 16 * 256], BF16)
    sn_d = din("sn", [8, 128, 16 * 256], BF16)
    c256_d = din("c256", [128, 512], BF16)
    s256_d = din("s256", [128, 512], BF16)
    c128_d = din("c128", [128, 128], BF16)
    ns128_d = din("ns128", [128, 128], BF16)
    out_d = nc.dram_tensor("out", [NLAT, D], F32, kind="ExternalOutput").ap()
    ht_d = nc.dram_tensor("ht_scr", [NT, 128, 1024], BF16).ap()
    y_d = [nc.dram_tensor("y_scr%d" % i, [128, 4, NT * 128], BF16).ap() for i in range(3)]
    dbg_d = {}

    with contextlib.ExitStack() as st:
        arena_t = st.enter_context(nc.sbuf_tensor("arena", [128, ARENA_BYTES // 2], BF16))
        psum = [st.enter_context(nc.psum_tensor("ps%d" % i, [128, 512], F32)) for i in range(8)]
        P = Prog(nc)
        A = Arena(arena_t, ARENA_BYTES)
        PS = [Res("ps%d" % i) for i in range(8)]
        R_ht = [Res("ht%d" % i) for i in range(NT)]
        R_y = [Res("y%d" % i) for i in range(3)]
        R_out = Res("out")

        def psb(i):
            return psum[i][:].bitcast(BF16)

        def rl(xs):
            out = []
            for v in xs:
                if isinstance(v, Buf):
                    out.extend(v.res)
                elif isinstance(v, Res):
                    out.append(v)
                elif v is not None:
                    out.extend(v)
            return out

        def dma(eng, out, in_, reads, writes):
            P.op(eng, lambda e: e.dma_start(out=out, in_=in_), rl(reads), rl(writes), dma=True)

        def mm(out, lhsT, rhs, start, stop, reads, writes):
            P.op("pe", lambda e: e.matmul(out, lhsT, rhs, start=start, stop=stop), rl(reads), rl(writes))

        def tr(out, in_, ident, reads, writes):
            P.op("pe", lambda e: e.transpose(out, in_, ident), rl(reads), rl(writes))

        def act(out, in_, func, reads, writes, bias=0.0, scale=1.0, accum=None):
            if accum is None:
                P.op("act", lambda e: e.activation(out, in_, func, bias=bias, scale=scale), rl(reads), rl(writes))
            else:
                P.op("act", lambda e: e.activation(out, in_, func, bias=bias, scale=scale, accum_out=accum),
                     rl(reads), rl(writes))

        def tt(eng, out, in0, in1, op, reads, writes):
            P.op(eng, lambda e: e.tensor_tensor(out, in0, in1, op), rl(reads), rl(writes))

        def ts(eng, out, in0, s1, s2, op0, op1, reads, writes):
            if s2 is None:
                P.op(eng, lambda e: e.tensor_scalar(out, in0, s1, None, op0), rl(reads), rl(writes))
            else:
                P.op(eng, lambda e: e.tensor_scalar(out, in0, s1, s2, op0, op1), rl(reads), rl(writes))

        def cp(eng, out, in_, reads, writes):
            if eng == "act":
                P.op("act", lambda e: e.activation(out, in_, AF.Copy), rl(reads), rl(writes))
            else:
                P.op(eng, lambda e: e.tensor_copy(out, in_), rl(reads), rl(writes))

        def recip(out, in_, reads, writes):
            P.op("dve", lambda e: e.reciprocal(out, in_), rl(reads), rl(writes))

        def red(out, in_, reads, writes):
            P.op("dve", lambda e: e.tensor_reduce(out, in_, AX.X, ALU.add), rl(reads), rl(writes))

        def memset(eng, ap, val, writes):
            P.op(eng, lambda e: e.memset(ap, val), [], rl(writes))

        def v3(ap, a):
            return ap.rearrange("p (a b) -> p a b", a=a)

        def rsqrt_into(dst, ss, n, inv, tmp, reads, writes_tmp, writes_dst):
            act(tmp, ss, AF.Sqrt, reads, writes_tmp, bias=EPS_AP[0], scale=inv)
            recip(dst, tmp, writes_tmp, writes_dst)

        X = A.alloc("X", NT * D * 4, F32, nres=NT)
        Xv = v3(X.ap, NT)
        XR = X.res
        identf = A.alloc("identf", 512, F32)
        identb = A.alloc("identb", 256, BF16)
        c128 = A.alloc("c128", 256, BF16)
        ns128 = A.alloc("ns128", 256, BF16)
        c256 = A.alloc("c256", 1024, BF16)
        s256 = A.alloc("s256", 1024, BF16)
        vfm = A.alloc("vfm", DEPTH * FM_W * 4, F32)
        c2 = A.alloc("c2", 64, F32)
        sc2 = A.alloc("sc2", 32, BF16)
        epsb = A.alloc("eps", 64, F32)
        modT = A.alloc("modT", DEPTH * 48 * 2 * 4, F32)
        amod = A.alloc("amod", DEPTH * 2 * 8 * 2 * 4, F32)
        EPS_AP = [epsb.ap[:, 0:1]]

        for i in range(NLAT // 128):
            dma("sp", Xv[:, i, :], x_d[i * 128:(i + 1) * 128, :], [], [XR[i]])
        for i in range(2):
            dma("sp", Xv[:, 16 + i, :], ctx_d[i * 128:(i + 1) * 128, :], [], [XR[16 + i]])
        dma("sp", identf.ap, identf_d, [], [identf])
        dma("sp", c128.ap, c128_d, [], [c128])
        dma("sp", ns128.ap, ns128_d, [], [ns128])
        dma("sp", c256.ap, c256_d, [], [c256])
        dma("sp", s256.ap, s256_d, [], [s256])
        dma("sp", vfm.ap, vfm_d, [], [vfm])
        dma("sp", c2.ap, c2_d, [], [c2])
        memset("dve", epsb.ap, EPS, [epsb])
        cp("dve", identb.ap, identf.ap, [identf], [identb])
        act(sc2.ap, c2.ap, AF.Silu, [c2], [sc2])
        sc2v = v3(sc2.ap, 8)
        modTv = modT.ap.rearrange("p (l f j) -> p l f j", l=DEPTH, f=48)
        amodv = amod.ap.rearrange("p (l n k j) -> p l n k j", l=DEPTH, n=2, k=8)
        vfmv = v3(vfm.ap, DEPTH)

        def fm(l, name):
            o, w = FM[name]
            return vfmv[:, l, o:o + w]

        def wview(w_ap, l, c0, c1):
            return w_ap[l].rearrange("(kc p) n -> p kc n", p=128)[:, :, c0:c1]

        def load_w(name, w_ap, l, c0, c1, kcs=None):
            K = w_ap.shape[1]
            nk = K // 128
            n = c1 - c0
            b = A.alloc(name, nk * n * 2, BF16)
            bv = v3(b.ap, nk)
            src = wview(w_ap, l, c0, c1)
            for kc in range(nk):
                dma("pool", bv[:, kc, :], src[:, kc, :], [], [b])
            return b, bv

        def load_row(name, l, key, eng="sp"):
            o, w = RV[key]
            b = A.alloc(name, w * 4, F32)
            dma(eng, b.ap, vrow_d[l, o:o + w].partition_broadcast(128), [], [b])
            return b

        def compute_mod_fm(l):
            for blk in (0, 1, 3, 4):
                wb, wbv = load_w("wmod", w_mod_d, l, blk * D, (blk + 1) * D)
                pv = psum[0][:, 0:16].rearrange("p (f j) -> p f j", f=8)
                for f in range(8):
                    for kc in range(8):
                        mm(pv[:, f, :], wbv[:, kc, f * 128:(f + 1) * 128], sc2v[:, kc, :], kc == 0, kc == 7,
                           [wb, sc2], [PS[0]])
                bm = fm(l, "bmod")[:, blk * 8:(blk + 1) * 8]
                tt("dve", modTv[:, l, blk * 8:(blk + 1) * 8, :], pv, bm.unsqueeze(2).to_broadcast([128, 8, 2]), ALU.add,
                   [PS[0], vfm], [modT])
                wb.free()
            for n, (blk, g) in enumerate(((1, "g1"), (4, "g2"))):
                P.op("dve", lambda e, n=n, blk=blk, g=g: e.scalar_tensor_tensor(
                    out=amodv[:, l, n, :, :], in0=modTv[:, l, blk * 8:(blk + 1) * 8, :], scalar=1.0,
                    in1=fm(l, g).unsqueeze(2).to_broadcast([128, 8, 2]), op0=ALU.add, op1=ALU.mult),
                    rl([modT, vfm]), rl([amod]))

        def compute_gate_rows(l, which, need_ctx):
            blk = 2 if which == 0 else 5
            wb, wbv = load_w("wmodg", w_mod_d, l, blk * D, (blk + 1) * D)
            brow = load_row("bgt", l, "b_gt1" if which == 0 else "b_gt2")
            scb = A.alloc("scb", 8 * 2 * 128 * 2, BF16)
            scbv = scb.ap.rearrange("p (k j m) -> p k j m", k=8, j=2)
            cp("dve", scbv, sc2v.unsqueeze(3).to_broadcast([128, 8, 2, 128]), [sc2], [scb])
            outs = []
            for j in range(2 if need_ctx else 1):
                g = A.alloc("gt%d" % j, 4096, F32)
                for nch in range(2):
                    for kc in range(8):
                        mm(psum[nch][:], scbv[:, kc, j, :], wbv[:, kc, nch * 512:(nch + 1) * 512], kc == 0, kc == 7,
                           [scb, wb], [PS[nch]])
                    tt("dve", g.ap[:, nch * 512:(nch + 1) * 512], psum[nch][:], brow.ap[:, nch * 512:(nch + 1) * 512],
                       ALU.add, [PS[nch], brow], [g])
                outs.append(g)
            wb.free()
            brow.free()
            scb.free()
            return outs

        def build_ht(l, n, tiles):
            xn = [A.alloc("xn%d" % i, 4096, F32) for i in range(2)]
            junk = A.alloc("junk", 2048, BF16)
            hts = [A.alloc("hts%d" % i, 2048, BF16) for i in range(2)]
            st_ = A.alloc("nstat", 64, F32)
            for it, t in enumerate(tiles):
                j = 1 if t >= 16 else 0
                xb = xn[it % 2]
                hb = hts[it % 2]
                hbv = v3(hb.ap, 8)
                sv = st_.ap
                act(junk.ap, Xv[:, t, :], AF.Square, [XR[t]], [junk, st_], accum=sv[:, 0:1])
                act(sv[:, 1:2], sv[:, 0:1], AF.Sqrt, [st_], [st_], bias=EPS_AP[0], scale=1.0 / D)
                recip(sv[:, 2:3], sv[:, 1:2], [st_], [st_])
                ts("dve", xb.ap, Xv[:, t, :], sv[:, 2:3], None, ALU.mult, None, [XR[t], st_], [xb])
                for half in range(2):
                    pb = 6 + half
                    pv = v3(psum[pb][:], 4)
                    for c in range(4):
                        kc = half * 4 + c
                        tr(pv[:, c, :], xb.ap[:, kc * 128:(kc + 1) * 128], identf.ap, [xb, identf], [PS[pb]])
                    for c in range(4):
                        kc = half * 4 + c
                        a_ap = amodv[:, l, n, kc, j:j + 1]
                        b_ap = modTv[:, l, (0 if n == 0 else 24) + kc, j:j + 1]
                        if c % 2 == 0:
                            act(hbv[:, kc, :], pv[:, c, :], AF.Identity, [PS[pb], amod, modT], [hb], bias=b_ap, scale=a_ap)
                        else:
                            ts("dve", hbv[:, kc, :], pv[:, c, :], a_ap, b_ap, ALU.mult, ALU.add, [PS[pb], amod, modT], [hb])
                dma("sp", ht_d[t], hb.ap, [hb], [R_ht[t]])
            for b in xn + hts + [junk, st_]:
                b.free()

        def headnorm(src, nh, hd, stride, off, gain, cos, sin, out, reads_src, S, stat, wr_out):
            sv = v3(src[:, 0:nh * stride], nh)[:, :, off:off + hd]
            n = nh * hd
            s1 = v3(S[0].ap[:, 0:n], nh)
            s2 = v3(S[1].ap[:, 0:n], nh)
            s3 = v3(S[2].ap[:, 0:n], nh)
            stv = stat.ap
            act(s1, sv, AF.Square, reads_src, [S[0]])
            red(stv[:, 0:nh], s1, [S[0]], [stat])
            act(stv[:, 16:16 + nh], stv[:, 0:nh], AF.Sqrt, [stat], [stat], bias=EPS_AP[0], scale=1.0 / hd)
            recip(stv[:, 32:32 + nh], stv[:, 16:16 + nh], [stat], [stat])
            tt("dve", s2, sv, stv[:, 32:32 + nh].unsqueeze(2).to_broadcast([128, nh, hd]), ALU.mult,
               list(reads_src) + [stat], [S[1]])
            gb = gain.ap.unsqueeze(1).to_broadcast([128, nh, hd])
            if cos is None:
                tt("dve", out, s2, gb, ALU.mult, [S[1], gain], wr_out)
                return
            tt("dve", s1, s2, gb, ALU.mult, [S[1], gain], [S[0]])
            q = hd // 4
            cb = cos.unsqueeze(1).to_broadcast([128, nh, hd])
            tt("dve", s2, s1, cb, ALU.mult, [S[0], ROPE], [S[1]])
            x5 = s1.rearrange("p h (a b f) -> p h a b f", a=2, b=2)
            u5 = s3.rearrange("p h (a b f) -> p h a b f", a=2, b=2)
            sn5 = sin.unsqueeze(1).to_broadcast([128, nh, hd]).rearrange("p h (a b f) -> p h a b f", a=2, b=2)
            for part in range(2):
                for a in range(2):
                    tt("pool", u5[:, :, a, part, :], x5[:, :, a, 1 - part, :], sn5[:, :, a, part, :], ALU.mult,
                       [S[0], ROPE], [S[2]])
            tt("dve", out, s2, s3, ALU.add, [S[1], S[2]], wr_out)

        ROPE = Res("rope")

        def attention(units, kt_list, qcols, nq, scale, YT):
            for ui, u in enumerate(units):
                sb_ = (ui % 2) * 2
                ob_ = 4 + (ui % 2) * 2 + ((ui // 2) % 2)
                ops_ = psum[ob_][:, 0:nq]
                for ki, kt in enumerate(kt_list):
                    sbank = sb_ + (ki % 2)
                    sp_ = psum[sbank][:, 0:nq]
                    mm(sp_, u["kT"][:, kt * 128:(kt + 1) * 128], u["qT"][:, qcols:qcols + nq], True, True,
                       u["reads"], [PS[sbank]])
                    pt = PT[(ui % 2) * 2 + (ki % 2)]
                    act(pt.ap[:, 0:nq], sp_, AF.Exp, [PS[sbank]], [pt], scale=scale)
                    mm(ops_, u["v"](kt), pt.ap[:, 0:nq], ki == 0, ki == len(kt_list) - 1, [pt] + u["vreads"], [PS[ob_]])
                ob = u["ob"]
                rc = REC[ui % 2]
                recip(rc.ap[0:64, 0:nq], psum[ob_][64:128, 0:nq], [PS[ob_]], [rc])
                tt("dve", YT[ob:ob + 64, u["ych"], qcols:qcols + nq], psum[ob_][0:64, 0:nq], rc.ap[0:64, 0:nq], ALU.mult,
                   [PS[ob_], rc], u["ywr"])

        for l in range(depth):
            last = (l == depth - 1)
            act_tiles = list(range(16)) if last else list(range(NT))
            if l == 0:
                compute_mod_fm(0)
            build_ht(l, 0, list(range(NT)))
            if l + 1 < depth:
                compute_mod_fm(l + 1)

            S = [A.alloc("S%d" % i, 2048, F32) for i in range(3)]
            stat = A.alloc("stat", 256, F32)
            hbuf = [A.alloc("hbuf%d" % i, 2048, BF16) for i in range(2)]
            PT = [A.alloc("PT%d" % i, 1024, BF16) for i in range(4)]
            REC = [A.alloc("REC%d" % i, 2048, F32) for i in range(2)]

            def load_ht(it, t):
                hb = hbuf[it % 2]
                dma("sp", hb.ap, ht_d[t], [R_ht[t]], [hb])
                return hb, v3(hb.ap, 8)

            wA, wAv = load_w("wA", w_in_d, l, 0, 768)
            gq = load_row("gq", l, "gqa_q")
            gk = load_row("gk", l, "gqa_k")
            cosa = A.alloc("cosa", NT * 64 * 4, F32)
            sina = A.alloc("sina", NT * 64 * 4, F32)
            dma("sp", cosa.ap, cosa_d, [], [cosa, ROPE])
            dma("sp", sina.ap, sina_d, [], [sina, ROPE])
            cosav = v3(cosa.ap, NT)
            sinav = v3(sina.ap, NT)
            QT = A.alloc("QaT", 4 * NT * 128 * 2, BF16)
            KT = A.alloc("KaT", 2 * NT * 128 * 2, BF16)
            VA = A.alloc("Va", NT * 2 * 128 * 2, BF16)
            YT = A.alloc("YT", 4 * NT * 128 * 2, BF16)
            QTv = v3(QT.ap, 4)
            KTv = v3(KT.ap, 2)
            VAv = VA.ap.rearrange("p (t k c) -> p t k c", t=NT, k=2)
            YTv = v3(YT.ap, 4)
            qtok = A.alloc("qtok", 1024, BF16)
            ktok = A.alloc("ktok", 512, BF16)
            memset("pool", VA.ap, 1.0, [VA])
            for it, t in enumerate(range(NT)):
                need_q = t in act_tiles
                hb, hbv = load_ht(it, t)
                if need_q:
                    for kc in range(8):
                        mm(psum[0][:], hbv[:, kc, :], wAv[:, kc, 0:512], kc == 0, kc == 7, [hb, wA], [PS[0]])
                for kc in range(8):
                    mm(psum[1][:, 0:256], hbv[:, kc, :], wAv[:, kc, 512:768], kc == 0, kc == 7, [hb, wA], [PS[1]])
                if need_q:
                    headnorm(psum[0][:], 8, 64, 64, 0, gq, cosav[:, t, :], sinav[:, t, :], v3(qtok.ap, 8),
                             [PS[0]], S, stat, [qtok])
                    pv = v3(psb(2)[:, 0:512], 4)
                    for c in range(4):
                        tr(pv[:, c, :], qtok.ap[:, c * 128:(c + 1) * 128], identb.ap, [qtok, identb], [PS[2]])
                    cp("act", QTv[:, :, t * 128:(t + 1) * 128], pv, [PS[2]], [QT])
                k4 = ktok.ap.rearrange("p (k r d) -> p k r d", k=2, r=2)
                headnorm(psum[1][:, 0:128], 2, 64, 64, 0, gk, cosav[:, t, :], sinav[:, t, :], k4[:, :, 0, :],
                         [PS[1]], S, stat, [ktok])
                cp("dve", k4[:, :, 1, :], k4[:, :, 0, :], [ktok], [ktok])
                pv = v3(psb(3)[:, 0:256], 2)
                for c in range(2):
                    tr(pv[:, c, :], ktok.ap[:, c * 128:(c + 1) * 128], identb.ap, [ktok, identb], [PS[3]])
                cp("act", KTv[:, :, t * 128:(t + 1) * 128], pv, [PS[3]], [KT])
                cp("act", VAv[:, t, :, 0:64], v3(psum[1][:, 128:256], 2), [PS[1]], [VA])
            wA.free()
            for b in (gq, gk, cosa, sina, qtok, ktok):
                b.free()
            jobs = [(qc * 512, 512, list(range(NT))) for qc in range(4)]
            if not last:
                jobs.append((2048, 256, [16, 17]))
            for (qcols, nq, kts) in jobs:
                units = []
                for p in range(4):
                    kv = p // 2
                    for r in range(2):
                        units.append(dict(kT=KTv[64 * r:64 * r + 64, kv, :], qT=QTv[64 * r:64 * r + 64, p, :],
                                          v=(lambda kt, kv=kv: VAv[:, kt, kv, :]), reads=[KT, QT], vreads=[VA],
                                          ob=64 * r, ych=p, ywr=[YT]))
                attention(units, kts, qcols, nq, GQA_SCALE, YTv)
            dma("sp", y_d[0], YTv, [YT], [R_y[0]])
            for b in (QT, KT, VA):
                b.free()

            wC, wCv = load_w("wC", w_in_d, l, 1440, 1952)
            PF = A.alloc("PF", NT * 512 * 2, BF16)
            PFv = v3(PF.ap, NT)
            for it, t in enumerate(range(NT) if not last else range(16)):
                hb, hbv = load_ht(it, t)
                pb = it % 2
                for kc in range(8):
                    mm(psum[pb][:], hbv[:, kc, :], wCv[:, kc, :], kc == 0, kc == 7, [hb, wC], [PS[pb]])
                cp("act" if it % 2 == 0 else "dve", PFv[:, t, :], psum[pb][:], [PS[pb]], [PF])
            wC.free()
            TB = [[A.alloc("TB%d%d" % (i, j), 16 * 256 * 2, BF16) for j in range(2)] for i in range(2)]
            PQ = [A.alloc("PQ%d" % i, 1024, BF16) for i in range(2)]

            def dft_chunk(tabs, ncs, tile0, ycol0, idx):
                tcv, tsv, tres = tabs
                for g in range(4):
                    u = idx * 4 + g
                    pa, pq_, py = (u % 2) * 3, (u % 2) * 3 + 1, (u % 2) * 3 + 2
                    for nc_ in range(ncs):
                        mm(psum[pa][:, 0:256], PFv[:, tile0 + nc_, g * 128:(g + 1) * 128], tcv[:, nc_, :], nc_ == 0,
                           nc_ == ncs - 1, [PF] + tres, [PS[pa]])
                    for nc_ in range(ncs):
                        mm(psum[pq_][:, 0:256], PFv[:, tile0 + nc_, g * 128:(g + 1) * 128], tsv[:, nc_, :], nc_ == 0,
                           nc_ == ncs - 1, [PF] + tres, [PS[pq_]])
                    pq = PQ[u % 2]
                    cp("act", pq.ap[:, 0:256], psum[pa][:, 0:256], [PS[pa]], [pq])
                    cp("dve", pq.ap[:, 256:512], psum[pq_][:, 0:256], [PS[pq_]], [pq])
                    mm(psum[py][:, 0:256], c128.ap, pq.ap[:, 0:256], True, False, [c128, pq], [PS[py]])
                    mm(psum[py][:, 0:256], ns128.ap, pq.ap[:, 256:512], False, True, [ns128, pq], [PS[py]])
                    cp("act" if g % 2 else "dve", YTv[:, g, ycol0:ycol0 + 256], psum[py][:, 0:256], [PS[py]], [YT])

            for kc2 in range(8):
                tb = TB[kc2 % 2]
                dma("sp", tb[0].ap, cn_d[kc2], [], [tb[0]])
                dma("sp", tb[1].ap, sn_d[kc2], [], [tb[1]])
                dft_chunk((v3(tb[0].ap, 16), v3(tb[1].ap, 16), [tb[0], tb[1]]), 16, 0, kc2 * 256, kc2)
            if not last:
                dft_chunk((v3(c256.ap, 2), v3(s256.ap, 2), [c256, s256]), 2, 16, 2048, 8)
            dma("sp", y_d[2], YTv, [YT], [R_y[2]])
            for b in (PF, TB[0][0], TB[0][1], TB[1][0], TB[1][1], PQ[0], PQ[1]):
                b.free()

            wB, wBv = load_w("wB", w_in_d, l, 768, 1440)
            wuq, wuqv = load_w("wuq", w_uq_d, l, 0, 768)
            wukv, wukvv = load_w("wukv", w_ukv_d, l, 0, 1024)
            gqn = load_row("gqn", l, "q_nope")
            gkn = load_row("gkn", l, "k_nope")
            gqr = load_row("gqr", l, "q_rope")
            gkr = load_row("gkr", l, "k_rope")
            cosb = A.alloc("cosb", NT * 32 * 4, F32)
            sinb = A.alloc("sinb", NT * 32 * 4, F32)
            dma("sp", cosb.ap, cosb_d, [], [cosb, ROPE])
            dma("sp", sinb.ap, sinb_d, [], [sinb, ROPE])
            cosbv = v3(cosb.ap, NT)
            sinbv = v3(sinb.ap, NT)
            CQT = A.alloc("CQT", 3 * NT * 128 * 2, BF16)
            CKVT = A.alloc("CKVT", 2 * NT * 128 * 2, BF16)
            KR = A.alloc("KR", NT * 32 * 2, BF16)
            CQTv = v3(CQT.ap, 3)
            CKVTv = v3(CKVT.ap, 2)
            KRv = v3(KR.ap, NT)
            cn_ = A.alloc("cqn", 384 * 2 + 256 * 2, BF16)
            gcq = fm(l, "gcq")
            gckv = fm(l, "gckv")
            for it, t in enumerate(range(NT)):
                need_q = t in act_tiles
                hb, hbv = load_ht(it, t)
                if need_q:
                    for kc in range(8):
                        mm(psum[0][:, 0:384], hbv[:, kc, :], wBv[:, kc, 0:384], kc == 0, kc == 7, [hb, wB], [PS[0]])
                for kc in range(8):
                    mm(psum[1][:, 0:288], hbv[:, kc, :], wBv[:, kc, 384:672], kc == 0, kc == 7, [hb, wB], [PS[1]])
                sv = stat.ap
                jk = S[0].ap
                if need_q:
                    act(jk[:, 0:384], psum[0][:, 0:384], AF.Square, [PS[0]], [S[0], stat], accum=sv[:, 48:49])
                    act(sv[:, 51:52], sv[:, 48:49], AF.Sqrt, [stat], [stat], bias=EPS_AP[0], scale=1.0 / 384)
                act(jk[:, 0:256], psum[1][:, 0:256], AF.Square, [PS[1]], [S[0], stat], accum=sv[:, 49:50])
                act(sv[:, 52:53], sv[:, 49:50], AF.Sqrt, [stat], [stat], bias=EPS_AP[0], scale=1.0 / 256)
                if not need_q:
                    memset("dve", sv[:, 51:52], 1.0, [stat])
                recip(sv[:, 54:56], sv[:, 51:53], [stat], [stat])
                if need_q:
                    act(cn_.ap[:, 0:384], psum[0][:, 0:384], AF.Copy, [PS[0], stat], [cn_], scale=sv[:, 54:55])
                act(cn_.ap[:, 384:640], psum[1][:, 0:256], AF.Copy, [PS[1], stat], [cn_], scale=sv[:, 55:56])
                headnorm(psum[1][:, 256:288], 1, 32, 32, 0, gkr, cosbv[:, t, :], sinbv[:, t, :],
                         KRv[:, t, :].unsqueeze(1), [PS[1]], S, stat, [KR])
                pv = v3(psb(2)[:, 0:640], 5)
                for c in range(5):
                    if c < 3 and not need_q:
                        continue
                    tr(pv[:, c, :], cn_.ap[:, c * 128:(c + 1) * 128], identb.ap, [cn_, identb], [PS[2]])
                for c in range(5):
                    if c < 3 and not need_q:
                        continue
                    dst = CQTv[:, c, t * 128:(t + 1) * 128] if c < 3 else CKVTv[:, c - 3, t * 128:(t + 1) * 128]
                    gcol = gcq[:, c:c + 1] if c < 3 else gckv[:, c - 3:c - 2]
                    dbuf = CQT if c < 3 else CKVT
                    if c % 2 == 0:
                        act(dst, pv[:, c, :], AF.Copy, [PS[2], vfm], [dbuf], scale=gcol)
                    else:
                        ts("dve", dst, pv[:, c, :], gcol, None, ALU.mult, None, [PS[2], vfm], [dbuf])
            wB.free()
            cn_.free()
            gkr.free()
            HG = 2
            KBT = A.alloc("KBT", HG * NT * 128 * 2, BF16)
            QBT = A.alloc("QBT", HG * NT * 128 * 2, BF16)
            VB = A.alloc("VB", NT * HG * 128 * 2, BF16)
            KBTv = v3(KBT.ap, HG)
            QBTv = v3(QBT.ap, HG)
            VBv = VB.ap.rearrange("p (t h c) -> p t h c", t=NT, h=HG)
            ktk = A.alloc("ktk", HG * 96 * 2, BF16)
            qtk = A.alloc("qtk", HG * 96 * 2, BF16)
            ktkv = v3(ktk.ap, HG)
            qtkv = v3(qtk.ap, HG)
            memset("pool", VB.ap, 1.0, [VB])
            for hg in range(8 // HG):
                for it, t in enumerate(range(NT)):
                    need_q = t in act_tiles
                    for kc in range(2):
                        mm(psum[0][:, 0:HG * 128], CKVTv[:, kc, t * 128:(t + 1) * 128],
                           wukvv[:, kc, hg * HG * 128:(hg + 1) * HG * 128], kc == 0, kc == 1, [CKVT, wukv], [PS[0]])
                    headnorm(psum[0][:], HG, 64, 128, 0, gkn, None, None, ktkv[:, :, 0:64], [PS[0]], S, stat, [ktk])
                    cp("dve", ktkv[:, :, 64:96], KRv[:, t, :].unsqueeze(1).to_broadcast([128, HG, 32]), [KR], [ktk])
                    cp("act", VBv[:, t, :, 0:64], v3(psum[0][:, 0:HG * 128], HG)[:, :, 64:128], [PS[0]], [VB])
                    pv = v3(psb(2)[:, 0:HG * 128], HG)
                    for h in range(HG):
                        tr(pv[0:96, h, :], ktkv[:, h, :], identb.ap, [ktk, identb], [PS[2]])
                    cp("act", KBTv[0:96, :, t * 128:(t + 1) * 128], pv[0:96, :, :], [PS[2]], [KBT])
                    if need_q:
                        for kc in range(3):
                            mm(psum[1][:, 0:HG * 96], CQTv[:, kc, t * 128:(t + 1) * 128],
                               wuqv[:, kc, hg * HG * 96:(hg + 1) * HG * 96], kc == 0, kc == 2, [CQT, wuq], [PS[1]])
                        headnorm(psum[1][:], HG, 64, 96, 0, gqn, None, None, qtkv[:, :, 0:64], [PS[1]], S, stat, [qtk])
                        headnorm(psum[1][:], HG, 32, 96, 64, gqr, cosbv[:, t, :], sinbv[:, t, :], qtkv[:, :, 64:96],
                                 [PS[1]], S, stat, [qtk])
                        pv = v3(psb(3)[:, 0:HG * 128], HG)
                        for h in range(HG):
                            tr(pv[0:96, h, :], qtkv[:, h, :], identb.ap, [qtk, identb], [PS[3]])
                        cp("dve", QBTv[0:96, :, t * 128:(t + 1) * 128], pv[0:96, :, :], [PS[3]], [QBT])
                for (qcols, nq, kts) in jobs:
                    units = []
                    for h in range(HG):
                        hh = hg * HG + h
                        units.append(dict(kT=KBTv[0:96, h, :], qT=QBTv[0:96, h, :],
                                          v=(lambda kt, h=h: VBv[:, kt, h, :]), reads=[KBT, QBT], vreads=[VB],
                                          ob=64 * (hh % 2), ych=hh // 2, ywr=[YT]))
                    attention(units, kts, qcols, nq, MLA_SCALE, YTv)
            dma("sp", y_d[1], YTv, [YT], [R_y[1]])
            for b in (wuq, wukv, gqn, gkn, gqr, cosb, sinb, CQT, CKVT, KR, KBT, QBT, VB, ktk, qtk, YT):
                b.free()
            for b in PT + REC + S + [stat]:
                b.free()

            gts = compute_gate_rows(l, 0, not last)
            ybuf = [A.alloc("ybuf%d" % i, 3 * 4 * 128 * 2, BF16) for i in range(2)]
            sg = [A.alloc("sg%d" % i, 2048, F32) for i in range(2)]
            mac = A.alloc("mac", 2048, F32)
            mtok = A.alloc("mtok", 1024, BF16)
            mT = [A.alloc("mT%d" % i, 1024, BF16) for i in range(2)]
            bg = A.alloc("bg", 3 * 512 * 4, F32)
            wmrg = [None, None]

            def load_merge_w(half):
                c0 = half * 512
                ws = []
                for i in range(3):
                    ws.append(load_w("wg%d" % i, w_in_d, l, 1952 + i * D + c0, 1952 + i * D + c0 + 512))
                for i in range(3):
                    ws.append(load_w("wbr%d" % i, w_br_d[i], l, c0, c0 + 512))
                b = A.alloc("wo", 4 * 1024 * 2, BF16)
                bv = v3(b.ap, 4)
                src = w_out_d[l].rearrange("(kc p) n -> p kc n", p=128)
                for kc in range(4):
                    dma("pool", bv[:, kc, :], src[:, half * 4 + kc, :], [], [b])
                ws.append((b, bv))
                return ws

            wmrg[0] = load_merge_w(0)
            for half in range(2):
                if half == 0:
                    wmrg[1] = load_merge_w(1)
                ws = wmrg[half]
                c0 = half * 512
                for i in range(3):
                    o_ = RV["b_gate"][0] + i * D + half * 512
                    dma("sp", bg.ap[:, i * 512:(i + 1) * 512], vrow_d[l, o_:o_ + 512].partition_broadcast(128), [], [bg])
                for it, t in enumerate(act_tiles):
                    j = 1 if t >= 16 else 0
                    hb, hbv = load_ht(it, t)
                    yb = ybuf[it % 2]
                    ybv = yb.ap.rearrange("p (i k c) -> p i k c", i=3, k=4)
                    for i in range(3):
                        dma("sp", ybv[:, i, :, :], y_d[i][:, :, t * 128:(t + 1) * 128], [R_y[i]], [yb])
                    for i in range(3):
                        pg, pbk = i % 2, 2 + (i % 2)
                        wg, wgv = ws[i]
                        wb_, wbv_ = ws[3 + i]
                        for kc in range(8):
                            mm(psum[pg][:], hbv[:, kc, :], wgv[:, kc, :], kc == 0, kc == 7, [hb, wg], [PS[pg]])
                        for kc in range(4):
                            mm(psum[pbk][:], ybv[:, i, kc, :], wbv_[:, kc, :], kc == 0, kc == 3, [yb, wb_], [PS[pbk]])
                        s_ = sg[i % 2]
                        tt("dve", s_.ap, psum[pg][:], bg.ap[:, i * 512:(i + 1) * 512], ALU.add, [PS[pg], bg], [s_])
                        act(s_.ap, s_.ap, AF.Sigmoid, [s_], [s_])
                        if i == 0:
                            tt("dve", mac.ap, s_.ap, psum[pbk][:], ALU.mult, [s_, PS[pbk]], [mac])
                        else:
                            tt("dve", s_.ap, s_.ap, psum[pbk][:], ALU.mult, [s_, PS[pbk]], [s_])
                            if i == 1:
                                tt("pool", mac.ap, mac.ap, s_.ap, ALU.add, [mac, s_], [mac])
                            else:
                                tt("dve", mtok.ap, mac.ap, s_.ap, ALU.add, [mac, s_], [mtok])
                    pv = v3(psb(4)[:, 0:512], 4)
                    for c in range(4):
                        tr(pv[:, c, :], mtok.ap[:, c * 128:(c + 1) * 128], identb.ap, [mtok, identb], [PS[4]])
                    m_ = mT[it % 2]
                    cp("act", v3(m_.ap, 4), pv, [PS[4]], [m_])
                    wo, wov = ws[6]
                    for nch in range(2):
                        pb = 5 + nch
                        for kc in range(4):
                            mm(psum[pb][:], v3(m_.ap, 4)[:, kc, :], wov[:, kc, nch * 512:(nch + 1) * 512], kc == 0, kc == 3,
                               [m_, wo], [PS[pb]])
                        tx = sg[nch]
                        tt("dve", tx.ap, psum[pb][:], gts[j].ap[:, nch * 512:(nch + 1) * 512],
                           ALU.mult, [PS[pb], gts[j]], [tx])
                        tt("pool", Xv[:, t, nch * 512:(nch + 1) * 512], Xv[:, t, nch * 512:(nch + 1) * 512], tx.ap, ALU.add,
                           [XR[t], tx], [XR[t]])
                for (b, _) in ws:
                    b.free()
            for b in gts + [bg] + ybuf + sg + [mac, mtok] + mT + hbuf:
                b.free()

            build_ht(l, 1, act_tiles)
            gts = compute_gate_rows(l, 1, not last)
            pieces = [(0, 6), (6, 6), (12, 5), (17, 5)]
            hb4 = [A.alloc("hb4_%d" % i, 8 * 512 * 2, BF16) for i in range(2)]
            aT = [A.alloc("aT%d" % i, 6 * 512 * 2, BF16) for i in range(2)]
            sl = [A.alloc("sl%d" % i, 2048, F32) for i in range(2)]
            tmpx = [A.alloc("tmpy%d" % i, 4096, F32) for i in range(2)]

            def load_ffn_w(pi):
                h0, nh = pieces[pi]
                wg = load_w("wfg", w_fi_d, l, h0 * 128, (h0 + nh) * 128)
                wu = load_w("wfu", w_fi_d, l, FFN_H + h0 * 128, FFN_H + (h0 + nh) * 128)
                b = A.alloc("wfo", nh * 1024 * 2, BF16)
                bv = v3(b.ap, nh)
                src = w_fo_d[l].rearrange("(kc p) n -> p kc n", p=128)
                for kc in range(nh):
                    dma("pool", bv[:, kc, :], src[:, h0 + kc, :], [], [b])
                return wg, wu, (b, bv)

            chunks = [list(range(c * 4, c * 4 + 4)) for c in range(4)]
            if not last:
                chunks.append([16, 17])
            wf = [None] * 4
            wf[0] = load_ffn_w(0)
            cnt = 0
            for pi in range(4):
                if pi + 1 < 4:
                    wf[pi + 1] = load_ffn_w(pi + 1)
                (wg, wgv), (wu, wuv), (wo, wov) = wf[pi]
                h0, nh = pieces[pi]
                for ci, tiles in enumerate(chunks):
                    ntk = len(tiles) * 128
                    hb = hb4[cnt % 2]
                    a_ = aT[cnt % 2]
                    cnt += 1
                    hbv = hb.ap.rearrange("p (k t c) -> p k t c", k=8, t=4)
                    for ti, t in enumerate(tiles):
                        dma("sp", hbv[:, :, ti, :], ht_d[t].rearrange("p (k c) -> p k c", k=8), [R_ht[t]], [hb])
                    hbk = hb.ap.rearrange("p (k n) -> p k n", k=8)
                    av = v3(a_.ap, 6)
                    for hc in range(nh):
                        pg, pu = (hc % 2) * 2, (hc % 2) * 2 + 1
                        for kc in range(8):
                            mm(psum[pg][:, 0:ntk], wgv[:, kc, hc * 128:(hc + 1) * 128], hbk[:, kc, 0:ntk], kc == 0, kc == 7,
                               [wg, hb], [PS[pg]])
                        for kc in range(8):
                            mm(psum[pu][:, 0:ntk], wuv[:, kc, hc * 128:(hc + 1) * 128], hbk[:, kc, 0:ntk], kc == 0, kc == 7,
                               [wu, hb], [PS[pu]])
                        s_ = sl[hc % 2]
                        act(s_.ap[:, 0:ntk], psum[pg][:, 0:ntk], AF.Silu, [PS[pg]], [s_])
                        tt("dve", av[:, hc, 0:ntk], s_.ap[:, 0:ntk], psum[pu][:, 0:ntk], ALU.mult, [s_, PS[pu]], [a_])
                    for ti, t in enumerate(tiles):
                        j = 1 if t >= 16 else 0
                        tx = tmpx[ti % 2]
                        for nch in range(2):
                            pb = 4 + (ti % 2) * 2 + nch
                            for hc in range(nh):
                                mm(psum[pb][:], av[:, hc, ti * 128:(ti + 1) * 128], wov[:, hc, nch * 512:(nch + 1) * 512],
                                   hc == 0, hc == nh - 1, [a_, wo], [PS[pb]])
                            tt("dve", tx.ap[:, nch * 512:(nch + 1) * 512], psum[pb][:],
                               gts[j].ap[:, nch * 512:(nch + 1) * 512], ALU.mult, [PS[pb], gts[j]], [tx])
                        tt("pool", Xv[:, t, :], Xv[:, t, :], tx.ap, ALU.add, [XR[t], tx], [XR[t]])
                for (b, _) in (wf[pi][0], wf[pi][1], wf[pi][2]):
                    b.free()
            for b in gts + hb4 + aT + sl + tmpx:
                b.free()

        for i in range(16):
            dma("sp", out_d[i * 128:(i + 1) * 128, :], Xv[:, i, :], [XR[i]], [R_out])
        P.emit()
        _CACHE["peak"] = A.peak
        _CACHE["nops"] = len(P.ops)
    return nc


def _host_consts():
    if "consts" in _CACHE:
        return _CACHE["consts"]
    cosa, sina = _rope_tables(64)
    cosb, sinb = _rope_tables(32)
    cn, sn, c256, s256, c128, ns128 = _dft_tables()
    c = dict(identf=np.eye(128, dtype=np.float32),
             cosa=cosa.reshape(128, -1), sina=sina.reshape(128, -1),
             cosb=cosb.reshape(128, -1), sinb=sinb.reshape(128, -1),
             cn=cn.reshape(8, 128, -1), sn=sn.reshape(8, 128, -1),
             c256=c256.reshape(128, -1), s256=s256.reshape(128, -1), c128=c128, ns128=ns128)
    _CACHE["consts"] = c
    return c


def kernel(x, c, ctx, c_ctx, w_mod, b_mod, g_norm1, g_norm2, w_in, g_q_gqa, g_k_gqa, g_cq, g_ckv, w_uq, w_ukv,
           g_q_nope, g_k_nope, g_q_rope, g_k_rope, b_gate, w_br_a, w_br_b, w_br_c, w_out, w_ffn_in, w_ffn_out):
    f = lambda a: np.ascontiguousarray(np.asarray(a, dtype=np.float32))
    x, c, ctx, c_ctx = f(x), f(c), f(ctx), f(c_ctx)
    B = x.shape[0]
    if "nc" not in _CACHE:
        _CACHE["nc"] = build_nc()
    nc = _CACHE["nc"]
    consts = _host_consts()
    b_mod = f(b_mod)
    vrow = np.zeros((DEPTH, RV_W), np.float32)
    for name, arr in (("gqa_q", g_q_gqa), ("gqa_k", g_k_gqa), ("q_nope", g_q_nope), ("k_nope", g_k_nope),
                      ("q_rope", g_q_rope), ("k_rope", g_k_rope), ("b_gate", b_gate)):
        o, w = RV[name]
        vrow[:, o:o + w] = f(arr)
    vrow[:, RV["b_gt1"][0]:RV["b_gt1"][0] + D] = b_mod[:, 2 * D:3 * D]
    vrow[:, RV["b_gt2"][0]:RV["b_gt2"][0] + D] = b_mod[:, 5 * D:6 * D]
    vfm = np.zeros((128, DEPTH, FM_W), np.float32)

    def fmaj(v):
        L = v.shape[0]
        return f(v).reshape(L, -1, 128).transpose(2, 0, 1)

    for name, arr in (("g1", g_norm1), ("g2", g_norm2), ("bmod", b_mod), ("gcq", g_cq), ("gckv", g_ckv)):
        o, w = FM[name]
        vfm[:, :, o:o + w] = fmaj(arr)
    vfm = np.ascontiguousarray(vfm.reshape(128, DEPTH * FM_W))
    shared = dict(w_mod=f(w_mod), w_in=f(w_in), w_uq=f(w_uq), w_ukv=f(w_ukv), w_br_a=f(w_br_a), w_br_b=f(w_br_b),
                  w_br_c=f(w_br_c), w_out=f(w_out), w_ffn_in=f(w_ffn_in), w_ffn_out=f(w_ffn_out), vrow=vrow, vfm=vfm)
    shared.update(consts)
    in_maps = []
    for b in range(B):
        c2 = np.stack([c[b], c_ctx], axis=-1).reshape(8, 128, 2).transpose(1, 0, 2).reshape(128, 16)
        m = dict(shared)
        m.update(x=x[b], ctx=ctx[b], c2=np.ascontiguousarray(c2))
        in_maps.append(m)
    res = run_bass_kernel_spmd(nc, in_maps, core_ids=list(range(B)))
    return np.stack([np.asarray(r["out"], dtype=np.float32) for r in res.results], axis=0)
```

```python
import contextlib
import numpy as np
import ml_dtypes
import concourse.bass as bass
import concourse.mybir as mybir
from concourse.bass_utils import run_bass_kernel_spmd

F32 = mybir.dt.float32
BF16 = mybir.dt.bfloat16
AF = mybir.ActivationFunctionType
ALU = mybir.AluOpType
AX = mybir.AxisListType

D = 1024
NLAT = 2048
NCTX = 256
NT = 18
DEPTH = 2
EPS = 1e-6
FFN_H = 2816
IN_W = 5024
GQA_SCALE = 64 ** -0.5
MLA_SCALE = 96 ** -0.5
N_DMA_SEMS = 64
ARENA_BYTES = 207 * 1024


class Res:
    __slots__ = ("name", "last_w", "readers")

    def __init__(self, name):
        self.name = name
        self.last_w = None
        self.readers = []


class Op:
    __slots__ = ("id", "eng", "fn", "deps", "dma", "pos", "waits", "signals", "sigval", "dsem", "dval")

    def __init__(self, id, eng, fn, deps, dma):
        self.id = id
        self.eng = eng
        self.fn = fn
        self.deps = deps
        self.dma = dma
        self.pos = -1
        self.waits = []
        self.signals = False
        self.sigval = 0
        self.dsem = -1
        self.dval = 0


class Prog:
    ENGS = ("pe", "act", "dve", "pool", "sp")

    def __init__(self, nc):
        self.nc = nc
        self.ops = []
        self.dma_count = 0
        self.dma_count_sw = 0
        self.dma_last_on_sem = {}

    def op(self, eng, fn, reads=(), writes=(), dma=False):
        deps = {}
        for w in writes:
            if w.last_w is not None:
                deps[w.last_w] = False
            for rr in w.readers:
                deps[rr] = False
        for r in reads:
            if r.last_w is not None:
                deps[r.last_w] = True
        o = Op(len(self.ops), eng, fn, deps, dma)
        if dma:
            half = N_DMA_SEMS // 2
            if eng == "pool":
                s = self.dma_count_sw % half
                self.dma_count_sw += 1
            else:
                s = half + self.dma_count % half
                self.dma_count += 1
            prev = self.dma_last_on_sem.get(s)
            o.dsem = s
            if prev is not None:
                o.deps.setdefault(prev.id, False)
                o.dval = prev.dval + 16
            else:
                o.dval = 16
            self.dma_last_on_sem[s] = o
        for r in reads:
            r.readers.append(o.id)
        for w in writes:
            w.last_w = o.id
            w.readers = []
        self.ops.append(o)
        return o

    def schedule(self):
        ops = self.ops
        per_eng = {e: [] for e in self.ENGS}
        for o in ops:
            o.pos = len(per_eng[o.eng])
            per_eng[o.eng].append(o)
        known = {e: {} for e in self.ENGS}
        snap = {}
        for o in ops:
            k = known[o.eng]
            for d in sorted(o.deps):
                dop = ops[d]
                if dop.dma:
                    key = ("d", dop.dsem)
                    need = dop.dval
                else:
                    if dop.eng == o.eng and (o.eng == "pe" or not o.deps[d]):
                        continue
                    key = dop.eng
                    need = dop.pos + 1
                if k.get(key, 0) >= need:
                    continue
                o.waits.append(d)
                dop.signals = True
                ks = snap.get(d)
                if ks is not None:
                    for kk, vv in ks.items():
                        if k.get(kk, 0) < vv:
                            k[kk] = vv
                if k.get(key, 0) < need:
                    k[key] = need
            snap[o.id] = dict(k)
        for e in self.ENGS:
            c = 0
            for o in per_eng[e]:
                if o.dma:
                    continue
                if o.signals:
                    c += 1
                    o.sigval = c
        self.per_eng = per_eng

    def emit(self):
        nc = self.nc
        self.schedule()
        ops = self.ops
        with contextlib.ExitStack() as st:
            tsem = {e: st.enter_context(nc.semaphore("ts_" + e)) for e in ("pe", "act", "dve", "pool")}
            dsems = [st.enter_context(nc.semaphore("dm_%d" % i)) for i in range(N_DMA_SEMS)]
            block = st.enter_context(nc.Block())

            def run(engname):
                def body(eng):
                    for o in self.per_eng[engname]:
                        for d in o.waits:
                            dop = ops[d]
                            if dop.dma:
                                eng.wait_ge(dsems[dop.dsem], dop.dval)
                            else:
                                eng.wait_ge(tsem[dop.eng], dop.sigval)
                        ins = o.fn(eng)
                        if o.dma:
                            ins.then_inc(dsems[o.dsem], 16)
                        elif o.signals:
                            ins.then_inc(tsem[o.eng], 1)
                    for s, lo in self.dma_last_on_sem.items():
                        if lo.eng == engname:
                            eng.wait_ge(dsems[s], lo.dval)
                return body

            block.tensor(run("pe"))
            block.scalar(run("act"))
            block.vector(run("dve"))
            block.gpsimd(run("pool"))
            block.sync(run("sp"))


class Arena:
    def __init__(self, tensor, nbytes):
        self.t = tensor
        self.size = nbytes
        self.live = []
        self.ghosts = []
        self.peak = 0

    def alloc(self, name, nbytes, dtype=BF16, nres=1):
        req = nbytes
        nbytes = (nbytes + 63) // 64 * 64
        off = 0
        for (o, s, _) in sorted(self.live, key=lambda b: b[0]):
            if off + nbytes <= o:
                break
            off = max(off, o + s)
        if off + nbytes > self.size:
            raise RuntimeError("arena full allocating %s (%d B); live=%s" % (
                name, nbytes, [(r[2][0].name, r[1]) for r in self.live]))
        self.peak = max(self.peak, off + nbytes)
        ids = []
        keep = []
        for (go, gs, gids) in self.ghosts:
            if go < off + nbytes and off < go + gs:
                ids.extend(gids)
                if go >= off and go + gs <= off + nbytes:
                    continue
            keep.append((go, gs, gids))
        self.ghosts = keep
        ids = sorted(set(ids))
        res = []
        for i in range(nres):
            r = Res("%s_%d" % (name, i))
            r.readers = list(ids)
            res.append(r)
        self.live.append((off, nbytes, res))
        ap = self.t[:, off // 2:(off + req) // 2]
        if dtype == F32:
            ap = ap.bitcast(F32)
        return Buf(ap, res, off, self)

    def free(self, buf):
        for i, (o, s, res) in enumerate(self.live):
            if o == buf.off and res is buf.res:
                ids = []
                for r in res:
                    ids.extend(r.readers)
                    if r.last_w is not None:
                        ids.append(r.last_w)
                self.ghosts.append((o, s, sorted(set(ids))))
                del self.live[i]
                return
        raise RuntimeError("free of unknown buffer")


class Buf:
    def __init__(self, ap, res, off, arena):
        self.ap = ap
        self.res = res
        self.off = off
        self.arena = arena

    @property
    def r(self):
        return self.res[0]

    def free(self):
        self.arena.free(self)


def _rope_tables(dim):
    half = dim // 2
    freqs = 10000.0 ** (-np.arange(0, half, 2, dtype=np.float32) / half)
    t = np.arange(NLAT)
    row = (t // 64).astype(np.float32)
    col = (t % 64).astype(np.float32)

    def ax(pos):
        a = pos[:, None] * freqs[None, :]
        return np.concatenate([a, a], axis=-1)

    ang = np.concatenate([ax(row), ax(col)], axis=-1).astype(np.float32)
    cos = np.cos(ang).astype(np.float32)
    sin = np.sin(ang).astype(np.float32)
    q = dim // 4
    sgn = np.ones((dim,), np.float32)
    for a in range(2):
        sgn[a * 2 * q: a * 2 * q + q] = -1.0
    sins = sin * sgn[None, :]
    cos_f = np.ones((NT * 128, dim), np.float32)
    sin_f = np.zeros((NT * 128, dim), np.float32)
    cos_f[:NLAT] = cos
    sin_f[:NLAT] = sins
    cos_t = cos_f.reshape(NT, 128, dim).transpose(1, 0, 2)
    sin_t = sin_f.reshape(NT, 128, dim).transpose(1, 0, 2)
    return np.ascontiguousarray(cos_t), np.ascontiguousarray(sin_t)


def _dft_tables():
    bf = ml_dtypes.bfloat16
    n = np.arange(NLAT, dtype=np.int64)
    ph = (np.outer(n, n) % NLAT).astype(np.float64) * (2.0 * np.pi / NLAT)
    cn = (np.cos(ph) / np.sqrt(NLAT)).astype(np.float32)
    sn = (np.sin(ph) / np.sqrt(NLAT)).astype(np.float32)

    def lay(m):
        return np.ascontiguousarray(m.reshape(16, 128, 8, 256).transpose(2, 1, 0, 3)).astype(bf)

    n2 = np.arange(NCTX, dtype=np.int64)
    ph2 = (np.outer(n2, n2) % NCTX).astype(np.float64) * (2.0 * np.pi / NCTX)
    c256 = (np.cos(ph2) / np.sqrt(NCTX)).astype(np.float32)
    s256 = (np.sin(ph2) / np.sqrt(NCTX)).astype(np.float32)

    def lay2(m):
        return np.ascontiguousarray(m.reshape(2, 128, 256).transpose(1, 0, 2)).astype(bf)

    n3 = np.arange(128, dtype=np.int64)
    ph3 = (np.outer(n3, n3) % 128).astype(np.float64) * (2.0 * np.pi / 128)
    c128 = (np.cos(ph3) / np.sqrt(128.0)).astype(np.float32).astype(bf)
    ns128 = (-np.sin(ph3) / np.sqrt(128.0)).astype(np.float32).astype(bf)
    return lay(cn), lay(sn), lay2(c256), lay2(s256), c128, ns128


RV = {}
_o = 0
for _n, _w in (("gqa_q", 64), ("gqa_k", 64), ("q_nope", 64), ("k_nope", 64), ("q_rope", 32), ("k_rope", 32),
               ("b_gate", 3072), ("b_gt1", 1024), ("b_gt2", 1024)):
    RV[_n] = (_o, _w)
    _o += _w
RV_W = _o
FM = {"g1": (0, 8), "g2": (8, 8), "bmod": (16, 48), "gcq": (64, 3), "gckv": (67, 2)}
FM_W = 69

_CACHE = {}


def build_nc(depth=DEPTH, dbg=()):
    nc = bass.Bass("TRN2", target_bir_lowering=False)
    din = lambda n, s, dt=F32: nc.dram_tensor(n, list(s), dt, kind="ExternalInput").ap()
    x_d = din("x", [NLAT, D])
    ctx_d = din("ctx", [NCTX, D])
    c2_d = din("c2", [128, 16])
    w_mod_d = din("w_mod", [DEPTH, D, 6 * D])
    w_in_d = din("w_in", [DEPTH, D, IN_W])
    w_uq_d = din("w_uq", [DEPTH, 384, 768])
    w_ukv_d = din("w_ukv", [DEPTH, 256, 1024])
    w_br_d = [din("w_br_" + s, [DEPTH, 512, D]) for s in "abc"]
    w_out_d = din("w_out", [DEPTH, D, D])
    w_fi_d = din("w_ffn_in", [DEPTH, D, 2 * FFN_H])
    w_fo_d = din("w_ffn_out", [DEPTH, FFN_H, D])
    vrow_d = din("vrow", [DEPTH, RV_W])
    vfm_d = din("vfm", [128, DEPTH * FM_W])
    identf_d = din("identf", [128, 128])
    cosa_d = din("cosa", [128, NT * 64])
    sina_d = din("sina", [128, NT * 64])
    cosb_d = din("cosb", [128, NT * 32])
    sinb_d = din("sinb", [128, NT * 32])
    cn_d = din("cn", [8, 128, 16 * 256], BF16)
    sn_d = din("sn", [8, 128, 16 * 256], BF16)
    c256_d = din("c256", [128, 512], BF16)
    s256_d = din("s256", [128, 512], BF16)
    c128_d = din("c128", [128, 128], BF16)
    ns128_d = din("ns128", [128, 128], BF16)
    out_d = nc.dram_tensor("out", [NLAT, D], F32, kind="ExternalOutput").ap()
    ht_d = nc.dram_tensor("ht_scr", [NT, 128, 1024], BF16).ap()
    y_d = [nc.dram_tensor("y_scr%d" % i, [128, 4, NT * 128], BF16).ap() for i in range(3)]
    dbg_d = {}

    with contextlib.ExitStack() as st:
        arena_t = st.enter_context(nc.sbuf_tensor("arena", [128, ARENA_BYTES // 2], BF16))
        psum = [st.enter_context(nc.psum_tensor("ps%d" % i, [128, 512], F32)) for i in range(8)]
        P = Prog(nc)
        A = Arena(arena_t, ARENA_BYTES)
        PS = [Res("ps%d" % i) for i in range(8)]
        R_ht = [Res("ht%d" % i) for i in range(NT)]
        R_y = [Res("y%d" % i) for i in range(3)]
        R_out = Res("out")

        def psb(i):
            return psum[i][:].bitcast(BF16)

        def rl(xs):
            out = []
            for v in xs:
                if isinstance(v, Buf):
                    out.extend(v.res)
                elif isinstance(v, Res):
                    out.append(v)
                elif v is not None:
                    out.extend(v)
            return out

        def dma(eng, out, in_, reads, writes):
            P.op(eng, lambda e: e.dma_start(out=out, in_=in_), rl(reads), rl(writes), dma=True)

        def mm(out, lhsT, rhs, start, stop, reads, writes):
            P.op("pe", lambda e: e.matmul(out, lhsT, rhs, start=start, stop=stop), rl(reads), rl(writes))

        def tr(out, in_, ident, reads, writes):
            P.op("pe", lambda e: e.transpose(out, in_, ident), rl(reads), rl(writes))

        def act(out, in_, func, reads, writes, bias=0.0, scale=1.0, accum=None):
            if accum is None:
                P.op("act", lambda e: e.activation(out, in_, func, bias=bias, scale=scale), rl(reads), rl(writes))
            else:
                P.op("act", lambda e: e.activation(out, in_, func, bias=bias, scale=scale, accum_out=accum),
                     rl(reads), rl(writes))

        def tt(eng, out, in0, in1, op, reads, writes):
            P.op(eng, lambda e: e.tensor_tensor(out, in0, in1, op), rl(reads), rl(writes))

        def ts(eng, out, in0, s1, s2, op0, op1, reads, writes):
            if s2 is None:
                P.op(eng, lambda e: e.tensor_scalar(out, in0, s1, None, op0), rl(reads), rl(writes))
            else:
                P.op(eng, lambda e: e.tensor_scalar(out, in0, s1, s2, op0, op1), rl(reads), rl(writes))

        def cp(eng, out, in_, reads, writes):
            if eng == "act":
                P.op("act", lambda e: e.activation(out, in_, AF.Copy), rl(reads), rl(writes))
            else:
                P.op(eng, lambda e: e.tensor_copy(out, in_), rl(reads), rl(writes))

        def recip(out, in_, reads, writes):
            P.op("dve", lambda e: e.reciprocal(out, in_), rl(reads), rl(writes))

        def red(out, in_, reads, writes):
            P.op("dve", lambda e: e.tensor_reduce(out, in_, AX.X, ALU.add), rl(reads), rl(writes))

        def memset(eng, ap, val, writes):
            P.op(eng, lambda e: e.memset(ap, val), [], rl(writes))

        def v3(ap, a):
            return ap.rearrange("p (a b) -> p a b", a=a)

        def rsqrt_into(dst, ss, n, inv, tmp, reads, writes_tmp, writes_dst):
            act(tmp, ss, AF.Sqrt, reads, writes_tmp, bias=EPS_AP[0], scale=inv)
            recip(dst, tmp, writes_tmp, writes_dst)

        X = A.alloc("X", NT * D * 4, F32, nres=NT)
        Xv = v3(X.ap, NT)
        XR = X.res
        identf = A.alloc("identf", 512, F32)
        identb = A.alloc("identb", 256, BF16)
        c128 = A.alloc("c128", 256, BF16)
        ns128 = A.alloc("ns128", 256, BF16)
        c256 = A.alloc("c256", 1024, BF16)
        s256 = A.alloc("s256", 1024, BF16)
        vfm = A.alloc("vfm", DEPTH * FM_W * 4, F32)
        c2 = A.alloc("c2", 64, F32)
        sc2 = A.alloc("sc2", 32, BF16)
        epsb = A.alloc("eps", 64, F32)
        modT = A.alloc("modT", DEPTH * 48 * 2 * 4, F32)
        amod = A.alloc("amod", DEPTH * 2 * 8 * 2 * 4, F32)
        EPS_AP = [epsb.ap[:, 0:1]]

        for i in range(NLAT // 128):
            dma("sp", Xv[:, i, :], x_d[i * 128:(i + 1) * 128, :], [], [XR[i]])
        for i in range(2):
            dma("sp", Xv[:, 16 + i, :], ctx_d[i * 128:(i + 1) * 128, :], [], [XR[16 + i]])
        dma("sp", identf.ap, identf_d, [], [identf])
        dma("sp", c128.ap, c128_d, [], [c128])
        dma("sp", ns128.ap, ns128_d, [], [ns128])
        dma("sp", c256.ap, c256_d, [], [c256])
        dma("sp", s256.ap, s256_d, [], [s256])
        dma("sp", vfm.ap, vfm_d, [], [vfm])
        dma("sp", c2.ap, c2_d, [], [c2])
        memset("dve", epsb.ap, EPS, [epsb])
        cp("dve", identb.ap, identf.ap, [identf], [identb])
        act(sc2.ap, c2.ap, AF.Silu, [c2], [sc2])
        sc2v = v3(sc2.ap, 8)
        modTv = modT.ap.rearrange("p (l f j) -> p l f j", l=DEPTH, f=48)
        amodv = amod.ap.rearrange("p (l n k j) -> p l n k j", l=DEPTH, n=2, k=8)
        vfmv = v3(vfm.ap, DEPTH)

        def fm(l, name):
            o, w = FM[name]
            return vfmv[:, l, o:o + w]

        def wview(w_ap, l, c0, c1):
            return w_ap[l].rearrange("(kc p) n -> p kc n", p=128)[:, :, c0:c1]

        def load_w(name, w_ap, l, c0, c1, kcs=None):
            K = w_ap.shape[1]
            nk = K // 128
            n = c1 - c0
            b = A.alloc(name, nk * n * 2, BF16)
            bv = v3(b.ap, nk)
            src = wview(w_ap, l, c0, c1)
            for kc in range(nk):
                dma("pool", bv[:, kc, :], src[:, kc, :], [], [b])
            return b, bv

        def load_row(name, l, key, eng="sp"):
            o, w = RV[key]
            b = A.alloc(name, w * 4, F32)
            dma(eng, b.ap, vrow_d[l, o:o + w].partition_broadcast(128), [], [b])
            return b

        MOD_BLKS = (0, 1, 3, 4)

        def mod_load(l, blk):
            return load_w("wmod", w_mod_d, l, blk * D, (blk + 1) * D)

        def mod_compute(l, blk, wb, wbv):
            pv = psum[0][:, 0:16].rearrange("p (f j) -> p f j", f=8)
            for f in range(8):
                for kc in range(8):
                    mm(pv[:, f, :], wbv[:, kc, f * 128:(f + 1) * 128], sc2v[:, kc, :], kc == 0, kc == 7,
                       [wb, sc2], [PS[0]])
            bm = fm(l, "bmod")[:, blk * 8:(blk + 1) * 8]
            tt("dve", modTv[:, l, blk * 8:(blk + 1) * 8, :], pv, bm.unsqueeze(2).to_broadcast([128, 8, 2]), ALU.add,
               [PS[0], vfm], [modT])
            wb.free()

        def mod_finish(l):
            for n, (blk, g) in enumerate(((1, "g1"), (4, "g2"))):
                P.op("dve", lambda e, n=n, blk=blk, g=g: e.scalar_tensor_tensor(
                    out=amodv[:, l, n, :, :], in0=modTv[:, l, blk * 8:(blk + 1) * 8, :], scalar=1.0,
                    in1=fm(l, g).unsqueeze(2).to_broadcast([128, 8, 2]), op0=ALU.add, op1=ALU.mult),
                    rl([modT, vfm]), rl([amod]))

        def compute_mod_fm(l):
            for blk in MOD_BLKS:
                wb, wbv = mod_load(l, blk)
                mod_compute(l, blk, wb, wbv)
            mod_finish(l)

        def compute_gate_rows(l, which, need_ctx):
            blk = 2 if which == 0 else 5
            wb, wbv = load_w("wmodg", w_mod_d, l, blk * D, (blk + 1) * D)
            brow = load_row("bgt", l, "b_gt1" if which == 0 else "b_gt2")
            scb = A.alloc("scb", 8 * 2 * 128 * 2, BF16)
            scbv = scb.ap.rearrange("p (k j m) -> p k j m", k=8, j=2)
            cp("dve", scbv, sc2v.unsqueeze(3).to_broadcast([128, 8, 2, 128]), [sc2], [scb])
            outs = []
            for j in range(2 if need_ctx else 1):
                g = A.alloc("gt%d" % j, 4096, F32)
                for nch in range(2):
                    for kc in range(8):
                        mm(psum[nch][:], scbv[:, kc, j, :], wbv[:, kc, nch * 512:(nch + 1) * 512], kc == 0, kc == 7,
                           [scb, wb], [PS[nch]])
                    tt("dve", g.ap[:, nch * 512:(nch + 1) * 512], psum[nch][:], brow.ap[:, nch * 512:(nch + 1) * 512],
                       ALU.add, [PS[nch], brow], [g])
                outs.append(g)
            wb.free()
            brow.free()
            scb.free()
            return outs

        def build_ht(l, n, tiles):
            xn = [A.alloc("xn%d" % i, 4096, F32) for i in range(2)]
            junk = A.alloc("junk", 2048, BF16)
            hts = [A.alloc("hts%d" % i, 2048, BF16) for i in range(2)]
            st_ = A.alloc("nstat", 64, F32)
            for it, t in enumerate(tiles):
                j = 1 if t >= 16 else 0
                xb = xn[it % 2]
                hb = hts[it % 2]
                hbv = v3(hb.ap, 8)
                sv = st_.ap
                act(junk.ap, Xv[:, t, :], AF.Square, [XR[t]], [junk, st_], accum=sv[:, 0:1])
                act(sv[:, 1:2], sv[:, 0:1], AF.Sqrt, [st_], [st_], bias=EPS_AP[0], scale=1.0 / D)
                recip(sv[:, 2:3], sv[:, 1:2], [st_], [st_])
                ts("dve", xb.ap, Xv[:, t, :], sv[:, 2:3], None, ALU.mult, None, [XR[t], st_], [xb])
                for half in range(2):
                    pb = 6 + half
                    pv = v3(psum[pb][:], 4)
                    for c in range(4):
                        kc = half * 4 + c
                        tr(pv[:, c, :], xb.ap[:, kc * 128:(kc + 1) * 128], identf.ap, [xb, identf], [PS[pb]])
                    for c in range(4):
                        kc = half * 4 + c
                        a_ap = amodv[:, l, n, kc, j:j + 1]
                        b_ap = modTv[:, l, (0 if n == 0 else 24) + kc, j:j + 1]
                        if c % 2 == 0:
                            act(hbv[:, kc, :], pv[:, c, :], AF.Identity, [PS[pb], amod, modT], [hb], bias=b_ap, scale=a_ap)
                        else:
                            ts("dve", hbv[:, kc, :], pv[:, c, :], a_ap, b_ap, ALU.mult, ALU.add, [PS[pb], amod, modT], [hb])
                dma("sp", ht_d[t], hb.ap, [hb], [R_ht[t]])
            for b in xn + hts + [junk, st_]:
                b.free()

        def headnorm(src, nh, hd, stride, off, gain, cos, sin, out, reads_src, S, stat, wr_out):
            sv = v3(src[:, 0:nh * stride], nh)[:, :, off:off + hd]
            n = nh * hd
            s1 = v3(S[0].ap[:, 0:n], nh)
            s2 = v3(S[1].ap[:, 0:n], nh)
            s3 = v3(S[2].ap[:, 0:n], nh)
            stv = stat.ap
            act(s1, sv, AF.Square, reads_src, [S[0]])
            red(stv[:, 0:nh], s1, [S[0]], [stat])
            act(stv[:, 16:16 + nh], stv[:, 0:nh], AF.Sqrt, [stat], [stat], bias=EPS_AP[0], scale=1.0 / hd)
            recip(stv[:, 32:32 + nh], stv[:, 16:16 + nh], [stat], [stat])
            tt("dve", s2, sv, stv[:, 32:32 + nh].unsqueeze(2).to_broadcast([128, nh, hd]), ALU.mult,
               list(reads_src) + [stat], [S[1]])
            gb = gain.ap.unsqueeze(1).to_broadcast([128, nh, hd])
            if cos is None:
                tt("dve", out, s2, gb, ALU.mult, [S[1], gain], wr_out)
                return
            tt("dve", s1, s2, gb, ALU.mult, [S[1], gain], [S[0]])
            q = hd // 4
            cb = cos.unsqueeze(1).to_broadcast([128, nh, hd])
            tt("dve", s2, s1, cb, ALU.mult, [S[0], ROPE], [S[1]])
            x5 = s1.rearrange("p h (a b f) -> p h a b f", a=2, b=2)
            u5 = s3.rearrange("p h (a b f) -> p h a b f", a=2, b=2)
            sn5 = sin.unsqueeze(1).to_broadcast([128, nh, hd]).rearrange("p h (a b f) -> p h a b f", a=2, b=2)
            for part in range(2):
                for a in range(2):
                    tt("pool", u5[:, :, a, part, :], x5[:, :, a, 1 - part, :], sn5[:, :, a, part, :], ALU.mult,
                       [S[0], ROPE], [S[2]])
            tt("dve", out, s2, s3, ALU.add, [S[1], S[2]], wr_out)

        ROPE = Res("rope")

        def attention(units, kt_list, qcols, nq, scale, YT):
            for ui, u in enumerate(units):
                sb_ = (ui % 2) * 2
                ob_ = 4 + (ui % 2) * 2 + ((ui // 2) % 2)
                ops_ = psum[ob_][:, 0:nq]
                n = len(kt_list)

                def score(ki):
                    kt = kt_list[ki]
                    sbank = sb_ + (ki % 2)
                    mm(psum[sbank][:, 0:nq], u["kT"][:, kt * 128:(kt + 1) * 128], u["qT"][:, qcols:qcols + nq], True, True,
                       u["reads"], [PS[sbank]])

                score(0)
                for ki, kt in enumerate(kt_list):
                    if ki + 1 < n:
                        score(ki + 1)
                    sbank = sb_ + (ki % 2)
                    pt = PT[(ui % 2) * 2 + (ki % 2)]
                    act(pt.ap[:, 0:nq], psum[sbank][:, 0:nq], AF.Exp, [PS[sbank]], [pt], scale=scale)
                    mm(ops_, u["v"](kt), pt.ap[:, 0:nq], ki == 0, ki == n - 1, [pt] + u["vreads"], [PS[ob_]])
                ob = u["ob"]
                rc = REC[ui % 2]
                recip(rc.ap[0:64, 0:nq], psum[ob_][64:128, 0:nq], [PS[ob_]], [rc])
                tt("dve", YT[ob:ob + 64, u["ych"], qcols:qcols + nq], psum[ob_][0:64, 0:nq], rc.ap[0:64, 0:nq], ALU.mult,
                   [PS[ob_], rc], u["ywr"])

        for l in range(depth):
            last = (l == depth - 1)
            act_tiles = list(range(16)) if last else list(range(NT))
            if l == 0:
                compute_mod_fm(0)
            build_ht(l, 0, list(range(NT)))
            S = [A.alloc("S%d" % i, 2048, F32) for i in range(3)]
            stat = A.alloc("stat", 256, F32)
            hbuf = [A.alloc("hbuf%d" % i, 2048, BF16) for i in range(2)]
            PT = [A.alloc("PT%d" % i, 1024, BF16) for i in range(4)]
            REC = [A.alloc("REC%d" % i, 2048, F32) for i in range(2)]

            def load_ht(it, t):
                hb = hbuf[it % 2]
                dma("sp", hb.ap, ht_d[t], [R_ht[t]], [hb])
                return hb, v3(hb.ap, 8)

            YT = A.alloc("YT", 4 * NT * 128 * 2, BF16)
            YTv = v3(YT.ap, 4)
            jobs = [(qc * 512, 512, list(range(NT))) for qc in range(4)]
            if not last:
                jobs.append((2048, 256, [16, 17]))

            wB, wBv = load_w("wB", w_in_d, l, 768, 1440)
            wuq, wuqv = load_w("wuq", w_uq_d, l, 0, 768)
            wukv, wukvv = load_w("wukv", w_ukv_d, l, 0, 1024)
            gqn = load_row("gqn", l, "q_nope")
            gkn = load_row("gkn", l, "k_nope")
            gqr = load_row("gqr", l, "q_rope")
            gkr = load_row("gkr", l, "k_rope")
            cosb = A.alloc("cosb", NT * 32 * 4, F32)
            sinb = A.alloc("sinb", NT * 32 * 4, F32)
            dma("sp", cosb.ap, cosb_d, [], [cosb, ROPE])
            dma("sp", sinb.ap, sinb_d, [], [sinb, ROPE])
            cosbv = v3(cosb.ap, NT)
            sinbv = v3(sinb.ap, NT)
            CQT = A.alloc("CQT", 3 * NT * 128 * 2, BF16)
            CKVT = A.alloc("CKVT", 2 * NT * 128 * 2, BF16)
            KR = A.alloc("KR", NT * 32 * 2, BF16)
            CQTv = v3(CQT.ap, 3)
            CKVTv = v3(CKVT.ap, 2)
            KRv = v3(KR.ap, NT)
            cn_ = A.alloc("cqn", 384 * 2 + 256 * 2, BF16)
            gcq = fm(l, "gcq")
            gckv = fm(l, "gckv")
            for it, t in enumerate(range(NT)):
                need_q = t in act_tiles
                hb, hbv = load_ht(it, t)
                if need_q:
                    for kc in range(8):
                        mm(psum[0][:, 0:384], hbv[:, kc, :], wBv[:, kc, 0:384], kc == 0, kc == 7, [hb, wB], [PS[0]])
                for kc in range(8):
                    mm(psum[1][:, 0:288], hbv[:, kc, :], wBv[:, kc, 384:672], kc == 0, kc == 7, [hb, wB], [PS[1]])
                sv = stat.ap
                jk = S[0].ap
                if need_q:
                    act(jk[:, 0:384], psum[0][:, 0:384], AF.Square, [PS[0]], [S[0], stat], accum=sv[:, 48:49])
                    act(sv[:, 51:52], sv[:, 48:49], AF.Sqrt, [stat], [stat], bias=EPS_AP[0], scale=1.0 / 384)
                act(jk[:, 0:256], psum[1][:, 0:256], AF.Square, [PS[1]], [S[0], stat], accum=sv[:, 49:50])
                act(sv[:, 52:53], sv[:, 49:50], AF.Sqrt, [stat], [stat], bias=EPS_AP[0], scale=1.0 / 256)
                if not need_q:
                    memset("dve", sv[:, 51:52], 1.0, [stat])
                recip(sv[:, 54:56], sv[:, 51:53], [stat], [stat])
                if need_q:
                    act(cn_.ap[:, 0:384], psum[0][:, 0:384], AF.Copy, [PS[0], stat], [cn_], scale=sv[:, 54:55])
                act(cn_.ap[:, 384:640], psum[1][:, 0:256], AF.Copy, [PS[1], stat], [cn_], scale=sv[:, 55:56])
                headnorm(psum[1][:, 256:288], 1, 32, 32, 0, gkr, cosbv[:, t, :], sinbv[:, t, :],
                         KRv[:, t, :].unsqueeze(1), [PS[1]], S, stat, [KR])
                pv = v3(psb(2)[:, 0:640], 5)
                for c in range(5):
                    if c < 3 and not need_q:
                        continue
                    tr(pv[:, c, :], cn_.ap[:, c * 128:(c + 1) * 128], identb.ap, [cn_, identb], [PS[2]])
                for c in range(5):
                    if c < 3 and not need_q:
                        continue
                    dst = CQTv[:, c, t * 128:(t + 1) * 128] if c < 3 else CKVTv[:, c - 3, t * 128:(t + 1) * 128]
                    gcol = gcq[:, c:c + 1] if c < 3 else gckv[:, c - 3:c - 2]
                    dbuf = CQT if c < 3 else CKVT
                    if c % 2 == 0:
                        act(dst, pv[:, c, :], AF.Copy, [PS[2], vfm], [dbuf], scale=gcol)
                    else:
                        ts("dve", dst, pv[:, c, :], gcol, None, ALU.mult, None, [PS[2], vfm], [dbuf])
            wB.free()
            cn_.free()
            gkr.free()
            HG = 2
            KBT = A.alloc("KBT", HG * NT * 128 * 2, BF16)
            QBT = A.alloc("QBT", HG * NT * 128 * 2, BF16)
            VB = A.alloc("VB", NT * HG * 128 * 2, BF16)
            KBTv = v3(KBT.ap, HG)
            QBTv = v3(QBT.ap, HG)
            VBv = VB.ap.rearrange("p (t h c) -> p t h c", t=NT, h=HG)
            ktk = A.alloc("ktk", HG * 96 * 2, BF16)
            qtk = A.alloc("qtk", HG * 96 * 2, BF16)
            ktkv = v3(ktk.ap, HG)
            qtkv = v3(qtk.ap, HG)
            memset("pool", VB.ap, 1.0, [VB])
            for hg in range(8 // HG):
                for it, t in enumerate(range(NT)):
                    need_q = t in act_tiles
                    for kc in range(2):
                        mm(psum[0][:, 0:HG * 128], CKVTv[:, kc, t * 128:(t + 1) * 128],
                           wukvv[:, kc, hg * HG * 128:(hg + 1) * HG * 128], kc == 0, kc == 1, [CKVT, wukv], [PS[0]])
                    headnorm(psum[0][:], HG, 64, 128, 0, gkn, None, None, ktkv[:, :, 0:64], [PS[0]], S, stat, [ktk])
                    cp("dve", ktkv[:, :, 64:96], KRv[:, t, :].unsqueeze(1).to_broadcast([128, HG, 32]), [KR], [ktk])
                    cp("act", VBv[:, t, :, 0:64], v3(psum[0][:, 0:HG * 128], HG)[:, :, 64:128], [PS[0]], [VB])
                    pv = v3(psb(2)[:, 0:HG * 128], HG)
                    for h in range(HG):
                        tr(pv[0:96, h, :], ktkv[:, h, :], identb.ap, [ktk, identb], [PS[2]])
                    cp("act", KBTv[0:96, :, t * 128:(t + 1) * 128], pv[0:96, :, :], [PS[2]], [KBT])
                    if need_q:
                        for kc in range(3):
                            mm(psum[1][:, 0:HG * 96], CQTv[:, kc, t * 128:(t + 1) * 128],
                               wuqv[:, kc, hg * HG * 96:(hg + 1) * HG * 96], kc == 0, kc == 2, [CQT, wuq], [PS[1]])
                        headnorm(psum[1][:], HG, 64, 96, 0, gqn, None, None, qtkv[:, :, 0:64], [PS[1]], S, stat, [qtk])
                        headnorm(psum[1][:], HG, 32, 96, 64, gqr, cosbv[:, t, :], sinbv[:, t, :], qtkv[:, :, 64:96],
                                 [PS[1]], S, stat, [qtk])
                        pv = v3(psb(3)[:, 0:HG * 128], HG)
                        for h in range(HG):
                            tr(pv[0:96, h, :], qtkv[:, h, :], identb.ap, [qtk, identb], [PS[3]])
                        cp("dve", QBTv[0:96, :, t * 128:(t + 1) * 128], pv[0:96, :, :], [PS[3]], [QBT])
                for (qcols, nq, kts) in jobs:
                    units = []
                    for h in range(HG):
                        hh = hg * HG + h
                        units.append(dict(kT=KBTv[0:96, h, :], qT=QBTv[0:96, h, :],
                                          v=(lambda kt, h=h: VBv[:, kt, h, :]), reads=[KBT, QBT], vreads=[VB],
                                          ob=64 * (hh % 2), ych=hh // 2, ywr=[YT]))
                    attention(units, kts, qcols, nq, MLA_SCALE, YTv)
            dma("sp", y_d[1], YTv, [YT], [R_y[1]])
            for b in (wuq, wukv, gqn, gkn, gqr, cosb, sinb, CQT, CKVT, KR, KBT, QBT, VB, ktk, qtk):
                b.free()

            wA, wAv = load_w("wA", w_in_d, l, 0, 768)
            gq = load_row("gq", l, "gqa_q")
            gk = load_row("gk", l, "gqa_k")
            cosa = A.alloc("cosa", NT * 64 * 4, F32)
            sina = A.alloc("sina", NT * 64 * 4, F32)
            dma("sp", cosa.ap, cosa_d, [], [cosa, ROPE])
            dma("sp", sina.ap, sina_d, [], [sina, ROPE])
            cosav = v3(cosa.ap, NT)
            sinav = v3(sina.ap, NT)
            QT = A.alloc("QaT", 4 * NT * 128 * 2, BF16)
            KT = A.alloc("KaT", 2 * NT * 128 * 2, BF16)
            VA = A.alloc("Va", NT * 2 * 128 * 2, BF16)
            QTv = v3(QT.ap, 4)
            KTv = v3(KT.ap, 2)
            VAv = VA.ap.rearrange("p (t k c) -> p t k c", t=NT, k=2)
            qtok = A.alloc("qtok", 1024, BF16)
            ktok = A.alloc("ktok", 512, BF16)
            memset("pool", VA.ap, 1.0, [VA])
            for it, t in enumerate(range(NT)):
                need_q = t in act_tiles
                hb, hbv = load_ht(it, t)
                if need_q:
                    for kc in range(8):
                        mm(psum[0][:], hbv[:, kc, :], wAv[:, kc, 0:512], kc == 0, kc == 7, [hb, wA], [PS[0]])
                for kc in range(8):
                    mm(psum[1][:, 0:256], hbv[:, kc, :], wAv[:, kc, 512:768], kc == 0, kc == 7, [hb, wA], [PS[1]])
                if need_q:
                    headnorm(psum[0][:], 8, 64, 64, 0, gq, cosav[:, t, :], sinav[:, t, :], v3(qtok.ap, 8),
                             [PS[0]], S, stat, [qtok])
                    pv = v3(psb(2)[:, 0:512], 4)
                    for c in range(4):
                        tr(pv[:, c, :], qtok.ap[:, c * 128:(c + 1) * 128], identb.ap, [qtok, identb], [PS[2]])
                    cp("act", QTv[:, :, t * 128:(t + 1) * 128], pv, [PS[2]], [QT])
                k4 = ktok.ap.rearrange("p (k r d) -> p k r d", k=2, r=2)
                headnorm(psum[1][:, 0:128], 2, 64, 64, 0, gk, cosav[:, t, :], sinav[:, t, :], k4[:, :, 0, :],
                         [PS[1]], S, stat, [ktok])
                cp("dve", k4[:, :, 1, :], k4[:, :, 0, :], [ktok], [ktok])
                pv = v3(psb(3)[:, 0:256], 2)
                for c in range(2):
                    tr(pv[:, c, :], ktok.ap[:, c * 128:(c + 1) * 128], identb.ap, [ktok, identb], [PS[3]])
                cp("act", KTv[:, :, t * 128:(t + 1) * 128], pv, [PS[3]], [KT])
                cp("act", VAv[:, t, :, 0:64], v3(psum[1][:, 128:256], 2), [PS[1]], [VA])
            wA.free()
            for b in (gq, gk, cosa, sina, qtok, ktok):
                b.free()
            for ji, (qcols, nq, kts) in enumerate(jobs):
                if l + 1 < depth and ji < 4:
                    wbm = mod_load(l + 1, MOD_BLKS[ji])
                units = []
                for p in range(4):
                    kv = p // 2
                    for r in range(2):
                        units.append(dict(kT=KTv[64 * r:64 * r + 64, kv, :], qT=QTv[64 * r:64 * r + 64, p, :],
                                          v=(lambda kt, kv=kv: VAv[:, kt, kv, :]), reads=[KT, QT], vreads=[VA],
                                          ob=64 * r, ych=p, ywr=[YT]))
                attention(units, kts, qcols, nq, GQA_SCALE, YTv)
                if l + 1 < depth and ji < 4:
                    mod_compute(l + 1, MOD_BLKS[ji], wbm[0], wbm[1])
            if l + 1 < depth:
                mod_finish(l + 1)
            dma("sp", y_d[0], YTv, [YT], [R_y[0]])
            for b in (QT, KT, VA):
                b.free()
            for b in PT + REC + S + [stat]:
                b.free()

            gts = compute_gate_rows(l, 0, not last)
            wmrg = [None, None]

            def load_merge_w(half):
                c0 = half * 512
                ws = []
                for i in range(3):
                    ws.append(load_w("wg%d" % i, w_in_d, l, 1952 + i * D + c0, 1952 + i * D + c0 + 512))
                for i in range(3):
                    ws.append(load_w("wbr%d" % i, w_br_d[i], l, c0, c0 + 512))
                b = A.alloc("wo", 4 * 1024 * 2, BF16)
                bv = v3(b.ap, 4)
                src = w_out_d[l].rearrange("(kc p) n -> p kc n", p=128)
                for kc in range(4):
                    dma("pool", bv[:, kc, :], src[:, half * 4 + kc, :], [], [b])
                ws.append((b, bv))
                return ws

            pieces = [(0, 6), (6, 6), (12, 5), (17, 5)]
            def load_ffn_wo(pi):
                h0, nh = pieces[pi]
                b = A.alloc("wfo", nh * 1024 * 2, BF16)
                bv = v3(b.ap, nh)
                src = w_fo_d[l].rearrange("(kc p) n -> p kc n", p=128)
                for kc in range(nh):
                    dma("pool", bv[:, kc, :], src[:, h0 + kc, :], [], [b])
                return (b, bv)

            def load_ffn_w(pi, with_wo=True):
                h0, nh = pieces[pi]
                wg = load_w("wfg", w_fi_d, l, h0 * 128, (h0 + nh) * 128)
                wu = load_w("wfu", w_fi_d, l, FFN_H + h0 * 128, FFN_H + (h0 + nh) * 128)
                return [wg, wu, load_ffn_wo(pi) if with_wo else None]

            wf = [None] * 4
            wC, wCv = load_w("wC", w_in_d, l, 1440, 1952)
            PF = A.alloc("PF", NT * 512 * 2, BF16)
            PFv = v3(PF.ap, NT)
            for it, t in enumerate(range(NT) if not last else range(16)):
                hb, hbv = load_ht(it, t)
                pb = it % 2
                for kc in range(8):
                    mm(psum[pb][:], hbv[:, kc, :], wCv[:, kc, :], kc == 0, kc == 7, [hb, wC], [PS[pb]])
                cp("act" if it % 2 == 0 else "dve", PFv[:, t, :], psum[pb][:], [PS[pb]], [PF])
            wC.free()
            wmrg[0] = load_merge_w(0)
            TB = [[A.alloc("TB%d%d" % (i, j), 16 * 256 * 2, BF16) for j in range(2)] for i in range(2)]
            PQ = [A.alloc("PQ%d" % i, 1024, BF16) for i in range(2)]

            def dft_chunk(tabs, ncs, tile0, ycol0, idx):
                tcv, tsv, tres = tabs
                for g in range(4):
                    u = idx * 4 + g
                    pa, pq_, py = (u % 2) * 3, (u % 2) * 3 + 1, (u % 2) * 3 + 2
                    for nc_ in range(ncs):
                        mm(psum[pa][:, 0:256], PFv[:, tile0 + nc_, g * 128:(g + 1) * 128], tcv[:, nc_, :], nc_ == 0,
                           nc_ == ncs - 1, [PF] + tres, [PS[pa]])
                    for nc_ in range(ncs):
                        mm(psum[pq_][:, 0:256], PFv[:, tile0 + nc_, g * 128:(g + 1) * 128], tsv[:, nc_, :], nc_ == 0,
                           nc_ == ncs - 1, [PF] + tres, [PS[pq_]])
                    pq = PQ[u % 2]
                    cp("act", pq.ap[:, 0:256], psum[pa][:, 0:256], [PS[pa]], [pq])
                    cp("dve", pq.ap[:, 256:512], psum[pq_][:, 0:256], [PS[pq_]], [pq])
                    mm(psum[py][:, 0:256], c128.ap, pq.ap[:, 0:256], True, False, [c128, pq], [PS[py]])
                    mm(psum[py][:, 0:256], ns128.ap, pq.ap[:, 256:512], False, True, [ns128, pq], [PS[py]])
                    cp("act" if g % 2 else "dve", YTv[:, g, ycol0:ycol0 + 256], psum[py][:, 0:256], [PS[py]], [YT])

            for kc2 in range(8):
                tb = TB[kc2 % 2]
                dma("sp", tb[0].ap, cn_d[kc2], [], [tb[0]])
                dma("sp", tb[1].ap, sn_d[kc2], [], [tb[1]])
                dft_chunk((v3(tb[0].ap, 16), v3(tb[1].ap, 16), [tb[0], tb[1]]), 16, 0, kc2 * 256, kc2)
            if not last:
                dft_chunk((v3(c256.ap, 2), v3(s256.ap, 2), [c256, s256]), 2, 16, 2048, 8)
            dma("sp", y_d[2], YTv, [YT], [R_y[2]])
            for b in (PF, TB[0][0], TB[0][1], TB[1][0], TB[1][1], PQ[0], PQ[1], YT):
                b.free()

            ybuf = [A.alloc("ybuf%d" % i, 3 * 4 * 128 * 2, BF16) for i in range(2)]
            sg = [A.alloc("sg%d" % i, 2048, F32) for i in range(2)]
            mac = A.alloc("mac", 2048, F32)
            mtok = A.alloc("mtok", 1024, BF16)
            mT = [A.alloc("mT%d" % i, 1024, BF16) for i in range(2)]
            bg = A.alloc("bg", 3 * 512 * 4, F32)
            for half in range(2):
                if half == 0:
                    wmrg[1] = load_merge_w(1)
                if half == 1:
                    wf[0] = load_ffn_w(0, with_wo=False)
                ws = wmrg[half]
                c0 = half * 512
                for i in range(3):
                    o_ = RV["b_gate"][0] + i * D + half * 512
                    dma("sp", bg.ap[:, i * 512:(i + 1) * 512], vrow_d[l, o_:o_ + 512].partition_broadcast(128), [], [bg])
                for it, t in enumerate(act_tiles):
                    j = 1 if t >= 16 else 0
                    hb, hbv = load_ht(it, t)
                    yb = ybuf[it % 2]
                    ybv = yb.ap.rearrange("p (i k c) -> p i k c", i=3, k=4)
                    for i in range(3):
                        dma("sp", ybv[:, i, :, :], y_d[i][:, :, t * 128:(t + 1) * 128], [R_y[i]], [yb])
                    for i in range(3):
                        pg, pbk = i % 2, 2 + (i % 2)
                        wg, wgv = ws[i]
                        wb_, wbv_ = ws[3 + i]
                        for kc in range(8):
                            mm(psum[pg][:], hbv[:, kc, :], wgv[:, kc, :], kc == 0, kc == 7, [hb, wg], [PS[pg]])
                        for kc in range(4):
                            mm(psum[pbk][:], ybv[:, i, kc, :], wbv_[:, kc, :], kc == 0, kc == 3, [yb, wb_], [PS[pbk]])
                        s_ = sg[i % 2]
                        tt("dve", s_.ap, psum[pg][:], bg.ap[:, i * 512:(i + 1) * 512], ALU.add, [PS[pg], bg], [s_])
                        act(s_.ap, s_.ap, AF.Sigmoid, [s_], [s_])
                        if i == 0:
                            tt("dve", mac.ap, s_.ap, psum[pbk][:], ALU.mult, [s_, PS[pbk]], [mac])
                        else:
                            tt("dve", s_.ap, s_.ap, psum[pbk][:], ALU.mult, [s_, PS[pbk]], [s_])
                            if i == 1:
                                tt("pool", mac.ap, mac.ap, s_.ap, ALU.add, [mac, s_], [mac])
                            else:
                                tt("dve", mtok.ap, mac.ap, s_.ap, ALU.add, [mac, s_], [mtok])
                    pv = v3(psb(4)[:, 0:512], 4)
                    for c in range(4):
                        tr(pv[:, c, :], mtok.ap[:, c * 128:(c + 1) * 128], identb.ap, [mtok, identb], [PS[4]])
                    m_ = mT[it % 2]
                    cp("act", v3(m_.ap, 4), pv, [PS[4]], [m_])
                    wo, wov = ws[6]
                    for nch in range(2):
                        pb = 5 + nch
                        for kc in range(4):
                            mm(psum[pb][:], v3(m_.ap, 4)[:, kc, :], wov[:, kc, nch * 512:(nch + 1) * 512], kc == 0, kc == 3,
                               [m_, wo], [PS[pb]])
                        tx = sg[nch]
                        tt("dve", tx.ap, psum[pb][:], gts[j].ap[:, nch * 512:(nch + 1) * 512],
                           ALU.mult, [PS[pb], gts[j]], [tx])
                        tt("pool", Xv[:, t, nch * 512:(nch + 1) * 512], Xv[:, t, nch * 512:(nch + 1) * 512], tx.ap, ALU.add,
                           [XR[t], tx], [XR[t]])
                for (b, _) in ws:
                    b.free()
            for b in gts + [bg] + ybuf + sg + [mac, mtok] + mT + hbuf:
                b.free()

            build_ht(l, 1, act_tiles)
            gts = compute_gate_rows(l, 1, not last)
            hb4 = [A.alloc("hb4_%d" % i, 8 * 512 * 2, BF16) for i in range(2)]
            aT = [A.alloc("aT%d" % i, 6 * 512 * 2, BF16) for i in range(2)]
            sl = [A.alloc("sl%d" % i, 2048, F32) for i in range(2)]
            tmpx = [A.alloc("tmpy%d" % i, 4096, F32) for i in range(2)]

            chunks = [list(range(c * 4, c * 4 + 4)) for c in range(4)]
            if not last:
                chunks.append([16, 17])
            cnt = 0
            for pi in range(4):
                if wf[pi][2] is None:
                    wf[pi][2] = load_ffn_wo(pi)
                if pi + 1 < 4:
                    wf[pi + 1] = load_ffn_w(pi + 1)
                (wg, wgv), (wu, wuv), (wo, wov) = wf[pi]
                h0, nh = pieces[pi]
                for ci, tiles in enumerate(chunks):
                    ntk = len(tiles) * 128
                    hb = hb4[cnt % 2]
                    a_ = aT[cnt % 2]
                    cnt += 1
                    hbv = hb.ap.rearrange("p (k t c) -> p k t c", k=8, t=4)
                    for ti, t in enumerate(tiles):
                        dma("sp", hbv[:, :, ti, :], ht_d[t].rearrange("p (k c) -> p k c", k=8), [R_ht[t]], [hb])
                    hbk = hb.ap.rearrange("p (k n) -> p k n", k=8)
                    av = v3(a_.ap, 6)
                    for hc in range(nh):
                        pg, pu = (hc % 2) * 2, (hc % 2) * 2 + 1
                        for kc in range(8):
                            mm(psum[pg][:, 0:ntk], wgv[:, kc, hc * 128:(hc + 1) * 128], hbk[:, kc, 0:ntk], kc == 0, kc == 7,
                               [wg, hb], [PS[pg]])
                        for kc in range(8):
                            mm(psum[pu][:, 0:ntk], wuv[:, kc, hc * 128:(hc + 1) * 128], hbk[:, kc, 0:ntk], kc == 0, kc == 7,
                               [wu, hb], [PS[pu]])
                        s_ = sl[hc % 2]
                        act(s_.ap[:, 0:ntk], psum[pg][:, 0:ntk], AF.Silu, [PS[pg]], [s_])
                        tt("dve", av[:, hc, 0:ntk], s_.ap[:, 0:ntk], psum[pu][:, 0:ntk], ALU.mult, [s_, PS[pu]], [a_])
                    for ti, t in enumerate(tiles):
                        j = 1 if t >= 16 else 0
                        tx = tmpx[ti % 2]
                        for nch in range(2):
                            pb = 4 + (ti % 2) * 2 + nch
                            for hc in range(nh):
                                mm(psum[pb][:], av[:, hc, ti * 128:(ti + 1) * 128], wov[:, hc, nch * 512:(nch + 1) * 512],
                                   hc == 0, hc == nh - 1, [a_, wo], [PS[pb]])
                            tt("dve", tx.ap[:, nch * 512:(nch + 1) * 512], psum[pb][:],
                               gts[j].ap[:, nch * 512:(nch + 1) * 512], ALU.mult, [PS[pb], gts[j]], [tx])
                        tt("pool", Xv[:, t, :], Xv[:, t, :], tx.ap, ALU.add, [XR[t], tx], [XR[t]])
                for (b, _) in (wf[pi][0], wf[pi][1], wf[pi][2]):
                    b.free()
            for b in gts + hb4 + aT + sl + tmpx:
                b.free()

        for i in range(16):
            dma("sp", out_d[i * 128:(i + 1) * 128, :], Xv[:, i, :], [XR[i]], [R_out])
        P.emit()
        _CACHE["peak"] = A.peak
        _CACHE["nops"] = len(P.ops)
    return nc


def _host_consts():
    if "consts" in _CACHE:
        return _CACHE["consts"]
    cosa, sina = _rope_tables(64)
    cosb, sinb = _rope_tables(32)
    cn, sn, c256, s256, c128, ns128 = _dft_tables()
    c = dict(identf=np.eye(128, dtype=np.float32),
             cosa=cosa.reshape(128, -1), sina=sina.reshape(128, -1),
             cosb=cosb.reshape(128, -1), sinb=sinb.reshape(128, -1),
             cn=cn.reshape(8, 128, -1), sn=sn.reshape(8, 128, -1),
             c256=c256.reshape(128, -1), s256=s256.reshape(128, -1), c128=c128, ns128=ns128)
    _CACHE["consts"] = c
    return c


def kernel(x, c, ctx, c_ctx, w_mod, b_mod, g_norm1, g_norm2, w_in, g_q_gqa, g_k_gqa, g_cq, g_ckv, w_uq, w_ukv,
           g_q_nope, g_k_nope, g_q_rope, g_k_rope, b_gate, w_br_a, w_br_b, w_br_c, w_out, w_ffn_in, w_ffn_out):
    f = lambda a: np.ascontiguousarray(np.asarray(a, dtype=np.float32))
    x, c, ctx, c_ctx = f(x), f(c), f(ctx), f(c_ctx)
    B = x.shape[0]
    if "nc" not in _CACHE:
        _CACHE["nc"] = build_nc()
    nc = _CACHE["nc"]
    consts = _host_consts()
    b_mod = f(b_mod)
    vrow = np.zeros((DEPTH, RV_W), np.float32)
    for name, arr in (("gqa_q", g_q_gqa), ("gqa_k", g_k_gqa), ("q_nope", g_q_nope), ("k_nope", g_k_nope),
                      ("q_rope", g_q_rope), ("k_rope", g_k_rope), ("b_gate", b_gate)):
        o, w = RV[name]
        vrow[:, o:o + w] = f(arr)
    vrow[:, RV["b_gt1"][0]:RV["b_gt1"][0] + D] = b_mod[:, 2 * D:3 * D]
    vrow[:, RV["b_gt2"][0]:RV["b_gt2"][0] + D] = b_mod[:, 5 * D:6 * D]
    vfm = np.zeros((128, DEPTH, FM_W), np.float32)

    def fmaj(v):
        L = v.shape[0]
        return f(v).reshape(L, -1, 128).transpose(2, 0, 1)

    for name, arr in (("g1", g_norm1), ("g2", g_norm2), ("bmod", b_mod), ("gcq", g_cq), ("gckv", g_ckv)):
        o, w = FM[name]
        vfm[:, :, o:o + w] = fmaj(arr)
    vfm = np.ascontiguousarray(vfm.reshape(128, DEPTH * FM_W))
    shared = dict(w_mod=f(w_mod), w_in=f(w_in), w_uq=f(w_uq), w_ukv=f(w_ukv), w_br_a=f(w_br_a), w_br_b=f(w_br_b),
                  w_br_c=f(w_br_c), w_out=f(w_out), w_ffn_in=f(w_ffn_in), w_ffn_out=f(w_ffn_out), vrow=vrow, vfm=vfm)
    shared.update(consts)
    in_maps = []
    for b in range(B):
        c2 = np.stack([c[b], c_ctx], axis=-1).reshape(8, 128, 2).transpose(1, 0, 2).reshape(128, 16)
        m = dict(shared)
        m.update(x=x[b], ctx=ctx[b], c2=np.ascontiguousarray(c2))
        in_maps.append(m)
    res = run_bass_kernel_spmd(nc, in_maps, core_ids=list(range(B)))
    return np.stack([np.asarray(r["out"], dtype=np.float32) for r in res.results], axis=0)
```

```python
import contextlib
import numpy as np
import ml_dtypes
import concourse.bass as bass
import concourse.mybir as mybir
from concourse.bass_utils import run_bass_kernel_spmd

F32 = mybir.dt.float32
BF16 = mybir.dt.bfloat16
AF = mybir.ActivationFunctionType
ALU = mybir.AluOpType
AX = mybir.AxisListType

D = 1024
NLAT = 2048
NCTX = 256
NT = 18
DEPTH = 2
EPS = 1e-6
FFN_H = 2816
IN_W = 5024
GQA_SCALE = 64 ** -0.5
MLA_SCALE = 96 ** -0.5
N_DMA_SEMS = 64
TRUST = {"pe", "act", "dve", "dma"}
ARENA_BYTES = 207 * 1024


class Res:
    __slots__ = ("name", "last_w", "readers", "excl")

    def __init__(self, name, excl=False):
        self.name = name
        self.last_w = None
        self.readers = []
        self.excl = excl


class Op:
    __slots__ = ("id", "eng", "fn", "deps", "dma", "pos", "waits", "signals", "sigval", "dsem", "dval")

    def __init__(self, id, eng, fn, deps, dma):
        self.id = id
        self.eng = eng
        self.fn = fn
        self.deps = deps
        self.dma = dma
        self.pos = -1
        self.waits = []
        self.signals = False
        self.sigval = 0
        self.dsem = -1
        self.dval = 0


class Prog:
    ENGS = ("pe", "act", "dve", "pool", "sp")

    def __init__(self, nc):
        self.nc = nc
        self.ops = []
        self.dma_count = 0
        self.dma_count_sw = 0
        self.last_pool = None
        self.dma_last_on_sem = {}

    def op(self, eng, fn, reads=(), writes=(), dma=False):
        deps = {}
        for w in writes:
            if w.last_w is not None:
                deps[w.last_w] = False
            for rr in w.readers:
                deps[rr] = False
        for r in reads:
            if r.excl:
                for rr in r.readers:
                    deps.setdefault(rr, False)
            if r.last_w is not None:
                deps[r.last_w] = True
        if eng == "pool" and not dma:
            if self.last_pool is not None:
                deps[self.last_pool] = True
        o = Op(len(self.ops), eng, fn, deps, dma)
        if eng == "pool" and not dma:
            self.last_pool = o.id
        if dma:
            half = N_DMA_SEMS // 2
            if eng == "pool":
                s = self.dma_count_sw % half
                self.dma_count_sw += 1
            else:
                s = half + self.dma_count % half
                self.dma_count += 1
            prev = self.dma_last_on_sem.get(s)
            o.dsem = s
            if prev is not None:
                o.deps.setdefault(prev.id, False)
                o.dval = prev.dval + 16
            else:
                o.dval = 16
            self.dma_last_on_sem[s] = o
        for r in reads:
            if eng == "pe":
                r.readers = [x for x in r.readers if self.ops[x].eng != "pe"]
            r.readers.append(o.id)
        for w in writes:
            w.last_w = o.id
            w.readers = []
        self.ops.append(o)
        return o

    def schedule(self):
        ops = self.ops
        per_eng = {e: [] for e in self.ENGS}
        for o in ops:
            o.pos = len(per_eng[o.eng])
            per_eng[o.eng].append(o)
        known = {e: {} for e in self.ENGS}
        snap = {}
        for o in ops:
            k = known[o.eng]
            for d in sorted(o.deps, reverse=True):
                dop = ops[d]
                if dop.dma:
                    key = ("d", dop.dsem)
                    need = dop.dval
                else:
                    if dop.eng == o.eng and (o.eng == "pe" or not o.deps[d]):
                        continue
                    key = dop.eng
                    need = dop.pos + 1
                if k.get(key, 0) >= need and ("dma" if dop.dma else dop.eng) in TRUST:
                    continue
                o.waits.append(d)
                dop.signals = True
                ks = snap.get(d)
                if ks is not None:
                    for kk, vv in ks.items():
                        if k.get(kk, 0) < vv:
                            k[kk] = vv
                if k.get(key, 0) < need:
                    k[key] = need
            snap[o.id] = dict(k)
        for e in self.ENGS:
            c = 0
            for o in per_eng[e]:
                if o.dma:
                    continue
                if o.signals:
                    c += 1
                    o.sigval = c
        self.per_eng = per_eng

    def emit(self):
        nc = self.nc
        self.schedule()
        ops = self.ops
        with contextlib.ExitStack() as st:
            tsem = {e: st.enter_context(nc.semaphore("ts_" + e)) for e in ("pe", "act", "dve", "pool")}
            dsems = [st.enter_context(nc.semaphore("dm_%d" % i)) for i in range(N_DMA_SEMS)]
            block = st.enter_context(nc.Block())

            def run(engname):
                def body(eng):
                    for o in self.per_eng[engname]:
                        for d in o.waits:
                            dop = ops[d]
                            if dop.dma:
                                eng.wait_ge(dsems[dop.dsem], dop.dval)
                            else:
                                eng.wait_ge(tsem[dop.eng], dop.sigval)
                        ins = o.fn(eng)
                        if o.dma:
                            ins.then_inc(dsems[o.dsem], 16)
                        elif o.signals:
                            ins.then_inc(tsem[o.eng], 1)
                    for s, lo in self.dma_last_on_sem.items():
                        if lo.eng == engname:
                            eng.wait_ge(dsems[s], lo.dval)
                return body

            block.tensor(run("pe"))
            block.scalar(run("act"))
            block.vector(run("dve"))
            block.gpsimd(run("pool"))
            block.sync(run("sp"))


class Arena:
    def __init__(self, tensor, nbytes):
        self.t = tensor
        self.size = nbytes
        self.live = []
        self.ghosts = []
        self.peak = 0

    def alloc(self, name, nbytes, dtype=BF16, nres=1):
        req = nbytes
        nbytes = (nbytes + 63) // 64 * 64
        off = 0
        for (o, s, _) in sorted(self.live, key=lambda b: b[0]):
            if off + nbytes <= o:
                break
            off = max(off, o + s)
        if off + nbytes > self.size:
            raise RuntimeError("arena full allocating %s (%d B); live=%s" % (
                name, nbytes, [(r[2][0].name, r[1]) for r in self.live]))
        self.peak = max(self.peak, off + nbytes)
        ids = []
        keep = []
        for (go, gs, gids) in self.ghosts:
            if go < off + nbytes and off < go + gs:
                ids.extend(gids)
                if go >= off and go + gs <= off + nbytes:
                    continue
            keep.append((go, gs, gids))
        self.ghosts = keep
        ids = sorted(set(ids))
        res = []
        for i in range(nres):
            r = Res("%s_%d" % (name, i))
            r.readers = list(ids)
            res.append(r)
        self.live.append((off, nbytes, res))
        ap = self.t[:, off // 2:(off + req) // 2]
        if dtype == F32:
            ap = ap.bitcast(F32)
        return Buf(ap, res, off, self)

    def free(self, buf):
        for i, (o, s, res) in enumerate(self.live):
            if o == buf.off and res is buf.res:
                ids = []
                for r in res:
                    ids.extend(r.readers)
                    if r.last_w is not None:
                        ids.append(r.last_w)
                self.ghosts.append((o, s, sorted(set(ids))))
                del self.live[i]
                return
        raise RuntimeError("free of unknown buffer")


class Buf:
    def __init__(self, ap, res, off, arena):
        self.ap = ap
        self.res = res
        self.off = off
        self.arena = arena

    @property
    def r(self):
        return self.res[0]

    def free(self):
        self.arena.free(self)


def _rope_tables(dim):
    half = dim // 2
    freqs = 10000.0 ** (-np.arange(0, half, 2, dtype=np.float32) / half)
    t = np.arange(NLAT)
    row = (t // 64).astype(np.float32)
    col = (t % 64).astype(np.float32)

    def ax(pos):
        a = pos[:, None] * freqs[None, :]
        return np.concatenate([a, a], axis=-1)

    ang = np.concatenate([ax(row), ax(col)], axis=-1).astype(np.float32)
    cos = np.cos(ang).astype(np.float32)
    sin = np.sin(ang).astype(np.float32)
    q = dim // 4
    sgn = np.ones((dim,), np.float32)
    for a in range(2):
        sgn[a * 2 * q: a * 2 * q + q] = -1.0
    sins = sin * sgn[None, :]
    cos_f = np.ones((NT * 128, dim), np.float32)
    sin_f = np.zeros((NT * 128, dim), np.float32)
    cos_f[:NLAT] = cos
    sin_f[:NLAT] = sins
    cos_t = cos_f.reshape(NT, 128, dim).transpose(1, 0, 2)
    sin_t = sin_f.reshape(NT, 128, dim).transpose(1, 0, 2)
    return np.ascontiguousarray(cos_t), np.ascontiguousarray(sin_t)


def _dft_tables():
    bf = ml_dtypes.bfloat16
    n = np.arange(NLAT, dtype=np.int64)
    ph = (np.outer(n, n) % NLAT).astype(np.float64) * (2.0 * np.pi / NLAT)
    cn = (np.cos(ph) / np.sqrt(NLAT)).astype(np.float32)
    sn = (np.sin(ph) / np.sqrt(NLAT)).astype(np.float32)

    def lay(m):
        return np.ascontiguousarray(m.reshape(16, 128, 8, 256).transpose(2, 1, 0, 3)).astype(bf)

    n2 = np.arange(NCTX, dtype=np.int64)
    ph2 = (np.outer(n2, n2) % NCTX).astype(np.float64) * (2.0 * np.pi / NCTX)
    c256 = (np.cos(ph2) / np.sqrt(NCTX)).astype(np.float32)
    s256 = (np.sin(ph2) / np.sqrt(NCTX)).astype(np.float32)

    def lay2(m):
        return np.ascontiguousarray(m.reshape(2, 128, 256).transpose(1, 0, 2)).astype(bf)

    n3 = np.arange(128, dtype=np.int64)
    ph3 = (np.outer(n3, n3) % 128).astype(np.float64) * (2.0 * np.pi / 128)
    c128 = (np.cos(ph3) / np.sqrt(128.0)).astype(np.float32).astype(bf)
    ns128 = (-np.sin(ph3) / np.sqrt(128.0)).astype(np.float32).astype(bf)
    return lay(cn), lay(sn), lay2(c256), lay2(s256), c128, ns128


RV = {}
_o = 0
for _n, _w in (("gqa_q", 64), ("gqa_k", 64), ("q_nope", 64), ("k_nope", 64), ("q_rope", 32), ("k_rope", 32),
               ("b_gate", 3072), ("b_gt1", 1024), ("b_gt2", 1024)):
    RV[_n] = (_o, _w)
    _o += _w
RV_W = _o
FM = {"g1": (0, 8), "g2": (8, 8), "bmod": (16, 48), "gcq": (64, 3), "gckv": (67, 2)}
FM_W = 69

_CACHE = {}


def build_nc(depth=DEPTH, dbg=()):
    nc = bass.Bass("TRN2", target_bir_lowering=False)
    din = lambda n, s, dt=F32: nc.dram_tensor(n, list(s), dt, kind="ExternalInput").ap()
    x_d = din("x", [NLAT, D])
    ctx_d = din("ctx", [NCTX, D])
    c2_d = din("c2", [128, 16])
    w_mod_d = din("w_mod", [DEPTH, D, 6 * D])
    w_in_d = din("w_in", [DEPTH, D, IN_W])
    w_uq_d = din("w_uq", [DEPTH, 384, 768])
    w_ukv_d = din("w_ukv", [DEPTH, 256, 1024])
    w_br_d = [din("w_br_" + s, [DEPTH, 512, D]) for s in "abc"]
    w_out_d = din("w_out", [DEPTH, D, D])
    w_fi_d = din("w_ffn_in", [DEPTH, D, 2 * FFN_H])
    w_fo_d = din("w_ffn_out", [DEPTH, FFN_H, D])
    vrow_d = din("vrow", [DEPTH, RV_W])
    vfm_d = din("vfm", [128, DEPTH * FM_W])
    identf_d = din("identf", [128, 128])
    cosa_d = din("cosa", [128, NT * 64])
    sina_d = din("sina", [128, NT * 64])
    cosb_d = din("cosb", [128, NT * 32])
    sinb_d = din("sinb", [128, NT * 32])
    cn_d = din("cn", [8, 128, 16 * 256], BF16)
    sn_d = din("sn", [8, 128, 16 * 256], BF16)
    c256_d = din("c256", [128, 512], BF16)
    s256_d = din("s256", [128, 512], BF16)
    c128_d = din("c128", [128, 128], BF16)
    ns128_d = din("ns128", [128, 128], BF16)
    out_d = nc.dram_tensor("out", [NLAT, D], F32, kind="ExternalOutput").ap()
    ht_d = nc.dram_tensor("ht_scr", [NT, 128, 1024], BF16).ap()
    y_d = [nc.dram_tensor("y_scr%d" % i, [128, 4, NT * 128], BF16).ap() for i in range(3)]
    dbg_d = {}

    with contextlib.ExitStack() as st:
        arena_t = st.enter_context(nc.sbuf_tensor("arena", [128, ARENA_BYTES // 2], BF16))
        psum = [st.enter_context(nc.psum_tensor("ps%d" % i, [128, 512], F32)) for i in range(8)]
        P = Prog(nc)
        A = Arena(arena_t, ARENA_BYTES)
        PS = [Res("ps%d" % i, excl=True) for i in range(8)]
        R_ht = [Res("ht%d" % i) for i in range(NT)]
        R_y = [Res("y%d" % i) for i in range(3)]
        R_out = Res("out")

        def psb(i):
            return psum[i][:].bitcast(BF16)

        def rl(xs):
            out = []
            for v in xs:
                if isinstance(v, Buf):
                    out.extend(v.res)
                elif isinstance(v, Res):
                    out.append(v)
                elif v is not None:
                    out.extend(v)
            return out

        def dma(eng, out, in_, reads, writes):
            P.op(eng, lambda e: e.dma_start(out=out, in_=in_), rl(reads), rl(writes), dma=True)

        def mm(out, lhsT, rhs, start, stop, reads, writes):
            P.op("pe", lambda e: e.matmul(out, lhsT, rhs, start=start, stop=stop), rl(reads), rl(writes))

        def tr(out, in_, ident, reads, writes):
            P.op("pe", lambda e: e.transpose(out, in_, ident), rl(reads), rl(writes))

        def act(out, in_, func, reads, writes, bias=0.0, scale=1.0, accum=None):
            if accum is None:
                P.op("act", lambda e: e.activation(out, in_, func, bias=bias, scale=scale), rl(reads), rl(writes))
            else:
                P.op("act", lambda e: e.activation(out, in_, func, bias=bias, scale=scale, accum_out=accum),
                     rl(reads), rl(writes))

        def tt(eng, out, in0, in1, op, reads, writes):
            P.op(eng, lambda e: e.tensor_tensor(out, in0, in1, op), rl(reads), rl(writes))

        def ts(eng, out, in0, s1, s2, op0, op1, reads, writes):
            if s2 is None:
                P.op(eng, lambda e: e.tensor_scalar(out, in0, s1, None, op0), rl(reads), rl(writes))
            else:
                P.op(eng, lambda e: e.tensor_scalar(out, in0, s1, s2, op0, op1), rl(reads), rl(writes))

        def cp(eng, out, in_, reads, writes):
            if eng == "act":
                P.op("act", lambda e: e.activation(out, in_, AF.Copy), rl(reads), rl(writes))
            else:
                P.op(eng, lambda e: e.tensor_copy(out, in_), rl(reads), rl(writes))

        def recip(out, in_, reads, writes):
            P.op("dve", lambda e: e.reciprocal(out, in_), rl(reads), rl(writes))

        def red(out, in_, reads, writes):
            P.op("dve", lambda e: e.tensor_reduce(out, in_, AX.X, ALU.add), rl(reads), rl(writes))

        def memset(eng, ap, val, writes):
            P.op(eng, lambda e: e.memset(ap, val), [], rl(writes))

        def v3(ap, a):
            return ap.rearrange("p (a b) -> p a b", a=a)

        def rsqrt_into(dst, ss, n, inv, tmp, reads, writes_tmp, writes_dst):
            act(tmp, ss, AF.Sqrt, reads, writes_tmp, bias=EPS_AP[0], scale=inv)
            recip(dst, tmp, writes_tmp, writes_dst)

        X = A.alloc("X", NT * D * 4, F32, nres=NT)
        Xv = v3(X.ap, NT)
        XR = X.res
        identf = A.alloc("identf", 512, F32)
        identb = A.alloc("identb", 256, BF16)
        c128 = A.alloc("c128", 256, BF16)
        ns128 = A.alloc("ns128", 256, BF16)
        c256 = A.alloc("c256", 1024, BF16)
        s256 = A.alloc("s256", 1024, BF16)
        vfm = A.alloc("vfm", DEPTH * FM_W * 4, F32)
        c2 = A.alloc("c2", 64, F32)
        sc2 = A.alloc("sc2", 32, BF16)
        epsb = A.alloc("eps", 64, F32)
        modT = A.alloc("modT", DEPTH * 48 * 2 * 4, F32)
        amod = A.alloc("amod", DEPTH * 2 * 8 * 2 * 4, F32)
        EPS_AP = [epsb.ap[:, 0:1]]

        for i in range(NLAT // 128):
            dma("sp", Xv[:, i, :], x_d[i * 128:(i + 1) * 128, :], [], [XR[i]])
        for i in range(2):
            dma("sp", Xv[:, 16 + i, :], ctx_d[i * 128:(i + 1) * 128, :], [], [XR[16 + i]])
        dma("sp", identf.ap, identf_d, [], [identf])
        dma("sp", c128.ap, c128_d, [], [c128])
        dma("sp", ns128.ap, ns128_d, [], [ns128])
        dma("sp", c256.ap, c256_d, [], [c256])
        dma("sp", s256.ap, s256_d, [], [s256])
        dma("sp", vfm.ap, vfm_d, [], [vfm])
        dma("sp", c2.ap, c2_d, [], [c2])
        memset("dve", epsb.ap, EPS, [epsb])
        cp("dve", identb.ap, identf.ap, [identf], [identb])
        act(sc2.ap, c2.ap, AF.Silu, [c2], [sc2])
        sc2v = v3(sc2.ap, 8)
        modTv = modT.ap.rearrange("p (l f j) -> p l f j", l=DEPTH, f=48)
        amodv = amod.ap.rearrange("p (l n k j) -> p l n k j", l=DEPTH, n=2, k=8)
        vfmv = v3(vfm.ap, DEPTH)

        def fm(l, name):
            o, w = FM[name]
            return vfmv[:, l, o:o + w]

        def wview(w_ap, l, c0, c1):
            return w_ap[l].rearrange("(kc p) n -> p kc n", p=128)[:, :, c0:c1]

        def load_w(name, w_ap, l, c0, c1, kcs=None):
            K = w_ap.shape[1]
            nk = K // 128
            n = c1 - c0
            b = A.alloc(name, nk * n * 2, BF16)
            bv = v3(b.ap, nk)
            src = wview(w_ap, l, c0, c1)
            for kc in range(nk):
                dma("pool", bv[:, kc, :], src[:, kc, :], [], [b])
            return b, bv

        def load_row(name, l, key, eng="sp"):
            o, w = RV[key]
            b = A.alloc(name, w * 4, F32)
            dma(eng, b.ap, vrow_d[l, o:o + w].partition_broadcast(128), [], [b])
            return b

        MOD_BLKS = (0, 1, 3, 4)

        def mod_load(l, blk):
            return load_w("wmod", w_mod_d, l, blk * D, (blk + 1) * D)

        def mod_compute(l, blk, wb, wbv):
            pv = psum[0][:, 0:16].rearrange("p (f j) -> p f j", f=8)
            for f in range(8):
                for kc in range(8):
                    mm(pv[:, f, :], wbv[:, kc, f * 128:(f + 1) * 128], sc2v[:, kc, :], kc == 0, kc == 7,
                       [wb, sc2], [PS[0]])
            bm = fm(l, "bmod")[:, blk * 8:(blk + 1) * 8]
            tt("dve", modTv[:, l, blk * 8:(blk + 1) * 8, :], pv, bm.unsqueeze(2).to_broadcast([128, 8, 2]), ALU.add,
               [PS[0], vfm], [modT])
            wb.free()

        def mod_finish(l):
            for n, (blk, g) in enumerate(((1, "g1"), (4, "g2"))):
                P.op("dve", lambda e, n=n, blk=blk, g=g: e.scalar_tensor_tensor(
                    out=amodv[:, l, n, :, :], in0=modTv[:, l, blk * 8:(blk + 1) * 8, :], scalar=1.0,
                    in1=fm(l, g).unsqueeze(2).to_broadcast([128, 8, 2]), op0=ALU.add, op1=ALU.mult),
                    rl([modT, vfm]), rl([amod]))

        def compute_mod_fm(l):
            for blk in MOD_BLKS:
                wb, wbv = mod_load(l, blk)
                mod_compute(l, blk, wb, wbv)
            mod_finish(l)

        def compute_gate_rows(l, which, need_ctx):
            blk = 2 if which == 0 else 5
            wb, wbv = load_w("wmodg", w_mod_d, l, blk * D, (blk + 1) * D)
            brow = load_row("bgt", l, "b_gt1" if which == 0 else "b_gt2")
            scb = A.alloc("scb", 8 * 2 * 128 * 2, BF16)
            scbv = scb.ap.rearrange("p (k j m) -> p k j m", k=8, j=2)
            cp("dve", scbv, sc2v.unsqueeze(3).to_broadcast([128, 8, 2, 128]), [sc2], [scb])
            outs = []
            for j in range(2 if need_ctx else 1):
                g = A.alloc("gt%d" % j, 4096, F32)
                for nch in range(2):
                    for kc in range(8):
                        mm(psum[nch][:], scbv[:, kc, j, :], wbv[:, kc, nch * 512:(nch + 1) * 512], kc == 0, kc == 7,
                           [scb, wb], [PS[nch]])
                    tt("dve", g.ap[:, nch * 512:(nch + 1) * 512], psum[nch][:], brow.ap[:, nch * 512:(nch + 1) * 512],
                       ALU.add, [PS[nch], brow], [g])
                outs.append(g)
            wb.free()
            brow.free()
            scb.free()
            return outs

        def build_ht(l, n, tiles):
            xn = [A.alloc("xn%d" % i, 4096, F32) for i in range(2)]
            junk = A.alloc("junk", 2048, BF16)
            hts = [A.alloc("hts%d" % i, 2048, BF16) for i in range(2)]
            st_ = A.alloc("nstat", 64, F32)
            for it, t in enumerate(tiles):
                j = 1 if t >= 16 else 0
                xb = xn[it % 2]
                hb = hts[it % 2]
                hbv = v3(hb.ap, 8)
                sv = st_.ap
                act(junk.ap, Xv[:, t, :], AF.Square, [XR[t]], [junk, st_], accum=sv[:, 0:1])
                act(sv[:, 1:2], sv[:, 0:1], AF.Sqrt, [st_], [st_], bias=EPS_AP[0], scale=1.0 / D)
                recip(sv[:, 2:3], sv[:, 1:2], [st_], [st_])
                ts("dve", xb.ap, Xv[:, t, :], sv[:, 2:3], None, ALU.mult, None, [XR[t], st_], [xb])
                for half in range(2):
                    pb = 6 + half
                    pv = v3(psum[pb][:], 4)
                    for c in range(4):
                        kc = half * 4 + c
                        tr(pv[:, c, :], xb.ap[:, kc * 128:(kc + 1) * 128], identf.ap, [xb, identf], [PS[pb]])
                    for c in range(4):
                        kc = half * 4 + c
                        a_ap = amodv[:, l, n, kc, j:j + 1]
                        b_ap = modTv[:, l, (0 if n == 0 else 24) + kc, j:j + 1]
                        if c % 2 == 0:
                            act(hbv[:, kc, :], pv[:, c, :], AF.Identity, [PS[pb], amod, modT], [hb], bias=b_ap, scale=a_ap)
                        else:
                            ts("dve", hbv[:, kc, :], pv[:, c, :], a_ap, b_ap, ALU.mult, ALU.add, [PS[pb], amod, modT], [hb])
                dma("sp", ht_d[t], hb.ap, [hb], [R_ht[t]])
            for b in xn + hts + [junk, st_]:
                b.free()

        def headnorm(src, nh, hd, stride, off, gain, cos, sin, out, reads_src, S, stat, wr_out):
            sv = v3(src[:, 0:nh * stride], nh)[:, :, off:off + hd]
            n = nh * hd
            s1 = v3(S[0].ap[:, 0:n], nh)
            s2 = v3(S[1].ap[:, 0:n], nh)
            s3 = v3(S[2].ap[:, 0:n], nh)
            stv = stat.ap
            act(s1, sv, AF.Square, reads_src, [S[0]])
            red(stv[:, 0:nh], s1, [S[0]], [stat])
            act(stv[:, 16:16 + nh], stv[:, 0:nh], AF.Sqrt, [stat], [stat], bias=EPS_AP[0], scale=1.0 / hd)
            recip(stv[:, 32:32 + nh], stv[:, 16:16 + nh], [stat], [stat])
            tt("dve", s2, sv, stv[:, 32:32 + nh].unsqueeze(2).to_broadcast([128, nh, hd]), ALU.mult,
               list(reads_src) + [stat], [S[1]])
            gb = gain.ap.unsqueeze(1).to_broadcast([128, nh, hd])
            if cos is None:
                tt("dve", out, s2, gb, ALU.mult, [S[1], gain], wr_out)
                return
            tt("dve", s1, s2, gb, ALU.mult, [S[1], gain], [S[0]])
            q = hd // 4
            cb = cos.unsqueeze(1).to_broadcast([128, nh, hd])
            tt("dve", s2, s1, cb, ALU.mult, [S[0], ROPE], [S[1]])
            x5 = s1.rearrange("p h (a b f) -> p h a b f", a=2, b=2)
            u5 = s3.rearrange("p h (a b f) -> p h a b f", a=2, b=2)
            sn5 = sin.unsqueeze(1).to_broadcast([128, nh, hd]).rearrange("p h (a b f) -> p h a b f", a=2, b=2)
            for part in range(2):
                for a in range(2):
                    tt("pool", u5[:, :, a, part, :], x5[:, :, a, 1 - part, :], sn5[:, :, a, part, :], ALU.mult,
                       [S[0], ROPE], [S[2]])
            tt("dve", out, s2, s3, ALU.add, [S[1], S[2]], wr_out)

        ROPE = Res("rope")

        def attention(units, kt_list, qcols, nq, scale, YT):
            n = len(kt_list)
            items = [(ui, ki) for ui in range(len(units)) for ki in range(n)]
            LOOK, NSB = 3, 5

            def score(i):
                ui, ki = items[i]
                u = units[ui]
                kt = kt_list[ki]
                sb = i % NSB
                mm(psum[sb][:, 0:nq], u["kT"][:, kt * 128:(kt + 1) * 128], u["qT"][:, qcols:qcols + nq], True, True,
                   u["reads"], [PS[sb]])

            for i in range(min(LOOK, len(items))):
                score(i)
            for i, (ui, ki) in enumerate(items):
                if i + LOOK < len(items):
                    score(i + LOOK)
                u = units[ui]
                kt = kt_list[ki]
                sb = i % NSB
                pt = PT[i % 4]
                ob_ = 5 + ui % 3
                act(pt.ap[:, 0:nq], psum[sb][:, 0:nq], AF.Exp, [PS[sb]], [pt], scale=scale)
                mm(psum[ob_][:, 0:nq], u["v"](kt), pt.ap[:, 0:nq], ki == 0, ki == n - 1, [pt] + u["vreads"], [PS[ob_]])
                if ki == n - 1:
                    ob = u["ob"]
                    rc = REC[ui % 2]
                    recip(rc.ap[0:64, 0:nq], psum[ob_][64:128, 0:nq], [PS[ob_]], [rc])
                    tt("dve", YT[ob:ob + 64, u["ych"], qcols:qcols + nq], psum[ob_][0:64, 0:nq], rc.ap[0:64, 0:nq],
                       ALU.mult, [PS[ob_], rc], u["ywr"])

        for l in range(depth):
            last = (l == depth - 1)
            act_tiles = list(range(16)) if last else list(range(NT))
            if l == 0:
                compute_mod_fm(0)
            build_ht(l, 0, list(range(NT)))
            S = [A.alloc("S%d" % i, 2048, F32) for i in range(3)]
            stat = A.alloc("stat", 256, F32)
            hbuf = [A.alloc("hbuf%d" % i, 2048, BF16) for i in range(2)]
            PT = [A.alloc("PT%d" % i, 1024, BF16) for i in range(4)]
            REC = [A.alloc("REC%d" % i, 2048, F32) for i in range(2)]

            def load_ht(it, t):
                hb = hbuf[it % 2]
                dma("sp", hb.ap, ht_d[t], [R_ht[t]], [hb])
                return hb, v3(hb.ap, 8)

            YT = A.alloc("YT", 4 * NT * 128 * 2, BF16)
            YTv = v3(YT.ap, 4)
            jobs = [(qc * 512, 512, list(range(NT))) for qc in range(4)]
            if not last:
                jobs.append((2048, 256, [16, 17]))

            wB, wBv = load_w("wB", w_in_d, l, 768, 1440)
            wuq, wuqv = load_w("wuq", w_uq_d, l, 0, 768)
            wukv, wukvv = load_w("wukv", w_ukv_d, l, 0, 1024)
            gqn = load_row("gqn", l, "q_nope")
            gkn = load_row("gkn", l, "k_nope")
            gqr = load_row("gqr", l, "q_rope")
            gkr = load_row("gkr", l, "k_rope")
            cosb = A.alloc("cosb", NT * 32 * 4, F32)
            sinb = A.alloc("sinb", NT * 32 * 4, F32)
            dma("sp", cosb.ap, cosb_d, [], [cosb, ROPE])
            dma("sp", sinb.ap, sinb_d, [], [sinb, ROPE])
            cosbv = v3(cosb.ap, NT)
            sinbv = v3(sinb.ap, NT)
            CQT = A.alloc("CQT", 3 * NT * 128 * 2, BF16)
            CKVT = A.alloc("CKVT", 2 * NT * 128 * 2, BF16)
            KR = A.alloc("KR", NT * 32 * 2, BF16)
            CQTv = v3(CQT.ap, 3)
            CKVTv = v3(CKVT.ap, 2)
            KRv = v3(KR.ap, NT)
            cn_ = A.alloc("cqn", 384 * 2 + 256 * 2, BF16)
            gcq = fm(l, "gcq")
            gckv = fm(l, "gckv")
            for it, t in enumerate(range(NT)):
                need_q = t in act_tiles
                hb, hbv = load_ht(it, t)
                if need_q:
                    for kc in range(8):
                        mm(psum[0][:, 0:384], hbv[:, kc, :], wBv[:, kc, 0:384], kc == 0, kc == 7, [hb, wB], [PS[0]])
                for kc in range(8):
                    mm(psum[1][:, 0:288], hbv[:, kc, :], wBv[:, kc, 384:672], kc == 0, kc == 7, [hb, wB], [PS[1]])
                sv = stat.ap
                jk = S[0].ap
                if need_q:
                    act(jk[:, 0:384], psum[0][:, 0:384], AF.Square, [PS[0]], [S[0], stat], accum=sv[:, 48:49])
                    act(sv[:, 51:52], sv[:, 48:49], AF.Sqrt, [stat], [stat], bias=EPS_AP[0], scale=1.0 / 384)
                act(jk[:, 0:256], psum[1][:, 0:256], AF.Square, [PS[1]], [S[0], stat], accum=sv[:, 49:50])
                act(sv[:, 52:53], sv[:, 49:50], AF.Sqrt, [stat], [stat], bias=EPS_AP[0], scale=1.0 / 256)
                if not need_q:
                    memset("dve", sv[:, 51:52], 1.0, [stat])
                recip(sv[:, 54:56], sv[:, 51:53], [stat], [stat])
                if need_q:
                    act(cn_.ap[:, 0:384], psum[0][:, 0:384], AF.Copy, [PS[0], stat], [cn_], scale=sv[:, 54:55])
                act(cn_.ap[:, 384:640], psum[1][:, 0:256], AF.Copy, [PS[1], stat], [cn_], scale=sv[:, 55:56])
                headnorm(psum[1][:, 256:288], 1, 32, 32, 0, gkr, cosbv[:, t, :], sinbv[:, t, :],
                         KRv[:, t, :].unsqueeze(1), [PS[1]], S, stat, [KR])
                pv = v3(psb(2)[:, 0:640], 5)
                for c in range(5):
                    if c < 3 and not need_q:
                        continue
                    tr(pv[:, c, :], cn_.ap[:, c * 128:(c + 1) * 128], identb.ap, [cn_, identb], [PS[2]])
                for c in range(5):
                    if c < 3 and not need_q:
                        continue
                    dst = CQTv[:, c, t * 128:(t + 1) * 128] if c < 3 else CKVTv[:, c - 3, t * 128:(t + 1) * 128]
                    gcol = gcq[:, c:c + 1] if c < 3 else gckv[:, c - 3:c - 2]
                    dbuf = CQT if c < 3 else CKVT
                    if c % 2 == 0:
                        act(dst, pv[:, c, :], AF.Copy, [PS[2], vfm], [dbuf], scale=gcol)
                    else:
                        ts("dve", dst, pv[:, c, :], gcol, None, ALU.mult, None, [PS[2], vfm], [dbuf])
            wB.free()
            cn_.free()
            gkr.free()
            HG = 2
            KBT = A.alloc("KBT", HG * NT * 128 * 2, BF16)
            QBT = A.alloc("QBT", HG * NT * 128 * 2, BF16)
            VB = A.alloc("VB", NT * HG * 128 * 2, BF16)
            KBTv = v3(KBT.ap, HG)
            QBTv = v3(QBT.ap, HG)
            VBv = VB.ap.rearrange("p (t h c) -> p t h c", t=NT, h=HG)
            ktk = A.alloc("ktk", HG * 96 * 2, BF16)
            qtk = A.alloc("qtk", HG * 96 * 2, BF16)
            ktkv = v3(ktk.ap, HG)
            qtkv = v3(qtk.ap, HG)
            memset("pool", VB.ap, 1.0, [VB])
            for hg in range(8 // HG):
                for it, t in enumerate(range(NT)):
                    need_q = t in act_tiles
                    for kc in range(2):
                        mm(psum[0][:, 0:HG * 128], CKVTv[:, kc, t * 128:(t + 1) * 128],
                           wukvv[:, kc, hg * HG * 128:(hg + 1) * HG * 128], kc == 0, kc == 1, [CKVT, wukv], [PS[0]])
                    headnorm(psum[0][:], HG, 64, 128, 0, gkn, None, None, ktkv[:, :, 0:64], [PS[0]], S, stat, [ktk])
                    cp("dve", ktkv[:, :, 64:96], KRv[:, t, :].unsqueeze(1).to_broadcast([128, HG, 32]), [KR], [ktk])
                    cp("act", VBv[:, t, :, 0:64], v3(psum[0][:, 0:HG * 128], HG)[:, :, 64:128], [PS[0]], [VB])
                    pv = v3(psb(2)[:, 0:HG * 128], HG)
                    for h in range(HG):
                        tr(pv[0:96, h, :], ktkv[:, h, :], identb.ap, [ktk, identb], [PS[2]])
                    cp("act", KBTv[0:96, :, t * 128:(t + 1) * 128], pv[0:96, :, :], [PS[2]], [KBT])
                    if need_q:
                        for kc in range(3):
                            mm(psum[1][:, 0:HG * 96], CQTv[:, kc, t * 128:(t + 1) * 128],
                               wuqv[:, kc, hg * HG * 96:(hg + 1) * HG * 96], kc == 0, kc == 2, [CQT, wuq], [PS[1]])
                        headnorm(psum[1][:], HG, 64, 96, 0, gqn, None, None, qtkv[:, :, 0:64], [PS[1]], S, stat, [qtk])
                        headnorm(psum[1][:], HG, 32, 96, 64, gqr, cosbv[:, t, :], sinbv[:, t, :], qtkv[:, :, 64:96],
                                 [PS[1]], S, stat, [qtk])
                        pv = v3(psb(3)[:, 0:HG * 128], HG)
                        for h in range(HG):
                            tr(pv[0:96, h, :], qtkv[:, h, :], identb.ap, [qtk, identb], [PS[3]])
                        cp("dve", QBTv[0:96, :, t * 128:(t + 1) * 128], pv[0:96, :, :], [PS[3]], [QBT])
                for (qcols, nq, kts) in jobs:
                    units = []
                    for h in range(HG):
                        hh = hg * HG + h
                        units.append(dict(kT=KBTv[0:96, h, :], qT=QBTv[0:96, h, :],
                                          v=(lambda kt, h=h: VBv[:, kt, h, :]), reads=[KBT, QBT], vreads=[VB],
                                          ob=64 * (hh % 2), ych=hh // 2, ywr=[YT]))
                    attention(units, kts, qcols, nq, MLA_SCALE, YTv)
            dma("sp", y_d[1], YTv, [YT], [R_y[1]])
            for b in (wuq, wukv, gqn, gkn, gqr, cosb, sinb, CQT, CKVT, KR, KBT, QBT, VB, ktk, qtk):
                b.free()

            wA, wAv = load_w("wA", w_in_d, l, 0, 768)
            gq = load_row("gq", l, "gqa_q")
            gk = load_row("gk", l, "gqa_k")
            cosa = A.alloc("cosa", NT * 64 * 4, F32)
            sina = A.alloc("sina", NT * 64 * 4, F32)
            dma("sp", cosa.ap, cosa_d, [], [cosa, ROPE])
            dma("sp", sina.ap, sina_d, [], [sina, ROPE])
            cosav = v3(cosa.ap, NT)
            sinav = v3(sina.ap, NT)
            QT = A.alloc("QaT", 4 * NT * 128 * 2, BF16)
            KT = A.alloc("KaT", 4 * NT * 128 * 2, BF16)
            VA = A.alloc("Va", NT * 2 * 128 * 2, BF16)
            QTv = v3(QT.ap, 4)
            KTv = KT.ap.rearrange("p (k r n) -> p k r n", k=2, r=2)
            VAv = VA.ap.rearrange("p (t k c) -> p t k c", t=NT, k=2)
            qtok = A.alloc("qtok", 1024, BF16)
            ktok = A.alloc("ktok", 512, BF16)
            memset("pool", VA.ap, 1.0, [VA])
            memset("pool", KT.ap, 0.0, [KT])
            for it, t in enumerate(range(NT)):
                need_q = t in act_tiles
                hb, hbv = load_ht(it, t)
                if need_q:
                    for kc in range(8):
                        mm(psum[0][:], hbv[:, kc, :], wAv[:, kc, 0:512], kc == 0, kc == 7, [hb, wA], [PS[0]])
                for kc in range(8):
                    mm(psum[1][:, 0:256], hbv[:, kc, :], wAv[:, kc, 512:768], kc == 0, kc == 7, [hb, wA], [PS[1]])
                if need_q:
                    headnorm(psum[0][:], 8, 64, 64, 0, gq, cosav[:, t, :], sinav[:, t, :], v3(qtok.ap, 8),
                             [PS[0]], S, stat, [qtok])
                    pv = v3(psb(2)[:, 0:512], 4)
                    for c in range(4):
                        tr(pv[:, c, :], qtok.ap[:, c * 128:(c + 1) * 128], identb.ap, [qtok, identb], [PS[2]])
                    cp("act", QTv[:, :, t * 128:(t + 1) * 128], pv, [PS[2]], [QT])
                k4 = ktok.ap.rearrange("p (k r d) -> p k r d", k=2, r=2)
                headnorm(psum[1][:, 0:128], 2, 64, 64, 0, gk, cosav[:, t, :], sinav[:, t, :], k4[:, :, 0, :],
                         [PS[1]], S, stat, [ktok])
                cp("dve", k4[:, :, 1, :], k4[:, :, 0, :], [ktok], [ktok])
                pv = v3(psb(3)[:, 0:256], 2)
                for c in range(2):
                    tr(pv[:, c, :], ktok.ap[:, c * 128:(c + 1) * 128], identb.ap, [ktok, identb], [PS[3]])
                cp("act", KTv[0:64, :, 0, t * 128:(t + 1) * 128], pv[0:64, :, :], [PS[3]], [KT])
                cp("dve", KTv[64:128, :, 1, t * 128:(t + 1) * 128], pv[64:128, :, :], [PS[3]], [KT])
                cp("act", VAv[:, t, :, 0:64], v3(psum[1][:, 128:256], 2), [PS[1]], [VA])
            wA.free()
            for b in (gq, gk, cosa, sina, qtok, ktok):
                b.free()
            for ji, (qcols, nq, kts) in enumerate(jobs):
                if l + 1 < depth and ji < 4:
                    wbm = mod_load(l + 1, MOD_BLKS[ji])
                units = []
                for p in range(4):
                    kv = p // 2
                    for r in range(2):
                        units.append(dict(kT=KTv[:, kv, r, :], qT=QTv[:, p, :],
                                          v=(lambda kt, kv=kv: VAv[:, kt, kv, :]), reads=[KT, QT], vreads=[VA],
                                          ob=64 * r, ych=p, ywr=[YT]))
                attention(units, kts, qcols, nq, GQA_SCALE, YTv)
                if l + 1 < depth and ji < 4:
                    mod_compute(l + 1, MOD_BLKS[ji], wbm[0], wbm[1])
            if l + 1 < depth:
                mod_finish(l + 1)
            dma("sp", y_d[0], YTv, [YT], [R_y[0]])
            for b in (QT, KT, VA):
                b.free()
            for b in PT + REC + S + [stat]:
                b.free()

            gts = compute_gate_rows(l, 0, not last)
            wmrg = [None, None]

            def load_merge_w(half):
                c0 = half * 512
                ws = []
                for i in range(3):
                    ws.append(load_w("wg%d" % i, w_in_d, l, 1952 + i * D + c0, 1952 + i * D + c0 + 512))
                for i in range(3):
                    ws.append(load_w("wbr%d" % i, w_br_d[i], l, c0, c0 + 512))
                b = A.alloc("wo", 4 * 1024 * 2, BF16)
                bv = v3(b.ap, 4)
                src = w_out_d[l].rearrange("(kc p) n -> p kc n", p=128)
                for kc in range(4):
                    dma("pool", bv[:, kc, :], src[:, half * 4 + kc, :], [], [b])
                ws.append((b, bv))
                return ws

            pieces = [(0, 6), (6, 6), (12, 5), (17, 5)]
            def load_ffn_wo(pi):
                h0, nh = pieces[pi]
                b = A.alloc("wfo", nh * 1024 * 2, BF16)
                bv = v3(b.ap, nh)
                src = w_fo_d[l].rearrange("(kc p) n -> p kc n", p=128)
                for kc in range(nh):
                    dma("pool", bv[:, kc, :], src[:, h0 + kc, :], [], [b])
                return (b, bv)

            def load_ffn_w(pi, with_wo=True):
                h0, nh = pieces[pi]
                wg = load_w("wfg", w_fi_d, l, h0 * 128, (h0 + nh) * 128)
                wu = load_w("wfu", w_fi_d, l, FFN_H + h0 * 128, FFN_H + (h0 + nh) * 128)
                return [wg, wu, load_ffn_wo(pi) if with_wo else None]

            wf = [None] * 4
            wC, wCv = load_w("wC", w_in_d, l, 1440, 1952)
            PF = A.alloc("PF", NT * 512 * 2, BF16)
            PFv = v3(PF.ap, NT)
            for it, t in enumerate(range(NT) if not last else range(16)):
                hb, hbv = load_ht(it, t)
                pb = it % 2
                for kc in range(8):
                    mm(psum[pb][:], hbv[:, kc, :], wCv[:, kc, :], kc == 0, kc == 7, [hb, wC], [PS[pb]])
                cp("act" if it % 2 == 0 else "dve", PFv[:, t, :], psum[pb][:], [PS[pb]], [PF])
            wC.free()
            wmrg[0] = load_merge_w(0)
            TB = [[A.alloc("TB%d%d" % (i, j), 16 * 256 * 2, BF16) for j in range(2)] for i in range(2)]
            PQ = [A.alloc("PQ%d" % i, 1024, BF16) for i in range(2)]

            def dft_chunk(tabs, ncs, tile0, ycol0, idx):
                tcv, tsv, tres = tabs
                for g in range(4):
                    u = idx * 4 + g
                    pa, pq_, py = (u % 2) * 3, (u % 2) * 3 + 1, (u % 2) * 3 + 2
                    for nc_ in range(ncs):
                        mm(psum[pa][:, 0:256], PFv[:, tile0 + nc_, g * 128:(g + 1) * 128], tcv[:, nc_, :], nc_ == 0,
                           nc_ == ncs - 1, [PF] + tres, [PS[pa]])
                    for nc_ in range(ncs):
                        mm(psum[pq_][:, 0:256], PFv[:, tile0 + nc_, g * 128:(g + 1) * 128], tsv[:, nc_, :], nc_ == 0,
                           nc_ == ncs - 1, [PF] + tres, [PS[pq_]])
                    pq = PQ[u % 2]
                    cp("act", pq.ap[:, 0:256], psum[pa][:, 0:256], [PS[pa]], [pq])
                    cp("dve", pq.ap[:, 256:512], psum[pq_][:, 0:256], [PS[pq_]], [pq])
                    mm(psum[py][:, 0:256], c128.ap, pq.ap[:, 0:256], True, False, [c128, pq], [PS[py]])
                    mm(psum[py][:, 0:256], ns128.ap, pq.ap[:, 256:512], False, True, [ns128, pq], [PS[py]])
                    cp("act" if g % 2 else "dve", YTv[:, g, ycol0:ycol0 + 256], psum[py][:, 0:256], [PS[py]], [YT])

            for kc2 in range(8):
                tb = TB[kc2 % 2]
                dma("sp", tb[0].ap, cn_d[kc2], [], [tb[0]])
                dma("sp", tb[1].ap, sn_d[kc2], [], [tb[1]])
                dft_chunk((v3(tb[0].ap, 16), v3(tb[1].ap, 16), [tb[0], tb[1]]), 16, 0, kc2 * 256, kc2)
            if not last:
                dft_chunk((v3(c256.ap, 2), v3(s256.ap, 2), [c256, s256]), 2, 16, 2048, 8)
            dma("sp", y_d[2], YTv, [YT], [R_y[2]])
            for b in (PF, TB[0][0], TB[0][1], TB[1][0], TB[1][1], PQ[0], PQ[1], YT):
                b.free()

            ybuf = [A.alloc("ybuf%d" % i, 3 * 4 * 128 * 2, BF16) for i in range(2)]
            sg = [A.alloc("sg%d" % i, 2048, F32) for i in range(2)]
            mac = A.alloc("mac", 2048, F32)
            mtok = A.alloc("mtok", 1024, BF16)
            mT = [A.alloc("mT%d" % i, 1024, BF16) for i in range(2)]
            bg = A.alloc("bg", 3 * 512 * 4, F32)
            for half in range(2):
                if half == 0:
                    wmrg[1] = load_merge_w(1)
                if half == 1:
                    wf[0] = load_ffn_w(0, with_wo=False)
                ws = wmrg[half]
                c0 = half * 512
                for i in range(3):
                    o_ = RV["b_gate"][0] + i * D + half * 512
                    dma("sp", bg.ap[:, i * 512:(i + 1) * 512], vrow_d[l, o_:o_ + 512].partition_broadcast(128), [], [bg])
                for it, t in enumerate(act_tiles):
                    j = 1 if t >= 16 else 0
                    hb, hbv = load_ht(it, t)
                    yb = ybuf[it % 2]
                    ybv = yb.ap.rearrange("p (i k c) -> p i k c", i=3, k=4)
                    for i in range(3):
                        dma("sp", ybv[:, i, :, :], y_d[i][:, :, t * 128:(t + 1) * 128], [R_y[i]], [yb])
                    for i in range(3):
                        pg, pbk = i % 2, 2 + (i % 2)
                        wg, wgv = ws[i]
                        wb_, wbv_ = ws[3 + i]
                        for kc in range(8):
                            mm(psum[pg][:], hbv[:, kc, :], wgv[:, kc, :], kc == 0, kc == 7, [hb, wg], [PS[pg]])
                        for kc in range(4):
                            mm(psum[pbk][:], ybv[:, i, kc, :], wbv_[:, kc, :], kc == 0, kc == 3, [yb, wb_], [PS[pbk]])
                        s_ = sg[i % 2]
                        tt("dve", s_.ap, psum[pg][:], bg.ap[:, i * 512:(i + 1) * 512], ALU.add, [PS[pg], bg], [s_])
                        act(s_.ap, s_.ap, AF.Sigmoid, [s_], [s_])
                        if i == 0:
                            tt("dve", mac.ap, s_.ap, psum[pbk][:], ALU.mult, [s_, PS[pbk]], [mac])
                        else:
                            tt("dve", s_.ap, s_.ap, psum[pbk][:], ALU.mult, [s_, PS[pbk]], [s_])
                            if i == 1:
                                tt("pool", mac.ap, mac.ap, s_.ap, ALU.add, [mac, s_], [mac])
                            else:
                                tt("dve", mtok.ap, mac.ap, s_.ap, ALU.add, [mac, s_], [mtok])
                    pv = v3(psb(4)[:, 0:512], 4)
                    for c in range(4):
                        tr(pv[:, c, :], mtok.ap[:, c * 128:(c + 1) * 128], identb.ap, [mtok, identb], [PS[4]])
                    m_ = mT[it % 2]
                    cp("act", v3(m_.ap, 4), pv, [PS[4]], [m_])
                    wo, wov = ws[6]
                    for nch in range(2):
                        pb = 5 + nch
                        for kc in range(4):
                            mm(psum[pb][:], v3(m_.ap, 4)[:, kc, :], wov[:, kc, nch * 512:(nch + 1) * 512], kc == 0, kc == 3,
                               [m_, wo], [PS[pb]])
                        tx = sg[nch]
                        tt("dve", tx.ap, psum[pb][:], gts[j].ap[:, nch * 512:(nch + 1) * 512],
                           ALU.mult, [PS[pb], gts[j]], [tx])
                        tt("pool", Xv[:, t, nch * 512:(nch + 1) * 512], Xv[:, t, nch * 512:(nch + 1) * 512], tx.ap, ALU.add,
                           [XR[t], tx], [XR[t]])
                for (b, _) in ws:
                    b.free()
            for b in gts + [bg] + ybuf + sg + [mac, mtok] + mT + hbuf:
                b.free()

            build_ht(l, 1, act_tiles)
            gts = compute_gate_rows(l, 1, not last)
            hb4 = [A.alloc("hb4_%d" % i, 8 * 512 * 2, BF16) for i in range(2)]
            aT = [A.alloc("aT%d" % i, 6 * 512 * 2, BF16) for i in range(2)]
            sl = [A.alloc("sl%d" % i, 2048, F32) for i in range(2)]
            tmpx = [A.alloc("tmpy%d" % i, 4096, F32) for i in range(2)]

            chunks = [list(range(c * 4, c * 4 + 4)) for c in range(4)]
            if not last:
                chunks.append([16, 17])
            cnt = 0
            for pi in range(4):
                if wf[pi][2] is None:
                    wf[pi][2] = load_ffn_wo(pi)
                if pi + 1 < 4:
                    wf[pi + 1] = load_ffn_w(pi + 1)
                (wg, wgv), (wu, wuv), (wo, wov) = wf[pi]
                h0, nh = pieces[pi]
                for ci, tiles in enumerate(chunks):
                    ntk = len(tiles) * 128
                    hb = hb4[cnt % 2]
                    a_ = aT[cnt % 2]
                    cnt += 1
                    hbv = hb.ap.rearrange("p (k t c) -> p k t c", k=8, t=4)
                    for ti, t in enumerate(tiles):
                        dma("sp", hbv[:, :, ti, :], ht_d[t].rearrange("p (k c) -> p k c", k=8), [R_ht[t]], [hb])
                    hbk = hb.ap.rearrange("p (k n) -> p k n", k=8)
                    av = v3(a_.ap, 6)
                    for hc in range(nh):
                        pg, pu = (hc % 2) * 2, (hc % 2) * 2 + 1
                        for kc in range(8):
                            mm(psum[pg][:, 0:ntk], wgv[:, kc, hc * 128:(hc + 1) * 128], hbk[:, kc, 0:ntk], kc == 0, kc == 7,
                               [wg, hb], [PS[pg]])
                        for kc in range(8):
                            mm(psum[pu][:, 0:ntk], wuv[:, kc, hc * 128:(hc + 1) * 128], hbk[:, kc, 0:ntk], kc == 0, kc == 7,
                               [wu, hb], [PS[pu]])
                        s_ = sl[hc % 2]
                        act(s_.ap[:, 0:ntk], psum[pg][:, 0:ntk], AF.Silu, [PS[pg]], [s_])
                        tt("dve", av[:, hc, 0:ntk], s_.ap[:, 0:ntk], psum[pu][:, 0:ntk], ALU.mult, [s_, PS[pu]], [a_])
                    for ti, t in enumerate(tiles):
                        j = 1 if t >= 16 else 0
                        tx = tmpx[ti % 2]
                        for nch in range(2):
                            pb = 4 + (ti % 2) * 2 + nch
                            for hc in range(nh):
                                mm(psum[pb][:], av[:, hc, ti * 128:(ti + 1) * 128], wov[:, hc, nch * 512:(nch + 1) * 512],
                                   hc == 0, hc == nh - 1, [a_, wo], [PS[pb]])
                            tt("dve", tx.ap[:, nch * 512:(nch + 1) * 512], psum[pb][:],
                               gts[j].ap[:, nch * 512:(nch + 1) * 512], ALU.mult, [PS[pb], gts[j]], [tx])
                        tt("pool", Xv[:, t, :], Xv[:, t, :], tx.ap, ALU.add, [XR[t], tx], [XR[t]])
                for (b, _) in (wf[pi][0], wf[pi][1], wf[pi][2]):
                    b.free()
            for b in gts + hb4 + aT + sl + tmpx:
                b.free()

        for i in range(16):
            dma("sp", out_d[i * 128:(i + 1) * 128, :], Xv[:, i, :], [XR[i]], [R_out])
        P.emit()
        _CACHE["peak"] = A.peak
        _CACHE["nops"] = len(P.ops)
    return nc


def _host_consts():
    if "consts" in _CACHE:
        return _CACHE["consts"]
    cosa, sina = _rope_tables(64)
    cosb, sinb = _rope_tables(32)
    cn, sn, c256, s256, c128, ns128 = _dft_tables()
    c = dict(identf=np.eye(128, dtype=np.float32),
             cosa=cosa.reshape(128, -1), sina=sina.reshape(128, -1),
             cosb=cosb.reshape(128, -1), sinb=sinb.reshape(128, -1),
             cn=cn.reshape(8, 128, -1), sn=sn.reshape(8, 128, -1),
             c256=c256.reshape(128, -1), s256=s256.reshape(128, -1), c128=c128, ns128=ns128)
    _CACHE["consts"] = c
    return c


def kernel(x, c, ctx, c_ctx, w_mod, b_mod, g_norm1, g_norm2, w_in, g_q_gqa, g_k_gqa, g_cq, g_ckv, w_uq, w_ukv,
           g_q_nope, g_k_nope, g_q_rope, g_k_rope, b_gate, w_br_a, w_br_b, w_br_c, w_out, w_ffn_in, w_ffn_out):
    f = lambda a: np.ascontiguousarray(np.asarray(a, dtype=np.float32))
    x, c, ctx, c_ctx = f(x), f(c), f(ctx), f(c_ctx)
    B = x.shape[0]
    if "nc" not in _CACHE:
        _CACHE["nc"] = build_nc()
    nc = _CACHE["nc"]
    consts = _host_consts()
    b_mod = f(b_mod)
    vrow = np.zeros((DEPTH, RV_W), np.float32)
    for name, arr in (("gqa_q", g_q_gqa), ("gqa_k", g_k_gqa), ("q_nope", g_q_nope), ("k_nope", g_k_nope),
                      ("q_rope", g_q_rope), ("k_rope", g_k_rope), ("b_gate", b_gate)):
        o, w = RV[name]
        vrow[:, o:o + w] = f(arr)
    vrow[:, RV["b_gt1"][0]:RV["b_gt1"][0] + D] = b_mod[:, 2 * D:3 * D]
    vrow[:, RV["b_gt2"][0]:RV["b_gt2"][0] + D] = b_mod[:, 5 * D:6 * D]
    vfm = np.zeros((128, DEPTH, FM_W), np.float32)

    def fmaj(v):
        L = v.shape[0]
        return f(v).reshape(L, -1, 128).transpose(2, 0, 1)

    for name, arr in (("g1", g_norm1), ("g2", g_norm2), ("bmod", b_mod), ("gcq", g_cq), ("gckv", g_ckv)):
        o, w = FM[name]
        vfm[:, :, o:o + w] = fmaj(arr)
    vfm = np.ascontiguousarray(vfm.reshape(128, DEPTH * FM_W))
    shared = dict(w_mod=f(w_mod), w_in=f(w_in), w_uq=f(w_uq), w_ukv=f(w_ukv), w_br_a=f(w_br_a), w_br_b=f(w_br_b),
                  w_br_c=f(w_br_c), w_out=f(w_out), w_ffn_in=f(w_ffn_in), w_ffn_out=f(w_ffn_out), vrow=vrow, vfm=vfm)
    shared.update(consts)
    in_maps = []
    for b in range(B):
        c2 = np.stack([c[b], c_ctx], axis=-1).reshape(8, 128, 2).transpose(1, 0, 2).reshape(128, 16)
        m = dict(shared)
        m.update(x=x[b], ctx=ctx[b], c2=np.ascontiguousarray(c2))
        in_maps.append(m)
    res = run_bass_kernel_spmd(nc, in_maps, core_ids=list(range(B)))
    return np.stack([np.asarray(r["out"], dtype=np.float32) for r in res.results], axis=0)
```

```python
import contextlib
import numpy as np
import ml_dtypes
import concourse.bass as bass
import concourse.mybir as mybir
from concourse.bass_utils import run_bass_kernel_spmd

F32 = mybir.dt.float32
BF16 = mybir.dt.bfloat16
AF = mybir.ActivationFunctionType
ALU = mybir.AluOpType
AX = mybir.AxisListType

D = 1024
NLAT = 2048
NCTX = 256
NT = 18
DEPTH = 2
EPS = 1e-6
FFN_H = 2816
IN_W = 5024
GQA_SCALE = 64 ** -0.5
MLA_SCALE = 96 ** -0.5
N_DMA_SEMS = 64
TRUST = {"pe", "act", "dve", "dma"}
ARENA_BYTES = 207 * 1024


class Res:
    __slots__ = ("name", "last_w", "readers", "excl")

    def __init__(self, name, excl=False):
        self.name = name
        self.last_w = None
        self.readers = []
        self.excl = excl


class Op:
    __slots__ = ("id", "eng", "fn", "deps", "dma", "pos", "waits", "signals", "sigval", "dsem", "dval")

    def __init__(self, id, eng, fn, deps, dma):
        self.id = id
        self.eng = eng
        self.fn = fn
        self.deps = deps
        self.dma = dma
        self.pos = -1
        self.waits = []
        self.signals = False
        self.sigval = 0
        self.dsem = -1
        self.dval = 0


class Prog:
    ENGS = ("pe", "act", "dve", "pool", "sp")

    def __init__(self, nc):
        self.nc = nc
        self.ops = []
        self.dma_count = 0
        self.dma_count_sw = 0
        self.last_pool = None
        self.dma_last_on_sem = {}

    def op(self, eng, fn, reads=(), writes=(), dma=False):
        deps = {}
        for w in writes:
            if w.last_w is not None:
                deps[w.last_w] = False
            for rr in w.readers:
                deps[rr] = False
        for r in reads:
            if r.excl:
                for rr in r.readers:
                    deps.setdefault(rr, False)
            if r.last_w is not None:
                deps[r.last_w] = True
        if eng == "pool" and not dma:
            if self.last_pool is not None:
                deps[self.last_pool] = True
        o = Op(len(self.ops), eng, fn, deps, dma)
        if eng == "pool" and not dma:
            self.last_pool = o.id
        if dma:
            half = N_DMA_SEMS // 2
            if eng == "pool":
                s = self.dma_count_sw % half
                self.dma_count_sw += 1
            else:
                s = half + self.dma_count % half
                self.dma_count += 1
            prev = self.dma_last_on_sem.get(s)
            o.dsem = s
            if prev is not None:
                o.deps.setdefault(prev.id, False)
                o.dval = prev.dval + 16
            else:
                o.dval = 16
            self.dma_last_on_sem[s] = o
        for r in reads:
            if eng == "pe":
                r.readers = [x for x in r.readers if self.ops[x].eng != "pe"]
            r.readers.append(o.id)
        for w in writes:
            w.last_w = o.id
            w.readers = []
        self.ops.append(o)
        return o

    def schedule(self):
        ops = self.ops
        per_eng = {e: [] for e in self.ENGS}
        for o in ops:
            o.pos = len(per_eng[o.eng])
            per_eng[o.eng].append(o)
        known = {e: {} for e in self.ENGS}
        snap = {}
        for o in ops:
            k = known[o.eng]
            for d in sorted(o.deps, reverse=True):
                dop = ops[d]
                if dop.dma:
                    key = ("d", dop.dsem)
                    need = dop.dval
                else:
                    if dop.eng == o.eng and (o.eng == "pe" or not o.deps[d]):
                        continue
                    key = dop.eng
                    need = dop.pos + 1
                if k.get(key, 0) >= need and ("dma" if dop.dma else dop.eng) in TRUST:
                    continue
                o.waits.append(d)
                dop.signals = True
                ks = snap.get(d)
                if ks is not None:
                    for kk, vv in ks.items():
                        if k.get(kk, 0) < vv:
                            k[kk] = vv
                if k.get(key, 0) < need:
                    k[key] = need
            snap[o.id] = dict(k)
        for e in self.ENGS:
            c = 0
            for o in per_eng[e]:
                if o.dma:
                    continue
                if o.signals:
                    c += 1
                    o.sigval = c
        self.per_eng = per_eng

    def emit(self):
        nc = self.nc
        self.schedule()
        ops = self.ops
        with contextlib.ExitStack() as st:
            tsem = {e: st.enter_context(nc.semaphore("ts_" + e)) for e in ("pe", "act", "dve", "pool")}
            dsems = [st.enter_context(nc.semaphore("dm_%d" % i)) for i in range(N_DMA_SEMS)]
            block = st.enter_context(nc.Block())

            def run(engname):
                def body(eng):
                    for o in self.per_eng[engname]:
                        for d in o.waits:
                            dop = ops[d]
                            if dop.dma:
                                eng.wait_ge(dsems[dop.dsem], dop.dval)
                            else:
                                eng.wait_ge(tsem[dop.eng], dop.sigval)
                        ins = o.fn(eng)
                        if o.dma:
                            ins.then_inc(dsems[o.dsem], 16)
                        elif o.signals:
                            ins.then_inc(tsem[o.eng], 1)
                    for s, lo in self.dma_last_on_sem.items():
                        if lo.eng == engname:
                            eng.wait_ge(dsems[s], lo.dval)
                return body

            block.tensor(run("pe"))
            block.scalar(run("act"))
            block.vector(run("dve"))
            block.gpsimd(run("pool"))
            block.sync(run("sp"))


class Arena:
    def __init__(self, tensor, nbytes):
        self.t = tensor
        self.size = nbytes
        self.live = []
        self.ghosts = []
        self.peak = 0

    def alloc(self, name, nbytes, dtype=BF16, nres=1):
        req = nbytes
        nbytes = (nbytes + 63) // 64 * 64
        off = 0
        for (o, s, _) in sorted(self.live, key=lambda b: b[0]):
            if off + nbytes <= o:
                break
            off = max(off, o + s)
        if off + nbytes > self.size:
            raise RuntimeError("arena full allocating %s (%d B); live=%s" % (
                name, nbytes, [(r[2][0].name, r[1]) for r in self.live]))
        self.peak = max(self.peak, off + nbytes)
        ids = []
        keep = []
        for (go, gs, gids) in self.ghosts:
            if go < off + nbytes and off < go + gs:
                ids.extend(gids)
                if go >= off and go + gs <= off + nbytes:
                    continue
            keep.append((go, gs, gids))
        self.ghosts = keep
        ids = sorted(set(ids))
        res = []
        for i in range(nres):
            r = Res("%s_%d" % (name, i))
            r.readers = list(ids)
            res.append(r)
        self.live.append((off, nbytes, res))
        ap = self.t[:, off // 2:(off + req) // 2]
        if dtype == F32:
            ap = ap.bitcast(F32)
        return Buf(ap, res, off, self)

    def free(self, buf):
        for i, (o, s, res) in enumerate(self.live):
            if o == buf.off and res is buf.res:
                ids = []
                for r in res:
                    ids.extend(r.readers)
                    if r.last_w is not None:
                        ids.append(r.last_w)
                self.ghosts.append((o, s, sorted(set(ids))))
                del self.live[i]
                return
        raise RuntimeError("free of unknown buffer")


class Buf:
    def __init__(self, ap, res, off, arena):
        self.ap = ap
        self.res = res
        self.off = off
        self.arena = arena

    @property
    def r(self):
        return self.res[0]

    def free(self):
        self.arena.free(self)


def interleave(factories, width=2):
    pending = list(factories)
    free_slots = list(range(width))
    active = []
    while pending or active:
        while len(active) < width and pending:
            sl = free_slots.pop(0)
            active.append((pending.pop(0)(sl), sl))
        for g, sl in list(active):
            try:
                next(g)
            except StopIteration:
                active.remove((g, sl))
                free_slots.append(sl)


def _rope_tables(dim):
    half = dim // 2
    freqs = 10000.0 ** (-np.arange(0, half, 2, dtype=np.float32) / half)
    t = np.arange(NLAT)
    row = (t // 64).astype(np.float32)
    col = (t % 64).astype(np.float32)

    def ax(pos):
        a = pos[:, None] * freqs[None, :]
        return np.concatenate([a, a], axis=-1)

    ang = np.concatenate([ax(row), ax(col)], axis=-1).astype(np.float32)
    cos = np.cos(ang).astype(np.float32)
    sin = np.sin(ang).astype(np.float32)
    q = dim // 4
    sgn = np.ones((dim,), np.float32)
    for a in range(2):
        sgn[a * 2 * q: a * 2 * q + q] = -1.0
    sins = sin * sgn[None, :]
    cos_f = np.ones((NT * 128, dim), np.float32)
    sin_f = np.zeros((NT * 128, dim), np.float32)
    cos_f[:NLAT] = cos
    sin_f[:NLAT] = sins
    cos_t = cos_f.reshape(NT, 128, dim).transpose(1, 0, 2)
    sin_t = sin_f.reshape(NT, 128, dim).transpose(1, 0, 2)
    return np.ascontiguousarray(cos_t), np.ascontiguousarray(sin_t)


def _dft_tables():
    bf = ml_dtypes.bfloat16
    n = np.arange(NLAT, dtype=np.int64)
    ph = (np.outer(n, n) % NLAT).astype(np.float64) * (2.0 * np.pi / NLAT)
    cn = (np.cos(ph) / np.sqrt(NLAT)).astype(np.float32)
    sn = (np.sin(ph) / np.sqrt(NLAT)).astype(np.float32)

    def lay(m):
        return np.ascontiguousarray(m.reshape(16, 128, 8, 256).transpose(2, 1, 0, 3)).astype(bf)

    n2 = np.arange(NCTX, dtype=np.int64)
    ph2 = (np.outer(n2, n2) % NCTX).astype(np.float64) * (2.0 * np.pi / NCTX)
    c256 = (np.cos(ph2) / np.sqrt(NCTX)).astype(np.float32)
    s256 = (np.sin(ph2) / np.sqrt(NCTX)).astype(np.float32)

    def lay2(m):
        return np.ascontiguousarray(m.reshape(2, 128, 256).transpose(1, 0, 2)).astype(bf)

    n3 = np.arange(128, dtype=np.int64)
    ph3 = (np.outer(n3, n3) % 128).astype(np.float64) * (2.0 * np.pi / 128)
    c128 = (np.cos(ph3) / np.sqrt(128.0)).astype(np.float32).astype(bf)
    ns128 = (-np.sin(ph3) / np.sqrt(128.0)).astype(np.float32).astype(bf)
    return lay(cn), lay(sn), lay2(c256), lay2(s256), c128, ns128


RV = {}
_o = 0
for _n, _w in (("gqa_q", 64), ("gqa_k", 64), ("q_nope", 64), ("k_nope", 64), ("q_rope", 32), ("k_rope", 32),
               ("b_gate", 3072), ("b_gt1", 1024), ("b_gt2", 1024)):
    RV[_n] = (_o, _w)
    _o += _w
RV_W = _o
FM = {"g1": (0, 8), "g2": (8, 8), "bmod": (16, 48), "gcq": (64, 3), "gckv": (67, 2)}
FM_W = 69

_CACHE = {}


def build_nc(depth=DEPTH, dbg=()):
    nc = bass.Bass("TRN2", target_bir_lowering=False)
    din = lambda n, s, dt=F32: nc.dram_tensor(n, list(s), dt, kind="ExternalInput").ap()
    x_d = din("x", [NLAT, D])
    ctx_d = din("ctx", [NCTX, D])
    c2_d = din("c2", [128, 16])
    w_mod_d = din("w_mod", [DEPTH, D, 6 * D])
    w_in_d = din("w_in", [DEPTH, D, IN_W])
    w_uq_d = din("w_uq", [DEPTH, 384, 768])
    w_ukv_d = din("w_ukv", [DEPTH, 256, 1024])
    w_br_d = [din("w_br_" + s, [DEPTH, 512, D]) for s in "abc"]
    w_out_d = din("w_out", [DEPTH, D, D])
    w_fi_d = din("w_ffn_in", [DEPTH, D, 2 * FFN_H])
    w_fo_d = din("w_ffn_out", [DEPTH, FFN_H, D])
    vrow_d = din("vrow", [DEPTH, RV_W])
    vfm_d = din("vfm", [128, DEPTH * FM_W])
    identf_d = din("identf", [128, 128])
    cosa_d = din("cosa", [128, NT * 64])
    sina_d = din("sina", [128, NT * 64])
    cosb_d = din("cosb", [128, NT * 32])
    sinb_d = din("sinb", [128, NT * 32])
    cn_d = din("cn", [8, 128, 16 * 256], BF16)
    sn_d = din("sn", [8, 128, 16 * 256], BF16)
    c256_d = din("c256", [128, 512], BF16)
    s256_d = din("s256", [128, 512], BF16)
    c128_d = din("c128", [128, 128], BF16)
    ns128_d = din("ns128", [128, 128], BF16)
    out_d = nc.dram_tensor("out", [NLAT, D], F32, kind="ExternalOutput").ap()
    ht_d = nc.dram_tensor("ht_scr", [NT, 128, 1024], BF16).ap()
    y_d = [nc.dram_tensor("y_scr%d" % i, [128, 4, NT * 128], BF16).ap() for i in range(3)]
    dbg_d = {}

    with contextlib.ExitStack() as st:
        arena_t = st.enter_context(nc.sbuf_tensor("arena", [128, ARENA_BYTES // 2], BF16))
        psum = [st.enter_context(nc.psum_tensor("ps%d" % i, [128, 512], F32)) for i in range(8)]
        P = Prog(nc)
        A = Arena(arena_t, ARENA_BYTES)
        PS = [Res("ps%d" % i, excl=True) for i in range(8)]
        R_ht = [Res("ht%d" % i) for i in range(NT)]
        R_y = [Res("y%d" % i) for i in range(3)]
        R_out = Res("out")

        def psb(i):
            return psum[i][:].bitcast(BF16)

        def rl(xs):
            out = []
            for v in xs:
                if isinstance(v, Buf):
                    out.extend(v.res)
                elif isinstance(v, Res):
                    out.append(v)
                elif v is not None:
                    out.extend(v)
            return out

        def dma(eng, out, in_, reads, writes):
            P.op(eng, lambda e: e.dma_start(out=out, in_=in_), rl(reads), rl(writes), dma=True)

        def mm(out, lhsT, rhs, start, stop, reads, writes):
            P.op("pe", lambda e: e.matmul(out, lhsT, rhs, start=start, stop=stop), rl(reads), rl(writes))

        def tr(out, in_, ident, reads, writes):
            P.op("pe", lambda e: e.transpose(out, in_, ident), rl(reads), rl(writes))

        def act(out, in_, func, reads, writes, bias=0.0, scale=1.0, accum=None):
            if accum is None:
                P.op("act", lambda e: e.activation(out, in_, func, bias=bias, scale=scale), rl(reads), rl(writes))
            else:
                P.op("act", lambda e: e.activation(out, in_, func, bias=bias, scale=scale, accum_out=accum),
                     rl(reads), rl(writes))

        def tt(eng, out, in0, in1, op, reads, writes):
            P.op(eng, lambda e: e.tensor_tensor(out, in0, in1, op), rl(reads), rl(writes))

        def ts(eng, out, in0, s1, s2, op0, op1, reads, writes):
            if s2 is None:
                P.op(eng, lambda e: e.tensor_scalar(out, in0, s1, None, op0), rl(reads), rl(writes))
            else:
                P.op(eng, lambda e: e.tensor_scalar(out, in0, s1, s2, op0, op1), rl(reads), rl(writes))

        def cp(eng, out, in_, reads, writes):
            if eng == "act":
                P.op("act", lambda e: e.activation(out, in_, AF.Copy), rl(reads), rl(writes))
            else:
                P.op(eng, lambda e: e.tensor_copy(out, in_), rl(reads), rl(writes))

        def recip(out, in_, reads, writes):
            P.op("dve", lambda e: e.reciprocal(out, in_), rl(reads), rl(writes))

        def red(out, in_, reads, writes):
            P.op("dve", lambda e: e.tensor_reduce(out, in_, AX.X, ALU.add), rl(reads), rl(writes))

        def memset(eng, ap, val, writes):
            P.op(eng, lambda e: e.memset(ap, val), [], rl(writes))

        def v3(ap, a):
            return ap.rearrange("p (a b) -> p a b", a=a)

        def rsqrt_into(dst, ss, n, inv, tmp, reads, writes_tmp, writes_dst):
            act(tmp, ss, AF.Sqrt, reads, writes_tmp, bias=EPS_AP[0], scale=inv)
            recip(dst, tmp, writes_tmp, writes_dst)

        X = A.alloc("X", NT * D * 4, F32, nres=NT)
        Xv = v3(X.ap, NT)
        XR = X.res
        identf = A.alloc("identf", 512, F32)
        identb = A.alloc("identb", 256, BF16)
        c128 = A.alloc("c128", 256, BF16)
        ns128 = A.alloc("ns128", 256, BF16)
        c256 = A.alloc("c256", 1024, BF16)
        s256 = A.alloc("s256", 1024, BF16)
        vfm = A.alloc("vfm", DEPTH * FM_W * 4, F32)
        c2 = A.alloc("c2", 64, F32)
        sc2 = A.alloc("sc2", 32, BF16)
        epsb = A.alloc("eps", 64, F32)
        modT = A.alloc("modT", DEPTH * 48 * 2 * 4, F32)
        amod = A.alloc("amod", DEPTH * 2 * 8 * 2 * 4, F32)
        EPS_AP = [epsb.ap[:, 0:1]]

        for i in range(NLAT // 128):
            dma("sp", Xv[:, i, :], x_d[i * 128:(i + 1) * 128, :], [], [XR[i]])
        for i in range(2):
            dma("sp", Xv[:, 16 + i, :], ctx_d[i * 128:(i + 1) * 128, :], [], [XR[16 + i]])
        dma("sp", identf.ap, identf_d, [], [identf])
        dma("sp", c128.ap, c128_d, [], [c128])
        dma("sp", ns128.ap, ns128_d, [], [ns128])
        dma("sp", c256.ap, c256_d, [], [c256])
        dma("sp", s256.ap, s256_d, [], [s256])
        dma("sp", vfm.ap, vfm_d, [], [vfm])
        dma("sp", c2.ap, c2_d, [], [c2])
        memset("dve", epsb.ap, EPS, [epsb])
        cp("dve", identb.ap, identf.ap, [identf], [identb])
        act(sc2.ap, c2.ap, AF.Silu, [c2], [sc2])
        sc2v = v3(sc2.ap, 8)
        modTv = modT.ap.rearrange("p (l f j) -> p l f j", l=DEPTH, f=48)
        amodv = amod.ap.rearrange("p (l n k j) -> p l n k j", l=DEPTH, n=2, k=8)
        vfmv = v3(vfm.ap, DEPTH)

        def fm(l, name):
            o, w = FM[name]
            return vfmv[:, l, o:o + w]

        def wview(w_ap, l, c0, c1):
            return w_ap[l].rearrange("(kc p) n -> p kc n", p=128)[:, :, c0:c1]

        def load_w(name, w_ap, l, c0, c1, kcs=None):
            K = w_ap.shape[1]
            nk = K // 128
            n = c1 - c0
            b = A.alloc(name, nk * n * 2, BF16)
            bv = v3(b.ap, nk)
            src = wview(w_ap, l, c0, c1)
            for kc in range(nk):
                dma("pool", bv[:, kc, :], src[:, kc, :], [], [b])
            return b, bv

        def load_row(name, l, key, eng="sp"):
            o, w = RV[key]
            b = A.alloc(name, w * 4, F32)
            dma(eng, b.ap, vrow_d[l, o:o + w].partition_broadcast(128), [], [b])
            return b

        MOD_BLKS = (0, 1, 3, 4)

        def mod_load(l, blk):
            return load_w("wmod", w_mod_d, l, blk * D, (blk + 1) * D)

        def mod_compute(l, blk, wb, wbv):
            pv = psum[0][:, 0:16].rearrange("p (f j) -> p f j", f=8)
            for f in range(8):
                for kc in range(8):
                    mm(pv[:, f, :], wbv[:, kc, f * 128:(f + 1) * 128], sc2v[:, kc, :], kc == 0, kc == 7,
                       [wb, sc2], [PS[0]])
            bm = fm(l, "bmod")[:, blk * 8:(blk + 1) * 8]
            tt("dve", modTv[:, l, blk * 8:(blk + 1) * 8, :], pv, bm.unsqueeze(2).to_broadcast([128, 8, 2]), ALU.add,
               [PS[0], vfm], [modT])
            wb.free()

        def mod_finish(l):
            for n, (blk, g) in enumerate(((1, "g1"), (4, "g2"))):
                P.op("dve", lambda e, n=n, blk=blk, g=g: e.scalar_tensor_tensor(
                    out=amodv[:, l, n, :, :], in0=modTv[:, l, blk * 8:(blk + 1) * 8, :], scalar=1.0,
                    in1=fm(l, g).unsqueeze(2).to_broadcast([128, 8, 2]), op0=ALU.add, op1=ALU.mult),
                    rl([modT, vfm]), rl([amod]))

        def compute_mod_fm(l):
            for blk in MOD_BLKS:
                wb, wbv = mod_load(l, blk)
                mod_compute(l, blk, wb, wbv)
            mod_finish(l)

        def compute_gate_rows(l, which, need_ctx):
            blk = 2 if which == 0 else 5
            wb, wbv = load_w("wmodg", w_mod_d, l, blk * D, (blk + 1) * D)
            brow = load_row("bgt", l, "b_gt1" if which == 0 else "b_gt2")
            scb = A.alloc("scb", 8 * 2 * 128 * 2, BF16)
            scbv = scb.ap.rearrange("p (k j m) -> p k j m", k=8, j=2)
            cp("dve", scbv, sc2v.unsqueeze(3).to_broadcast([128, 8, 2, 128]), [sc2], [scb])
            outs = []
            for j in range(2 if need_ctx else 1):
                g = A.alloc("gt%d" % j, 4096, F32)
                for nch in range(2):
                    for kc in range(8):
                        mm(psum[nch][:], scbv[:, kc, j, :], wbv[:, kc, nch * 512:(nch + 1) * 512], kc == 0, kc == 7,
                           [scb, wb], [PS[nch]])
                    tt("dve", g.ap[:, nch * 512:(nch + 1) * 512], psum[nch][:], brow.ap[:, nch * 512:(nch + 1) * 512],
                       ALU.add, [PS[nch], brow], [g])
                outs.append(g)
            wb.free()
            brow.free()
            scb.free()
            return outs

        def build_ht(l, n, tiles):
            xn = [A.alloc("xn%d" % i, 4096, F32) for i in range(2)]
            junk = [A.alloc("junk%d" % i, 2048, BF16) for i in range(2)]
            hts = [A.alloc("hts%d" % i, 2048, BF16) for i in range(2)]
            sts = [A.alloc("nstat%d" % i, 64, F32) for i in range(2)]

            def tile_gen(t):
                def gen(slot):
                    j = 1 if t >= 16 else 0
                    xb = xn[slot]
                    hb = hts[slot]
                    st_ = sts[slot]
                    hbv = v3(hb.ap, 8)
                    sv = st_.ap
                    act(junk[slot].ap, Xv[:, t, :], AF.Square, [XR[t]], [junk[slot], st_], accum=sv[:, 0:1])
                    act(sv[:, 1:2], sv[:, 0:1], AF.Sqrt, [st_], [st_], bias=EPS_AP[0], scale=1.0 / D)
                    yield
                    recip(sv[:, 2:3], sv[:, 1:2], [st_], [st_])
                    ts("dve", xb.ap, Xv[:, t, :], sv[:, 2:3], None, ALU.mult, None, [XR[t], st_], [xb])
                    yield
                    for half in range(2):
                        pb = 4 + slot * 2 + half
                        pv = v3(psum[pb][:], 4)
                        for c in range(4):
                            kc = half * 4 + c
                            tr(pv[:, c, :], xb.ap[:, kc * 128:(kc + 1) * 128], identf.ap, [xb, identf], [PS[pb]])
                    yield
                    for half in range(2):
                        pb = 4 + slot * 2 + half
                        pv = v3(psum[pb][:], 4)
                        for c in range(4):
                            kc = half * 4 + c
                            a_ap = amodv[:, l, n, kc, j:j + 1]
                            b_ap = modTv[:, l, (0 if n == 0 else 24) + kc, j:j + 1]
                            if c % 2 == 0:
                                act(hbv[:, kc, :], pv[:, c, :], AF.Identity, [PS[pb], amod, modT], [hb], bias=b_ap,
                                    scale=a_ap)
                            else:
                                ts("dve", hbv[:, kc, :], pv[:, c, :], a_ap, b_ap, ALU.mult, ALU.add,
                                   [PS[pb], amod, modT], [hb])
                    dma("sp", ht_d[t], hb.ap, [hb], [R_ht[t]])
                    yield
                return gen

            interleave([tile_gen(t) for t in tiles])
            for b in xn + hts + junk + sts:
                b.free()

        def headnorm(src, nh, hd, stride, off, gain, cos, sin, out, reads_src, S, stat, wr_out):
            sv = v3(src[:, 0:nh * stride], nh)[:, :, off:off + hd]
            n = nh * hd
            s1 = v3(S[0].ap[:, 0:n], nh)
            s2 = v3(S[1].ap[:, 0:n], nh)
            s3 = v3(S[2].ap[:, 0:n], nh)
            stv = stat.ap
            act(s1, sv, AF.Square, reads_src, [S[0]])
            yield
            red(stv[:, 0:nh], s1, [S[0]], [stat])
            yield
            act(stv[:, 16:16 + nh], stv[:, 0:nh], AF.Sqrt, [stat], [stat], bias=EPS_AP[0], scale=1.0 / hd)
            yield
            recip(stv[:, 32:32 + nh], stv[:, 16:16 + nh], [stat], [stat])
            tt("dve", s2, sv, stv[:, 32:32 + nh].unsqueeze(2).to_broadcast([128, nh, hd]), ALU.mult,
               list(reads_src) + [stat], [S[1]])
            gb = gain.ap.unsqueeze(1).to_broadcast([128, nh, hd])
            if cos is None:
                tt("dve", out, s2, gb, ALU.mult, [S[1], gain], wr_out)
                yield
                return
            tt("dve", s1, s2, gb, ALU.mult, [S[1], gain], [S[0]])
            yield
            q = hd // 4
            cb = cos.unsqueeze(1).to_broadcast([128, nh, hd])
            tt("dve", s2, s1, cb, ALU.mult, [S[0], ROPE], [S[1]])
            x5 = s1.rearrange("p h (a b f) -> p h a b f", a=2, b=2)
            u5 = s3.rearrange("p h (a b f) -> p h a b f", a=2, b=2)
            sn5 = sin.unsqueeze(1).to_broadcast([128, nh, hd]).rearrange("p h (a b f) -> p h a b f", a=2, b=2)
            for part in range(2):
                for a in range(2):
                    tt("pool", u5[:, :, a, part, :], x5[:, :, a, 1 - part, :], sn5[:, :, a, part, :], ALU.mult,
                       [S[0], ROPE], [S[2]])
            yield
            tt("dve", out, s2, s3, ALU.add, [S[1], S[2]], wr_out)
            yield

        ROPE = Res("rope")

        def attention(units, kt_list, qcols, nq, scale, YT):
            n = len(kt_list)
            items = [(ui, ki) for ui in range(len(units)) for ki in range(n)]
            LOOK, NSB = 3, 5

            def score(i):
                ui, ki = items[i]
                u = units[ui]
                kt = kt_list[ki]
                sb = i % NSB
                mm(psum[sb][:, 0:nq], u["kT"][:, kt * 128:(kt + 1) * 128], u["qT"][:, qcols:qcols + nq], True, True,
                   u["reads"], [PS[sb]])

            for i in range(min(LOOK, len(items))):
                score(i)
            for i, (ui, ki) in enumerate(items):
                if i + LOOK < len(items):
                    score(i + LOOK)
                u = units[ui]
                kt = kt_list[ki]
                sb = i % NSB
                pt = PT[i % 4]
                ob_ = 5 + ui % 3
                act(pt.ap[:, 0:nq], psum[sb][:, 0:nq], AF.Exp, [PS[sb]], [pt], scale=scale)
                mm(psum[ob_][:, 0:nq], u["v"](kt), pt.ap[:, 0:nq], ki == 0, ki == n - 1, [pt] + u["vreads"], [PS[ob_]])
                if ki == n - 1:
                    ob = u["ob"]
                    rc = REC[ui % 2]
                    recip(rc.ap[0:64, 0:nq], psum[ob_][64:128, 0:nq], [PS[ob_]], [rc])
                    tt("dve", YT[ob:ob + 64, u["ych"], qcols:qcols + nq], psum[ob_][0:64, 0:nq], rc.ap[0:64, 0:nq],
                       ALU.mult, [PS[ob_], rc], u["ywr"])

        for l in range(depth):
            last = (l == depth - 1)
            act_tiles = list(range(16)) if last else list(range(NT))
            if l == 0:
                compute_mod_fm(0)
            build_ht(l, 0, list(range(NT)))
            SS = [[A.alloc("S%d_%d" % (sl, i), 2048, F32) for i in range(3)] for sl in range(2)]
            stats = [A.alloc("stat%d" % sl, 256, F32) for sl in range(2)]
            hbuf = [A.alloc("hbuf%d" % i, 2048, BF16) for i in range(2)]
            PT = [A.alloc("PT%d" % i, 1024, BF16) for i in range(4)]
            REC = [A.alloc("REC%d" % i, 2048, F32) for i in range(2)]

            def load_ht(slot, t):
                hb = hbuf[slot]
                dma("sp", hb.ap, ht_d[t], [R_ht[t]], [hb])
                return hb, v3(hb.ap, 8)

            YT = A.alloc("YT", 4 * NT * 128 * 2, BF16)
            YTv = v3(YT.ap, 4)
            jobs = [(qc * 512, 512, list(range(NT))) for qc in range(4)]
            if not last:
                jobs.append((2048, 256, [16, 17]))

            wB, wBv = load_w("wB", w_in_d, l, 768, 1440)
            wuq, wuqv = load_w("wuq", w_uq_d, l, 0, 768)
            wukv, wukvv = load_w("wukv", w_ukv_d, l, 0, 1024)
            gqn = load_row("gqn", l, "q_nope")
            gkn = load_row("gkn", l, "k_nope")
            gqr = load_row("gqr", l, "q_rope")
            gkr = load_row("gkr", l, "k_rope")
            cosb = A.alloc("cosb", NT * 32 * 4, F32)
            sinb = A.alloc("sinb", NT * 32 * 4, F32)
            dma("sp", cosb.ap, cosb_d, [], [cosb, ROPE])
            dma("sp", sinb.ap, sinb_d, [], [sinb, ROPE])
            cosbv = v3(cosb.ap, NT)
            sinbv = v3(sinb.ap, NT)
            CQT = A.alloc("CQT", 3 * NT * 128 * 2, BF16)
            CKVT = A.alloc("CKVT", 2 * NT * 128 * 2, BF16)
            KR = A.alloc("KR", NT * 32 * 2, BF16)
            CQTv = v3(CQT.ap, 3)
            CKVTv = v3(CKVT.ap, 2)
            KRv = v3(KR.ap, NT)
            cns = [A.alloc("cqn%d" % i, 384 * 2 + 256 * 2, BF16) for i in range(2)]
            gcq = fm(l, "gcq")
            gckv = fm(l, "gckv")

            def b1_tile(t):
                def gen(slot):
                    need_q = t in act_tiles
                    S, stat, cn_ = SS[slot], stats[slot], cns[slot]
                    b0 = slot * 4
                    hb, hbv = load_ht(slot, t)
                    if need_q:
                        for kc in range(8):
                            mm(psum[b0][:, 0:384], hbv[:, kc, :], wBv[:, kc, 0:384], kc == 0, kc == 7, [hb, wB], [PS[b0]])
                    for kc in range(8):
                        mm(psum[b0 + 1][:, 0:288], hbv[:, kc, :], wBv[:, kc, 384:672], kc == 0, kc == 7, [hb, wB],
                           [PS[b0 + 1]])
                    yield
                    sv = stat.ap
                    jk = S[0].ap
                    if need_q:
                        act(jk[:, 0:384], psum[b0][:, 0:384], AF.Square, [PS[b0]], [S[0], stat], accum=sv[:, 48:49])
                        act(sv[:, 51:52], sv[:, 48:49], AF.Sqrt, [stat], [stat], bias=EPS_AP[0], scale=1.0 / 384)
                    act(jk[:, 0:256], psum[b0 + 1][:, 0:256], AF.Square, [PS[b0 + 1]], [S[0], stat], accum=sv[:, 49:50])
                    act(sv[:, 52:53], sv[:, 49:50], AF.Sqrt, [stat], [stat], bias=EPS_AP[0], scale=1.0 / 256)
                    if not need_q:
                        memset("dve", sv[:, 51:52], 1.0, [stat])
                    yield
                    recip(sv[:, 54:56], sv[:, 51:53], [stat], [stat])
                    yield
                    if need_q:
                        act(cn_.ap[:, 0:384], psum[b0][:, 0:384], AF.Copy, [PS[b0], stat], [cn_], scale=sv[:, 54:55])
                    act(cn_.ap[:, 384:640], psum[b0 + 1][:, 0:256], AF.Copy, [PS[b0 + 1], stat], [cn_], scale=sv[:, 55:56])
                    yield
                    pv = v3(psb(b0 + 2)[:, 0:640], 5)
                    for c in range(5):
                        if c < 3 and not need_q:
                            continue
                        tr(pv[:, c, :], cn_.ap[:, c * 128:(c + 1) * 128], identb.ap, [cn_, identb], [PS[b0 + 2]])
                    yield from headnorm(psum[b0 + 1][:, 256:288], 1, 32, 32, 0, gkr, cosbv[:, t, :], sinbv[:, t, :],
                                        KRv[:, t, :].unsqueeze(1), [PS[b0 + 1]], S, stat, [KR])
                    for c in range(5):
                        if c < 3 and not need_q:
                            continue
                        dst = CQTv[:, c, t * 128:(t + 1) * 128] if c < 3 else CKVTv[:, c - 3, t * 128:(t + 1) * 128]
                        gcol = gcq[:, c:c + 1] if c < 3 else gckv[:, c - 3:c - 2]
                        dbuf = CQT if c < 3 else CKVT
                        if c % 2 == 0:
                            act(dst, pv[:, c, :], AF.Copy, [PS[b0 + 2], vfm], [dbuf], scale=gcol)
                        else:
                            ts("dve", dst, pv[:, c, :], gcol, None, ALU.mult, None, [PS[b0 + 2], vfm], [dbuf])
                    yield
                return gen

            interleave([b1_tile(t) for t in range(NT)])
            wB.free()
            for b in cns:
                b.free()
            gkr.free()
            HG = 2
            KBT = A.alloc("KBT", HG * NT * 128 * 2, BF16)
            QBT = A.alloc("QBT", HG * NT * 128 * 2, BF16)
            VB = A.alloc("VB", NT * HG * 128 * 2, BF16)
            KBTv = v3(KBT.ap, HG)
            QBTv = v3(QBT.ap, HG)
            VBv = VB.ap.rearrange("p (t h c) -> p t h c", t=NT, h=HG)
            ktks = [A.alloc("ktk%d" % i, HG * 96 * 2, BF16) for i in range(2)]
            qtks = [A.alloc("qtk%d" % i, HG * 96 * 2, BF16) for i in range(2)]
            memset("pool", VB.ap, 1.0, [VB])
            for hg in range(8 // HG):
                def b2_tile(t, hg=hg):
                    def gen(slot):
                        need_q = t in act_tiles
                        S, stat = SS[slot], stats[slot]
                        ktkv = v3(ktks[slot].ap, HG)
                        qtkv = v3(qtks[slot].ap, HG)
                        ktk, qtk = ktks[slot], qtks[slot]
                        b0 = slot * 4
                        for kc in range(2):
                            mm(psum[b0][:, 0:HG * 128], CKVTv[:, kc, t * 128:(t + 1) * 128],
                               wukvv[:, kc, hg * HG * 128:(hg + 1) * HG * 128], kc == 0, kc == 1, [CKVT, wukv], [PS[b0]])
                        if need_q:
                            for kc in range(3):
                                mm(psum[b0 + 1][:, 0:HG * 96], CQTv[:, kc, t * 128:(t + 1) * 128],
                                   wuqv[:, kc, hg * HG * 96:(hg + 1) * HG * 96], kc == 0, kc == 2, [CQT, wuq], [PS[b0 + 1]])
                        yield
                        yield from headnorm(psum[b0][:], HG, 64, 128, 0, gkn, None, None, ktkv[:, :, 0:64], [PS[b0]],
                                            S, stat, [ktk])
                        cp("dve", ktkv[:, :, 64:96], KRv[:, t, :].unsqueeze(1).to_broadcast([128, HG, 32]), [KR], [ktk])
                        cp("act", VBv[:, t, :, 0:64], v3(psum[b0][:, 0:HG * 128], HG)[:, :, 64:128], [PS[b0]], [VB])
                        yield
                        pv = v3(psb(b0 + 2)[:, 0:HG * 128], HG)
                        for h in range(HG):
                            tr(pv[0:96, h, :], ktkv[:, h, :], identb.ap, [ktk, identb], [PS[b0 + 2]])
                        yield
                        cp("act", KBTv[0:96, :, t * 128:(t + 1) * 128], pv[0:96, :, :], [PS[b0 + 2]], [KBT])
                        if need_q:
                            yield from headnorm(psum[b0 + 1][:], HG, 64, 96, 0, gqn, None, None, qtkv[:, :, 0:64],
                                                [PS[b0 + 1]], S, stat, [qtk])
                            yield from headnorm(psum[b0 + 1][:], HG, 32, 96, 64, gqr, cosbv[:, t, :], sinbv[:, t, :],
                                                qtkv[:, :, 64:96], [PS[b0 + 1]], S, stat, [qtk])
                            pv2 = v3(psb(b0 + 3)[:, 0:HG * 128], HG)
                            for h in range(HG):
                                tr(pv2[0:96, h, :], qtkv[:, h, :], identb.ap, [qtk, identb], [PS[b0 + 3]])
                            yield
                            cp("dve", QBTv[0:96, :, t * 128:(t + 1) * 128], pv2[0:96, :, :], [PS[b0 + 3]], [QBT])
                        yield
                    return gen

                interleave([b2_tile(t) for t in range(NT)])
                for (qcols, nq, kts) in jobs:
                    units = []
                    for h in range(HG):
                        hh = hg * HG + h
                        units.append(dict(kT=KBTv[0:96, h, :], qT=QBTv[0:96, h, :],
                                          v=(lambda kt, h=h: VBv[:, kt, h, :]), reads=[KBT, QBT], vreads=[VB],
                                          ob=64 * (hh % 2), ych=hh // 2, ywr=[YT]))
                    attention(units, kts, qcols, nq, MLA_SCALE, YTv)
            dma("sp", y_d[1], YTv, [YT], [R_y[1]])
            for b in [wuq, wukv, gqn, gkn, gqr, cosb, sinb, CQT, CKVT, KR, KBT, QBT, VB] + ktks + qtks:
                b.free()

            wA, wAv = load_w("wA", w_in_d, l, 0, 768)
            gq = load_row("gq", l, "gqa_q")
            gk = load_row("gk", l, "gqa_k")
            cosa = A.alloc("cosa", NT * 64 * 4, F32)
            sina = A.alloc("sina", NT * 64 * 4, F32)
            dma("sp", cosa.ap, cosa_d, [], [cosa, ROPE])
            dma("sp", sina.ap, sina_d, [], [sina, ROPE])
            cosav = v3(cosa.ap, NT)
            sinav = v3(sina.ap, NT)
            QT = A.alloc("QaT", 4 * NT * 128 * 2, BF16)
            KT = A.alloc("KaT", 4 * NT * 128 * 2, BF16)
            VA = A.alloc("Va", NT * 2 * 128 * 2, BF16)
            QTv = v3(QT.ap, 4)
            KTv = KT.ap.rearrange("p (k r n) -> p k r n", k=2, r=2)
            VAv = VA.ap.rearrange("p (t k c) -> p t k c", t=NT, k=2)
            qtoks = [A.alloc("qtok%d" % i, 1024, BF16) for i in range(2)]
            ktoks = [A.alloc("ktok%d" % i, 512, BF16) for i in range(2)]
            memset("pool", VA.ap, 1.0, [VA])
            memset("pool", KT.ap, 0.0, [KT])

            def a_tile(t):
                def gen(slot):
                    need_q = t in act_tiles
                    S, stat, qtok, ktok = SS[slot], stats[slot], qtoks[slot], ktoks[slot]
                    b0 = slot * 4
                    hb, hbv = load_ht(slot, t)
                    if need_q:
                        for kc in range(8):
                            mm(psum[b0][:], hbv[:, kc, :], wAv[:, kc, 0:512], kc == 0, kc == 7, [hb, wA], [PS[b0]])
                    for kc in range(8):
                        mm(psum[b0 + 1][:, 0:256], hbv[:, kc, :], wAv[:, kc, 512:768], kc == 0, kc == 7, [hb, wA],
                           [PS[b0 + 1]])
                    yield
                    if need_q:
                        yield from headnorm(psum[b0][:], 8, 64, 64, 0, gq, cosav[:, t, :], sinav[:, t, :],
                                            v3(qtok.ap, 8), [PS[b0]], S, stat, [qtok])
                        pv = v3(psb(b0 + 2)[:, 0:512], 4)
                        for c in range(4):
                            tr(pv[:, c, :], qtok.ap[:, c * 128:(c + 1) * 128], identb.ap, [qtok, identb], [PS[b0 + 2]])
                        yield
                        cp("act", QTv[:, :, t * 128:(t + 1) * 128], pv, [PS[b0 + 2]], [QT])
                    k4 = ktok.ap.rearrange("p (k r d) -> p k r d", k=2, r=2)
                    yield from headnorm(psum[b0 + 1][:, 0:128], 2, 64, 64, 0, gk, cosav[:, t, :], sinav[:, t, :],
                                        k4[:, :, 0, :], [PS[b0 + 1]], S, stat, [ktok])
                    cp("dve", k4[:, :, 1, :], k4[:, :, 0, :], [ktok], [ktok])
                    yield
                    pv = v3(psb(b0 + 3)[:, 0:256], 2)
                    for c in range(2):
                        tr(pv[:, c, :], ktok.ap[:, c * 128:(c + 1) * 128], identb.ap, [ktok, identb], [PS[b0 + 3]])
                    yield
                    cp("act", KTv[0:64, :, 0, t * 128:(t + 1) * 128], pv[0:64, :, :], [PS[b0 + 3]], [KT])
                    cp("dve", KTv[64:128, :, 1, t * 128:(t + 1) * 128], pv[64:128, :, :], [PS[b0 + 3]], [KT])
                    cp("act", VAv[:, t, :, 0:64], v3(psum[b0 + 1][:, 128:256], 2), [PS[b0 + 1]], [VA])
                    yield
                return gen

            interleave([a_tile(t) for t in range(NT)])
            wA.free()
            for b in [gq, gk, cosa, sina] + qtoks + ktoks:
                b.free()
            for ji, (qcols, nq, kts) in enumerate(jobs):
                if l + 1 < depth and ji < 4:
                    wbm = mod_load(l + 1, MOD_BLKS[ji])
                units = []
                for p in range(4):
                    kv = p // 2
                    for r in range(2):
                        units.append(dict(kT=KTv[:, kv, r, :], qT=QTv[:, p, :],
                                          v=(lambda kt, kv=kv: VAv[:, kt, kv, :]), reads=[KT, QT], vreads=[VA],
                                          ob=64 * r, ych=p, ywr=[YT]))
                attention(units, kts, qcols, nq, GQA_SCALE, YTv)
                if l + 1 < depth and ji < 4:
                    mod_compute(l + 1, MOD_BLKS[ji], wbm[0], wbm[1])
            if l + 1 < depth:
                mod_finish(l + 1)
            dma("sp", y_d[0], YTv, [YT], [R_y[0]])
            for b in (QT, KT, VA):
                b.free()
            for b in PT + REC + SS[0] + SS[1] + stats:
                b.free()

            gts = compute_gate_rows(l, 0, not last)
            wmrg = [None, None]

            def load_merge_w(half):
                c0 = half * 512
                ws = []
                for i in range(3):
                    ws.append(load_w("wg%d" % i, w_in_d, l, 1952 + i * D + c0, 1952 + i * D + c0 + 512))
                for i in range(3):
                    ws.append(load_w("wbr%d" % i, w_br_d[i], l, c0, c0 + 512))
                b = A.alloc("wo", 4 * 1024 * 2, BF16)
                bv = v3(b.ap, 4)
                src = w_out_d[l].rearrange("(kc p) n -> p kc n", p=128)
                for kc in range(4):
                    dma("pool", bv[:, kc, :], src[:, half * 4 + kc, :], [], [b])
                ws.append((b, bv))
                return ws

            pieces = [(0, 6), (6, 6), (12, 5), (17, 5)]
            def load_ffn_wo(pi):
                h0, nh = pieces[pi]
                b = A.alloc("wfo", nh * 1024 * 2, BF16)
                bv = v3(b.ap, nh)
                src = w_fo_d[l].rearrange("(kc p) n -> p kc n", p=128)
                for kc in range(nh):
                    dma("pool", bv[:, kc, :], src[:, h0 + kc, :], [], [b])
                return (b, bv)

            def load_ffn_w(pi, with_wo=True):
                h0, nh = pieces[pi]
                wg = load_w("wfg", w_fi_d, l, h0 * 128, (h0 + nh) * 128)
                wu = load_w("wfu", w_fi_d, l, FFN_H + h0 * 128, FFN_H + (h0 + nh) * 128)
                return [wg, wu, load_ffn_wo(pi) if with_wo else None]

            wf = [None] * 4
            wC, wCv = load_w("wC", w_in_d, l, 1440, 1952)
            PF = A.alloc("PF", NT * 512 * 2, BF16)
            PFv = v3(PF.ap, NT)
            for it, t in enumerate(range(NT) if not last else range(16)):
                hb, hbv = load_ht(it % 2, t)
                pb = it % 2
                for kc in range(8):
                    mm(psum[pb][:], hbv[:, kc, :], wCv[:, kc, :], kc == 0, kc == 7, [hb, wC], [PS[pb]])
                cp("act" if it % 2 == 0 else "dve", PFv[:, t, :], psum[pb][:], [PS[pb]], [PF])
            wC.free()
            wmrg[0] = load_merge_w(0)
            TB = [[A.alloc("TB%d%d" % (i, j), 16 * 256 * 2, BF16) for j in range(2)] for i in range(2)]
            PQ = [A.alloc("PQ%d" % i, 1024, BF16) for i in range(2)]

            def dft_chunk(tabs, ncs, tile0, ycol0, idx):
                tcv, tsv, tres = tabs
                for g in range(4):
                    u = idx * 4 + g
                    pa, pq_, py = (u % 2) * 3, (u % 2) * 3 + 1, (u % 2) * 3 + 2
                    for nc_ in range(ncs):
                        mm(psum[pa][:, 0:256], PFv[:, tile0 + nc_, g * 128:(g + 1) * 128], tcv[:, nc_, :], nc_ == 0,
                           nc_ == ncs - 1, [PF] + tres, [PS[pa]])
                    for nc_ in range(ncs):
                        mm(psum[pq_][:, 0:256], PFv[:, tile0 + nc_, g * 128:(g + 1) * 128], tsv[:, nc_, :], nc_ == 0,
                           nc_ == ncs - 1, [PF] + tres, [PS[pq_]])
                    pq = PQ[u % 2]
                    cp("act", pq.ap[:, 0:256], psum[pa][:, 0:256], [PS[pa]], [pq])
                    cp("dve", pq.ap[:, 256:512], psum[pq_][:, 0:256], [PS[pq_]], [pq])
                    mm(psum[py][:, 0:256], c128.ap, pq.ap[:, 0:256], True, False, [c128, pq], [PS[py]])
                    mm(psum[py][:, 0:256], ns128.ap, pq.ap[:, 256:512], False, True, [ns128, pq], [PS[py]])
                    cp("act" if g % 2 else "dve", YTv[:, g, ycol0:ycol0 + 256], psum[py][:, 0:256], [PS[py]], [YT])

            for kc2 in range(8):
                tb = TB[kc2 % 2]
                dma("sp", tb[0].ap, cn_d[kc2], [], [tb[0]])
                dma("sp", tb[1].ap, sn_d[kc2], [], [tb[1]])
                dft_chunk((v3(tb[0].ap, 16), v3(tb[1].ap, 16), [tb[0], tb[1]]), 16, 0, kc2 * 256, kc2)
            if not last:
                dft_chunk((v3(c256.ap, 2), v3(s256.ap, 2), [c256, s256]), 2, 16, 2048, 8)
            dma("sp", y_d[2], YTv, [YT], [R_y[2]])
            for b in (PF, TB[0][0], TB[0][1], TB[1][0], TB[1][1], PQ[0], PQ[1], YT):
                b.free()

            ybuf = [A.alloc("ybuf%d" % i, 3 * 4 * 128 * 2, BF16) for i in range(2)]
            sgs = [A.alloc("sg%d" % i, 2048, F32) for i in range(2)]
            macs = [A.alloc("mac%d" % i, 2048, F32) for i in range(2)]
            mtoks = [A.alloc("mtok%d" % i, 1024, BF16) for i in range(2)]
            mT = [A.alloc("mT%d" % i, 1024, BF16) for i in range(2)]
            bg = A.alloc("bg", 3 * 512 * 4, F32)
            for half in range(2):
                if half == 0:
                    wmrg[1] = load_merge_w(1)
                if half == 1:
                    wf[0] = load_ffn_w(0, with_wo=False)
                ws = wmrg[half]
                c0 = half * 512
                for i in range(3):
                    o_ = RV["b_gate"][0] + i * D + half * 512
                    dma("sp", bg.ap[:, i * 512:(i + 1) * 512], vrow_d[l, o_:o_ + 512].partition_broadcast(128), [], [bg])

                def m_tile(t, ws=ws):
                    def gen(slot):
                        j = 1 if t >= 16 else 0
                        b0 = slot * 4
                        s_, mac, mtok, m_ = sgs[slot], macs[slot], mtoks[slot], mT[slot]
                        hb, hbv = load_ht(slot, t)
                        yb = ybuf[slot]
                        ybv = yb.ap.rearrange("p (i k c) -> p i k c", i=3, k=4)
                        for i in range(3):
                            dma("sp", ybv[:, i, :, :], y_d[i][:, :, t * 128:(t + 1) * 128], [R_y[i]], [yb])
                        for i in range(3):
                            wg, wgv = ws[i]
                            wb_, wbv_ = ws[3 + i]
                            for kc in range(8):
                                mm(psum[b0][:], hbv[:, kc, :], wgv[:, kc, :], kc == 0, kc == 7, [hb, wg], [PS[b0]])
                            for kc in range(4):
                                mm(psum[b0 + 1][:], ybv[:, i, kc, :], wbv_[:, kc, :], kc == 0, kc == 3, [yb, wb_],
                                   [PS[b0 + 1]])
                            yield
                            tt("dve", s_.ap, psum[b0][:], bg.ap[:, i * 512:(i + 1) * 512], ALU.add, [PS[b0], bg], [s_])
                            yield
                            act(s_.ap, s_.ap, AF.Sigmoid, [s_], [s_])
                            yield
                            if i == 0:
                                tt("dve", mac.ap, s_.ap, psum[b0 + 1][:], ALU.mult, [s_, PS[b0 + 1]], [mac])
                            else:
                                tt("dve", s_.ap, s_.ap, psum[b0 + 1][:], ALU.mult, [s_, PS[b0 + 1]], [s_])
                                if i == 1:
                                    tt("dve", mac.ap, mac.ap, s_.ap, ALU.add, [mac, s_], [mac])
                                else:
                                    tt("dve", mtok.ap, mac.ap, s_.ap, ALU.add, [mac, s_], [mtok])
                            yield
                        pv = v3(psb(b0 + 2)[:, 0:512], 4)
                        for c in range(4):
                            tr(pv[:, c, :], mtok.ap[:, c * 128:(c + 1) * 128], identb.ap, [mtok, identb], [PS[b0 + 2]])
                        yield
                        cp("act", v3(m_.ap, 4), pv, [PS[b0 + 2]], [m_])
                        yield
                        wo, wov = ws[6]
                        for nch in range(2):
                            pb = b0 + 3
                            for kc in range(4):
                                mm(psum[pb][:], v3(m_.ap, 4)[:, kc, :], wov[:, kc, nch * 512:(nch + 1) * 512], kc == 0,
                                   kc == 3, [m_, wo], [PS[pb]])
                            yield
                            tx = s_ if nch == 0 else mac
                            tt("dve", tx.ap, psum[pb][:], gts[j].ap[:, nch * 512:(nch + 1) * 512],
                               ALU.mult, [PS[pb], gts[j]], [tx])
                            tt("pool", Xv[:, t, nch * 512:(nch + 1) * 512], Xv[:, t, nch * 512:(nch + 1) * 512], tx.ap,
                               ALU.add, [XR[t], tx], [XR[t]])
                            yield
                    return gen

                interleave([m_tile(t) for t in act_tiles])
                for (b, _) in ws:
                    b.free()
            for b in gts + [bg] + ybuf + sgs + macs + mtoks + mT + hbuf:
                b.free()

            build_ht(l, 1, act_tiles)
            gts = compute_gate_rows(l, 1, not last)
            hb4 = [A.alloc("hb4_%d" % i, 8 * 512 * 2, BF16) for i in range(2)]
            aT = [A.alloc("aT%d" % i, 6 * 512 * 2, BF16) for i in range(2)]
            sl = [A.alloc("sl%d" % i, 2048, F32) for i in range(2)]
            tmpx = [A.alloc("tmpy%d" % i, 4096, F32) for i in range(2)]

            chunks = [list(range(c * 4, c * 4 + 4)) for c in range(4)]
            if not last:
                chunks.append([16, 17])
            cnt = 0
            for pi in range(4):
                if wf[pi][2] is None:
                    wf[pi][2] = load_ffn_wo(pi)
                if pi + 1 < 4:
                    wf[pi + 1] = load_ffn_w(pi + 1)
                (wg, wgv), (wu, wuv), (wo, wov) = wf[pi]
                h0, nh = pieces[pi]
                for ci, tiles in enumerate(chunks):
                    ntk = len(tiles) * 128
                    hb = hb4[cnt % 2]
                    a_ = aT[cnt % 2]
                    cnt += 1
                    hbv = hb.ap.rearrange("p (k t c) -> p k t c", k=8, t=4)
                    for ti, t in enumerate(tiles):
                        dma("sp", hbv[:, :, ti, :], ht_d[t].rearrange("p (k c) -> p k c", k=8), [R_ht[t]], [hb])
                    hbk = hb.ap.rearrange("p (k n) -> p k n", k=8)
                    av = v3(a_.ap, 6)
                    for hc in range(nh):
                        pg, pu = (hc % 2) * 2, (hc % 2) * 2 + 1
                        for kc in range(8):
                            mm(psum[pg][:, 0:ntk], wgv[:, kc, hc * 128:(hc + 1) * 128], hbk[:, kc, 0:ntk], kc == 0, kc == 7,
                               [wg, hb], [PS[pg]])
                        for kc in range(8):
                            mm(psum[pu][:, 0:ntk], wuv[:, kc, hc * 128:(hc + 1) * 128], hbk[:, kc, 0:ntk], kc == 0, kc == 7,
                               [wu, hb], [PS[pu]])
                        s_ = sl[hc % 2]
                        act(s_.ap[:, 0:ntk], psum[pg][:, 0:ntk], AF.Silu, [PS[pg]], [s_])
                        tt("dve", av[:, hc, 0:ntk], s_.ap[:, 0:ntk], psum[pu][:, 0:ntk], ALU.mult, [s_, PS[pu]], [a_])
                    for ti, t in enumerate(tiles):
                        j = 1 if t >= 16 else 0
                        tx = tmpx[ti % 2]
                        for nch in range(2):
                            pb = 4 + (ti % 2) * 2 + nch
                            for hc in range(nh):
                                mm(psum[pb][:], av[:, hc, ti * 128:(ti + 1) * 128], wov[:, hc, nch * 512:(nch + 1) * 512],
                                   hc == 0, hc == nh - 1, [a_, wo], [PS[pb]])
                            tt("dve", tx.ap[:, nch * 512:(nch + 1) * 512], psum[pb][:],
                               gts[j].ap[:, nch * 512:(nch + 1) * 512], ALU.mult, [PS[pb], gts[j]], [tx])
                        tt("pool", Xv[:, t, :], Xv[:, t, :], tx.ap, ALU.add, [XR[t], tx], [XR[t]])
                for (b, _) in (wf[pi][0], wf[pi][1], wf[pi][2]):
                    b.free()
            for b in gts + hb4 + aT + sl + tmpx:
                b.free()

        for i in range(16):
            dma("sp", out_d[i * 128:(i + 1) * 128, :], Xv[:, i, :], [XR[i]], [R_out])
        P.emit()
        _CACHE["peak"] = A.peak
        _CACHE["nops"] = len(P.ops)
    return nc


def _host_consts():
    if "consts" in _CACHE:
        return _CACHE["consts"]
    cosa, sina = _rope_tables(64)
    cosb, sinb = _rope_tables(32)
    cn, sn, c256, s256, c128, ns128 = _dft_tables()
    c = dict(identf=np.eye(128, dtype=np.float32),
             cosa=cosa.reshape(128, -1), sina=sina.reshape(128, -1),
             cosb=cosb.reshape(128, -1), sinb=sinb.reshape(128, -1),
             cn=cn.reshape(8, 128, -1), sn=sn.reshape(8, 128, -1),
             c256=c256.reshape(128, -1), s256=s256.reshape(128, -1), c128=c128, ns128=ns128)
    _CACHE["consts"] = c
    return c


def kernel(x, c, ctx, c_ctx, w_mod, b_mod, g_norm1, g_norm2, w_in, g_q_gqa, g_k_gqa, g_cq, g_ckv, w_uq, w_ukv,
           g_q_nope, g_k_nope, g_q_rope, g_k_rope, b_gate, w_br_a, w_br_b, w_br_c, w_out, w_ffn_in, w_ffn_out):
    f = lambda a: np.ascontiguousarray(np.asarray(a, dtype=np.float32))
    x, c, ctx, c_ctx = f(x), f(c), f(ctx), f(c_ctx)
    B = x.shape[0]
    if "nc" not in _CACHE:
        _CACHE["nc"] = build_nc()
    nc = _CACHE["nc"]
    consts = _host_consts()
    b_mod = f(b_mod)
    vrow = np.zeros((DEPTH, RV_W), np.float32)
    for name, arr in (("gqa_q", g_q_gqa), ("gqa_k", g_k_gqa), ("q_nope", g_q_nope), ("k_nope", g_k_nope),
                      ("q_rope", g_q_rope), ("k_rope", g_k_rope), ("b_gate", b_gate)):
        o, w = RV[name]
        vrow[:, o:o + w] = f(arr)
    vrow[:, RV["b_gt1"][0]:RV["b_gt1"][0] + D] = b_mod[:, 2 * D:3 * D]
    vrow[:, RV["b_gt2"][0]:RV["b_gt2"][0] + D] = b_mod[:, 5 * D:6 * D]
    vfm = np.zeros((128, DEPTH, FM_W), np.float32)

    def fmaj(v):
        L = v.shape[0]
        return f(v).reshape(L, -1, 128).transpose(2, 0, 1)

    for name, arr in (("g1", g_norm1), ("g2", g_norm2), ("bmod", b_mod), ("gcq", g_cq), ("gckv", g_ckv)):
        o, w = FM[name]
        vfm[:, :, o:o + w] = fmaj(arr)
    vfm = np.ascontiguousarray(vfm.reshape(128, DEPTH * FM_W))
    shared = dict(w_mod=f(w_mod), w_in=f(w_in), w_uq=f(w_uq), w_ukv=f(w_ukv), w_br_a=f(w_br_a), w_br_b=f(w_br_b),
                  w_br_c=f(w_br_c), w_out=f(w_out), w_ffn_in=f(w_ffn_in), w_ffn_out=f(w_ffn_out), vrow=vrow, vfm=vfm)
    shared.update(consts)
    in_maps = []
    for b in range(B):
        c2 = np.stack([c[b], c_ctx], axis=-1).reshape(8, 128, 2).transpose(1, 0, 2).reshape(128, 16)
        m = dict(shared)
        m.update(x=x[b], ctx=ctx[b], c2=np.ascontiguousarray(c2))
        in_maps.append(m)
    res = run_bass_kernel_spmd(nc, in_maps, core_ids=list(range(B)))
    return np.stack([np.asarray(r["out"], dtype=np.float32) for r in res.results], axis=0)
```

```python
import contextlib
import numpy as np
import ml_dtypes
import concourse.bass as bass
import concourse.mybir as mybir
from concourse.bass_utils import run_bass_kernel_spmd

F32 = mybir.dt.float32
BF16 = mybir.dt.bfloat16
AF = mybir.ActivationFunctionType
ALU = mybir.AluOpType
AX = mybir.AxisListType

D = 1024
NLAT = 2048
NCTX = 256
NT = 18
DEPTH = 2
EPS = 1e-6
FFN_H = 2816
IN_W = 5024
GQA_SCALE = 64 ** -0.5
MLA_SCALE = 96 ** -0.5
N_DMA_SEMS = 64
TRUST = {"pe", "act", "dve", "dma"}
ARENA_BYTES = 207 * 1024


class Res:
    __slots__ = ("name", "last_w", "readers", "excl")

    def __init__(self, name, excl=False):
        self.name = name
        self.last_w = None
        self.readers = []
        self.excl = excl


class Op:
    __slots__ = ("id", "eng", "fn", "deps", "dma", "pos", "waits", "signals", "sigval", "dsem", "dval")

    def __init__(self, id, eng, fn, deps, dma):
        self.id = id
        self.eng = eng
        self.fn = fn
        self.deps = deps
        self.dma = dma
        self.pos = -1
        self.waits = []
        self.signals = False
        self.sigval = 0
        self.dsem = -1
        self.dval = 0


class Prog:
    ENGS = ("pe", "act", "dve", "pool", "sp")

    def __init__(self, nc):
        self.nc = nc
        self.ops = []
        self.dma_count = 0
        self.dma_count_sw = 0
        self.last_pool = None
        self.dma_last_on_sem = {}

    def op(self, eng, fn, reads=(), writes=(), dma=False):
        deps = {}
        for w in writes:
            if w.last_w is not None:
                deps[w.last_w] = False
            for rr in w.readers:
                deps[rr] = False
        for r in reads:
            if r.excl:
                for rr in r.readers:
                    deps.setdefault(rr, False)
            if r.last_w is not None:
                deps[r.last_w] = True
        if eng == "pool" and not dma:
            if self.last_pool is not None:
                deps[self.last_pool] = True
        o = Op(len(self.ops), eng, fn, deps, dma)
        if eng == "pool" and not dma:
            self.last_pool = o.id
        if dma:
            half = N_DMA_SEMS // 2
            if eng == "pool":
                s = self.dma_count_sw % half
                self.dma_count_sw += 1
            else:
                s = half + self.dma_count % half
                self.dma_count += 1
            prev = self.dma_last_on_sem.get(s)
            o.dsem = s
            if prev is not None:
                o.deps.setdefault(prev.id, False)
                o.dval = prev.dval + 16
            else:
                o.dval = 16
            self.dma_last_on_sem[s] = o
        for r in reads:
            if eng == "pe":
                r.readers = [x for x in r.readers if self.ops[x].eng != "pe"]
            r.readers.append(o.id)
        for w in writes:
            w.last_w = o.id
            w.readers = []
        self.ops.append(o)
        return o

    def schedule(self):
        ops = self.ops
        per_eng = {e: [] for e in self.ENGS}
        for o in ops:
            o.pos = len(per_eng[o.eng])
            per_eng[o.eng].append(o)
        known = {e: {} for e in self.ENGS}
        snap = {}
        for o in ops:
            k = known[o.eng]
            for d in sorted(o.deps, reverse=True):
                dop = ops[d]
                if dop.dma:
                    key = ("d", dop.dsem)
                    need = dop.dval
                else:
                    if dop.eng == o.eng and (o.eng == "pe" or not o.deps[d]):
                        continue
                    key = dop.eng
                    need = dop.pos + 1
                if k.get(key, 0) >= need and ("dma" if dop.dma else dop.eng) in TRUST:
                    continue
                o.waits.append(d)
                dop.signals = True
                ks = snap.get(d)
                if ks is not None:
                    for kk, vv in ks.items():
                        if k.get(kk, 0) < vv:
                            k[kk] = vv
                if k.get(key, 0) < need:
                    k[key] = need
            snap[o.id] = dict(k)
        for e in self.ENGS:
            c = 0
            for o in per_eng[e]:
                if o.dma:
                    continue
                if o.signals:
                    c += 1
                    o.sigval = c
        self.per_eng = per_eng

    def emit(self):
        nc = self.nc
        self.schedule()
        ops = self.ops
        with contextlib.ExitStack() as st:
            tsem = {e: st.enter_context(nc.semaphore("ts_" + e)) for e in ("pe", "act", "dve", "pool")}
            dsems = [st.enter_context(nc.semaphore("dm_%d" % i)) for i in range(N_DMA_SEMS)]
            block = st.enter_context(nc.Block())

            def run(engname):
                def body(eng):
                    for o in self.per_eng[engname]:
                        for d in o.waits:
                            dop = ops[d]
                            if dop.dma:
                                eng.wait_ge(dsems[dop.dsem], dop.dval)
                            else:
                                eng.wait_ge(tsem[dop.eng], dop.sigval)
                        ins = o.fn(eng)
                        if o.dma:
                            ins.then_inc(dsems[o.dsem], 16)
                        elif o.signals:
                            ins.then_inc(tsem[o.eng], 1)
                    for s, lo in self.dma_last_on_sem.items():
                        if lo.eng == engname:
                            eng.wait_ge(dsems[s], lo.dval)
                return body

            block.tensor(run("pe"))
            block.scalar(run("act"))
            block.vector(run("dve"))
            block.gpsimd(run("pool"))
            block.sync(run("sp"))


class Arena:
    def __init__(self, tensor, nbytes):
        self.t = tensor
        self.size = nbytes
        self.live = []
        self.ghosts = []
        self.peak = 0

    def alloc(self, name, nbytes, dtype=BF16, nres=1):
        req = nbytes
        nbytes = (nbytes + 63) // 64 * 64
        off = 0
        for (o, s, _) in sorted(self.live, key=lambda b: b[0]):
            if off + nbytes <= o:
                break
            off = max(off, o + s)
        if off + nbytes > self.size:
            raise RuntimeError("arena full allocating %s (%d B); live=%s" % (
                name, nbytes, [(r[2][0].name, r[1]) for r in self.live]))
        self.peak = max(self.peak, off + nbytes)
        ids = []
        keep = []
        for (go, gs, gids) in self.ghosts:
            if go < off + nbytes and off < go + gs:
                ids.extend(gids)
                if go >= off and go + gs <= off + nbytes:
                    continue
            keep.append((go, gs, gids))
        self.ghosts = keep
        ids = sorted(set(ids))
        res = []
        for i in range(nres):
            r = Res("%s_%d" % (name, i))
            r.readers = list(ids)
            res.append(r)
        self.live.append((off, nbytes, res))
        ap = self.t[:, off // 2:(off + req) // 2]
        if dtype == F32:
            ap = ap.bitcast(F32)
        return Buf(ap, res, off, self)

    def free(self, buf):
        for i, (o, s, res) in enumerate(self.live):
            if o == buf.off and res is buf.res:
                ids = []
                for r in res:
                    ids.extend(r.readers)
                    if r.last_w is not None:
                        ids.append(r.last_w)
                self.ghosts.append((o, s, sorted(set(ids))))
                del self.live[i]
                return
        raise RuntimeError("free of unknown buffer")


class Buf:
    def __init__(self, ap, res, off, arena):
        self.ap = ap
        self.res = res
        self.off = off
        self.arena = arena

    @property
    def r(self):
        return self.res[0]

    def free(self):
        self.arena.free(self)


def interleave(factories, width=2):
    pending = list(factories)
    free_slots = list(range(width))
    active = []
    while pending or active:
        while len(active) < width and pending:
            sl = free_slots.pop(0)
            active.append((pending.pop(0)(sl), sl))
        for g, sl in list(active):
            try:
                next(g)
            except StopIteration:
                active.remove((g, sl))
                free_slots.append(sl)


def _rope_tables(dim):
    half = dim // 2
    freqs = 10000.0 ** (-np.arange(0, half, 2, dtype=np.float32) / half)
    t = np.arange(NLAT)
    row = (t // 64).astype(np.float32)
    col = (t % 64).astype(np.float32)

    def ax(pos):
        a = pos[:, None] * freqs[None, :]
        return np.concatenate([a, a], axis=-1)

    ang = np.concatenate([ax(row), ax(col)], axis=-1).astype(np.float32)
    cos = np.cos(ang).astype(np.float32)
    sin = np.sin(ang).astype(np.float32)
    q = dim // 4
    sgn = np.ones((dim,), np.float32)
    for a in range(2):
        sgn[a * 2 * q: a * 2 * q + q] = -1.0
    sins = sin * sgn[None, :]
    cos_f = np.ones((NT * 128, dim), np.float32)
    sin_f = np.zeros((NT * 128, dim), np.float32)
    cos_f[:NLAT] = cos
    sin_f[:NLAT] = sins
    cos_t = cos_f.reshape(NT, 128, dim).transpose(1, 0, 2)
    sin_t = sin_f.reshape(NT, 128, dim).transpose(1, 0, 2)
    return np.ascontiguousarray(cos_t), np.ascontiguousarray(sin_t)


def _dft_tables():
    bf = ml_dtypes.bfloat16
    n = np.arange(NLAT, dtype=np.int64)
    ph = (np.outer(n, n) % NLAT).astype(np.float64) * (2.0 * np.pi / NLAT)
    cn = (np.cos(ph) / np.sqrt(NLAT)).astype(np.float32)
    sn = (np.sin(ph) / np.sqrt(NLAT)).astype(np.float32)

    def lay(m):
        return np.ascontiguousarray(m.reshape(16, 128, 8, 256).transpose(2, 1, 0, 3)).astype(bf)

    n2 = np.arange(NCTX, dtype=np.int64)
    ph2 = (np.outer(n2, n2) % NCTX).astype(np.float64) * (2.0 * np.pi / NCTX)
    c256 = (np.cos(ph2) / np.sqrt(NCTX)).astype(np.float32)
    s256 = (np.sin(ph2) / np.sqrt(NCTX)).astype(np.float32)

    def lay2(m):
        return np.ascontiguousarray(m.reshape(2, 128, 256).transpose(1, 0, 2)).astype(bf)

    n3 = np.arange(128, dtype=np.int64)
    ph3 = (np.outer(n3, n3) % 128).astype(np.float64) * (2.0 * np.pi / 128)
    c128 = (np.cos(ph3) / np.sqrt(128.0)).astype(np.float32).astype(bf)
    ns128 = (-np.sin(ph3) / np.sqrt(128.0)).astype(np.float32).astype(bf)
    return lay(cn), lay(sn), lay2(c256), lay2(s256), c128, ns128


RV = {}
_o = 0
for _n, _w in (("gqa_q", 64), ("gqa_k", 64), ("q_nope", 64), ("k_nope", 64), ("q_rope", 32), ("k_rope", 32),
               ("b_gate", 3072), ("b_gt1", 1024), ("b_gt2", 1024)):
    RV[_n] = (_o, _w)
    _o += _w
RV_W = _o
FM = {"g1": (0, 8), "g2": (8, 8), "bmod": (16, 48), "gcq": (64, 3), "gckv": (67, 2)}
FM_W = 69

_CACHE = {}


def build_nc(depth=DEPTH, dbg=()):
    nc = bass.Bass("TRN2", target_bir_lowering=False)
    din = lambda n, s, dt=F32: nc.dram_tensor(n, list(s), dt, kind="ExternalInput").ap()
    x_d = din("x", [NLAT, D])
    ctx_d = din("ctx", [NCTX, D])
    c2_d = din("c2", [128, 16])
    w_mod_d = din("w_mod", [DEPTH, D, 6 * D])
    w_in_d = din("w_in", [DEPTH, D, IN_W])
    w_uq_d = din("w_uq", [DEPTH, 384, 768])
    w_ukv_d = din("w_ukv", [DEPTH, 256, 1024])
    w_br_d = [din("w_br_" + s, [DEPTH, 512, D]) for s in "abc"]
    w_out_d = din("w_out", [DEPTH, D, D])
    w_fi_d = din("w_ffn_in", [DEPTH, D, 2 * FFN_H])
    w_fo_d = din("w_ffn_out", [DEPTH, FFN_H, D])
    vrow_d = din("vrow", [DEPTH, RV_W])
    vfm_d = din("vfm", [128, DEPTH * FM_W])
    identf_d = din("identf", [128, 128])
    cosa_d = din("cosa", [128, NT * 64])
    sina_d = din("sina", [128, NT * 64])
    cosb_d = din("cosb", [128, NT * 32])
    sinb_d = din("sinb", [128, NT * 32])
    cn_d = din("cn", [8, 128, 16 * 256], BF16)
    sn_d = din("sn", [8, 128, 16 * 256], BF16)
    c256_d = din("c256", [128, 512], BF16)
    s256_d = din("s256", [128, 512], BF16)
    c128_d = din("c128", [128, 128], BF16)
    ns128_d = din("ns128", [128, 128], BF16)
    out_d = nc.dram_tensor("out", [NLAT, D], F32, kind="ExternalOutput").ap()
    ht_d = nc.dram_tensor("ht_scr", [NT, 128, 1024], BF16).ap()
    y_d = [nc.dram_tensor("y_scr%d" % i, [128, 4, NT * 128], BF16).ap() for i in range(3)]
    dbg_d = {}

    with contextlib.ExitStack() as st:
        arena_t = st.enter_context(nc.sbuf_tensor("arena", [128, ARENA_BYTES // 2], BF16))
        psum = [st.enter_context(nc.psum_tensor("ps%d" % i, [128, 512], F32)) for i in range(8)]
        P = Prog(nc)
        A = Arena(arena_t, ARENA_BYTES)
        PS = [Res("ps%d" % i, excl=True) for i in range(8)]
        R_ht = [Res("ht%d" % i) for i in range(NT)]
        R_y = [Res("y%d" % i) for i in range(3)]
        R_out = Res("out")

        def psb(i):
            return psum[i][:].bitcast(BF16)

        def rl(xs):
            out = []
            for v in xs:
                if isinstance(v, Buf):
                    out.extend(v.res)
                elif isinstance(v, Res):
                    out.append(v)
                elif v is not None:
                    out.extend(v)
            return out

        def dma(eng, out, in_, reads, writes):
            P.op(eng, lambda e: e.dma_start(out=out, in_=in_), rl(reads), rl(writes), dma=True)

        def mm(out, lhsT, rhs, start, stop, reads, writes):
            P.op("pe", lambda e: e.matmul(out, lhsT, rhs, start=start, stop=stop), rl(reads), rl(writes))

        def tr(out, in_, ident, reads, writes):
            P.op("pe", lambda e: e.transpose(out, in_, ident), rl(reads), rl(writes))

        def act(out, in_, func, reads, writes, bias=0.0, scale=1.0, accum=None):
            if accum is None:
                P.op("act", lambda e: e.activation(out, in_, func, bias=bias, scale=scale), rl(reads), rl(writes))
            else:
                P.op("act", lambda e: e.activation(out, in_, func, bias=bias, scale=scale, accum_out=accum),
                     rl(reads), rl(writes))

        def tt(eng, out, in0, in1, op, reads, writes):
            P.op(eng, lambda e: e.tensor_tensor(out, in0, in1, op), rl(reads), rl(writes))

        def ts(eng, out, in0, s1, s2, op0, op1, reads, writes):
            if s2 is None:
                P.op(eng, lambda e: e.tensor_scalar(out, in0, s1, None, op0), rl(reads), rl(writes))
            else:
                P.op(eng, lambda e: e.tensor_scalar(out, in0, s1, s2, op0, op1), rl(reads), rl(writes))

        def cp(eng, out, in_, reads, writes):
            if eng == "act":
                P.op("act", lambda e: e.activation(out, in_, AF.Copy), rl(reads), rl(writes))
            else:
                P.op(eng, lambda e: e.tensor_copy(out, in_), rl(reads), rl(writes))

        def recip(out, in_, reads, writes):
            P.op("dve", lambda e: e.reciprocal(out, in_), rl(reads), rl(writes))

        def red(out, in_, reads, writes):
            P.op("dve", lambda e: e.tensor_reduce(out, in_, AX.X, ALU.add), rl(reads), rl(writes))

        def memset(eng, ap, val, writes):
            P.op(eng, lambda e: e.memset(ap, val), [], rl(writes))

        def v3(ap, a):
            return ap.rearrange("p (a b) -> p a b", a=a)

        def rsqrt_into(dst, ss, n, inv, tmp, reads, writes_tmp, writes_dst):
            act(tmp, ss, AF.Sqrt, reads, writes_tmp, bias=EPS_AP[0], scale=inv)
            recip(dst, tmp, writes_tmp, writes_dst)

        X = A.alloc("X", NT * D * 4, F32, nres=NT)
        Xv = v3(X.ap, NT)
        XR = X.res
        identf = A.alloc("identf", 512, F32)
        identb = A.alloc("identb", 256, BF16)
        c128 = A.alloc("c128", 256, BF16)
        ns128 = A.alloc("ns128", 256, BF16)
        c256 = A.alloc("c256", 1024, BF16)
        s256 = A.alloc("s256", 1024, BF16)
        vfm = A.alloc("vfm", DEPTH * FM_W * 4, F32)
        c2 = A.alloc("c2", 64, F32)
        sc2 = A.alloc("sc2", 32, BF16)
        epsb = A.alloc("eps", 64, F32)
        modT = A.alloc("modT", DEPTH * 48 * 2 * 4, F32)
        amod = A.alloc("amod", DEPTH * 2 * 8 * 2 * 4, F32)
        EPS_AP = [epsb.ap[:, 0:1]]

        for i in range(NLAT // 128):
            dma("sp", Xv[:, i, :], x_d[i * 128:(i + 1) * 128, :], [], [XR[i]])
        for i in range(2):
            dma("sp", Xv[:, 16 + i, :], ctx_d[i * 128:(i + 1) * 128, :], [], [XR[16 + i]])
        dma("sp", identf.ap, identf_d, [], [identf])
        dma("sp", c128.ap, c128_d, [], [c128])
        dma("sp", ns128.ap, ns128_d, [], [ns128])
        dma("sp", c256.ap, c256_d, [], [c256])
        dma("sp", s256.ap, s256_d, [], [s256])
        dma("sp", vfm.ap, vfm_d, [], [vfm])
        dma("sp", c2.ap, c2_d, [], [c2])
        memset("dve", epsb.ap, EPS, [epsb])
        cp("dve", identb.ap, identf.ap, [identf], [identb])
        act(sc2.ap, c2.ap, AF.Silu, [c2], [sc2])
        sc2v = v3(sc2.ap, 8)
        modTv = modT.ap.rearrange("p (l f j) -> p l f j", l=DEPTH, f=48)
        amodv = amod.ap.rearrange("p (l n k j) -> p l n k j", l=DEPTH, n=2, k=8)
        vfmv = v3(vfm.ap, DEPTH)

        def fm(l, name):
            o, w = FM[name]
            return vfmv[:, l, o:o + w]

        def wview(w_ap, l, c0, c1):
            return w_ap[l].rearrange("(kc p) n -> p kc n", p=128)[:, :, c0:c1]

        def load_w(name, w_ap, l, c0, c1, kcs=None):
            K = w_ap.shape[1]
            nk = K // 128
            n = c1 - c0
            b = A.alloc(name, nk * n * 2, BF16)
            bv = v3(b.ap, nk)
            src = wview(w_ap, l, c0, c1)
            for kc in range(nk):
                dma("pool", bv[:, kc, :], src[:, kc, :], [], [b])
            return b, bv

        def load_row(name, l, key, eng="sp"):
            o, w = RV[key]
            b = A.alloc(name, w * 4, F32)
            dma(eng, b.ap, vrow_d[l, o:o + w].partition_broadcast(128), [], [b])
            return b

        MOD_BLKS = (0, 1, 3, 4)

        def mod_load(l, blk):
            return load_w("wmod", w_mod_d, l, blk * D, (blk + 1) * D)

        def mod_compute(l, blk, wb, wbv):
            pv = psum[0][:, 0:16].rearrange("p (f j) -> p f j", f=8)
            for f in range(8):
                for kc in range(8):
                    mm(pv[:, f, :], wbv[:, kc, f * 128:(f + 1) * 128], sc2v[:, kc, :], kc == 0, kc == 7,
                       [wb, sc2], [PS[0]])
            bm = fm(l, "bmod")[:, blk * 8:(blk + 1) * 8]
            tt("dve", modTv[:, l, blk * 8:(blk + 1) * 8, :], pv, bm.unsqueeze(2).to_broadcast([128, 8, 2]), ALU.add,
               [PS[0], vfm], [modT])
            wb.free()

        def mod_finish(l):
            for n, (blk, g) in enumerate(((1, "g1"), (4, "g2"))):
                P.op("dve", lambda e, n=n, blk=blk, g=g: e.scalar_tensor_tensor(
                    out=amodv[:, l, n, :, :], in0=modTv[:, l, blk * 8:(blk + 1) * 8, :], scalar=1.0,
                    in1=fm(l, g).unsqueeze(2).to_broadcast([128, 8, 2]), op0=ALU.add, op1=ALU.mult),
                    rl([modT, vfm]), rl([amod]))

        def compute_mod_fm(l):
            for blk in MOD_BLKS:
                wb, wbv = mod_load(l, blk)
                mod_compute(l, blk, wb, wbv)
            mod_finish(l)

        def compute_gate_rows(l, which, need_ctx):
            blk = 2 if which == 0 else 5
            wb, wbv = load_w("wmodg", w_mod_d, l, blk * D, (blk + 1) * D)
            brow = load_row("bgt", l, "b_gt1" if which == 0 else "b_gt2")
            scb = A.alloc("scb", 8 * 2 * 128 * 2, BF16)
            scbv = scb.ap.rearrange("p (k j m) -> p k j m", k=8, j=2)
            cp("dve", scbv, sc2v.unsqueeze(3).to_broadcast([128, 8, 2, 128]), [sc2], [scb])
            outs = []
            for j in range(2 if need_ctx else 1):
                g = A.alloc("gt%d" % j, 4096, F32)
                for nch in range(2):
                    for kc in range(8):
                        mm(psum[nch][:], scbv[:, kc, j, :], wbv[:, kc, nch * 512:(nch + 1) * 512], kc == 0, kc == 7,
                           [scb, wb], [PS[nch]])
                    tt("dve", g.ap[:, nch * 512:(nch + 1) * 512], psum[nch][:], brow.ap[:, nch * 512:(nch + 1) * 512],
                       ALU.add, [PS[nch], brow], [g])
                outs.append(g)
            wb.free()
            brow.free()
            scb.free()
            return outs

        def build_ht(l, n, tiles):
            xn = [A.alloc("xn%d" % i, 4096, F32) for i in range(2)]
            junk = [A.alloc("junk%d" % i, 2048, BF16) for i in range(2)]
            hts = [A.alloc("hts%d" % i, 2048, BF16) for i in range(2)]
            sts = [A.alloc("nstat%d" % i, 64, F32) for i in range(2)]

            def tile_gen(t):
                def gen(slot):
                    j = 1 if t >= 16 else 0
                    xb = xn[slot]
                    hb = hts[slot]
                    st_ = sts[slot]
                    hbv = v3(hb.ap, 8)
                    sv = st_.ap
                    act(junk[slot].ap, Xv[:, t, :], AF.Square, [XR[t]], [junk[slot], st_], accum=sv[:, 0:1])
                    act(sv[:, 1:2], sv[:, 0:1], AF.Sqrt, [st_], [st_], bias=EPS_AP[0], scale=1.0 / D)
                    yield
                    recip(sv[:, 2:3], sv[:, 1:2], [st_], [st_])
                    ts("dve", xb.ap, Xv[:, t, :], sv[:, 2:3], None, ALU.mult, None, [XR[t], st_], [xb])
                    yield
                    for half in range(2):
                        pb = 4 + slot * 2 + half
                        pv = v3(psum[pb][:], 4)
                        for c in range(4):
                            kc = half * 4 + c
                            tr(pv[:, c, :], xb.ap[:, kc * 128:(kc + 1) * 128], identf.ap, [xb, identf], [PS[pb]])
                    yield
                    for half in range(2):
                        pb = 4 + slot * 2 + half
                        pv = v3(psum[pb][:], 4)
                        for c in range(4):
                            kc = half * 4 + c
                            a_ap = amodv[:, l, n, kc, j:j + 1]
                            b_ap = modTv[:, l, (0 if n == 0 else 24) + kc, j:j + 1]
                            if c % 2 == 0:
                                act(hbv[:, kc, :], pv[:, c, :], AF.Identity, [PS[pb], amod, modT], [hb], bias=b_ap,
                                    scale=a_ap)
                            else:
                                ts("dve", hbv[:, kc, :], pv[:, c, :], a_ap, b_ap, ALU.mult, ALU.add,
                                   [PS[pb], amod, modT], [hb])
                    dma("sp", ht_d[t], hb.ap, [hb], [R_ht[t]])
                    yield
                return gen

            interleave([tile_gen(t) for t in tiles])
            for b in xn + hts + junk + sts:
                b.free()

        def headnorm(src, nh, hd, stride, off, gain, cos, sin, out, reads_src, S, stat, wr_out):
            sv = v3(src[:, 0:nh * stride], nh)[:, :, off:off + hd]
            n = nh * hd
            s1 = v3(S[0].ap[:, 0:n], nh)
            s2 = v3(S[1].ap[:, 0:n], nh)
            s3 = v3(S[2].ap[:, 0:n], nh)
            stv = stat.ap
            act(s1, sv, AF.Square, reads_src, [S[0]])
            yield
            red(stv[:, 0:nh], s1, [S[0]], [stat])
            yield
            act(stv[:, 16:16 + nh], stv[:, 0:nh], AF.Sqrt, [stat], [stat], bias=EPS_AP[0], scale=1.0 / hd)
            yield
            recip(stv[:, 32:32 + nh], stv[:, 16:16 + nh], [stat], [stat])
            tt("dve", s2, sv, stv[:, 32:32 + nh].unsqueeze(2).to_broadcast([128, nh, hd]), ALU.mult,
               list(reads_src) + [stat], [S[1]])
            gb = gain.ap.unsqueeze(1).to_broadcast([128, nh, hd])
            if cos is None:
                tt("dve", out, s2, gb, ALU.mult, [S[1], gain], wr_out)
                yield
                return
            tt("dve", s1, s2, gb, ALU.mult, [S[1], gain], [S[0]])
            yield
            q = hd // 4
            cb = cos.unsqueeze(1).to_broadcast([128, nh, hd])
            tt("dve", s2, s1, cb, ALU.mult, [S[0], ROPE], [S[1]])
            x5 = s1.rearrange("p h (a b f) -> p h a b f", a=2, b=2)
            u5 = s3.rearrange("p h (a b f) -> p h a b f", a=2, b=2)
            sn5 = sin.unsqueeze(1).to_broadcast([128, nh, hd]).rearrange("p h (a b f) -> p h a b f", a=2, b=2)
            for part in range(2):
                tt("dve", u5[:, :, :, part, :], x5[:, :, :, 1 - part, :], sn5[:, :, :, part, :], ALU.mult,
                   [S[0], ROPE], [S[2]])
            yield
            tt("dve", out, s2, s3, ALU.add, [S[1], S[2]], wr_out)
            yield

        ROPE = Res("rope")

        def attention(units, kt_list, qcols, nq, scale, YT):
            n = len(kt_list)
            items = [(ui, ki) for ui in range(len(units)) for ki in range(n)]
            LOOK, NSB = 3, 5

            def score(i):
                ui, ki = items[i]
                u = units[ui]
                kt = kt_list[ki]
                sb = i % NSB
                mm(psum[sb][:, 0:nq], u["kT"][:, kt * 128:(kt + 1) * 128], u["qT"][:, qcols:qcols + nq], True, True,
                   u["reads"], [PS[sb]])

            for i in range(min(LOOK, len(items))):
                score(i)
            for i, (ui, ki) in enumerate(items):
                if i + LOOK < len(items):
                    score(i + LOOK)
                u = units[ui]
                kt = kt_list[ki]
                sb = i % NSB
                pt = PT[i % 4]
                ob_ = 5 + ui % 3
                act(pt.ap[:, 0:nq], psum[sb][:, 0:nq], AF.Exp, [PS[sb]], [pt], scale=scale)
                mm(psum[ob_][:, 0:nq], u["v"](kt), pt.ap[:, 0:nq], ki == 0, ki == n - 1, [pt] + u["vreads"], [PS[ob_]])
                if ki == n - 1:
                    ob = u["ob"]
                    rc = REC[ui % 2]
                    recip(rc.ap[0:64, 0:nq], psum[ob_][64:128, 0:nq], [PS[ob_]], [rc])
                    tt("dve", YT[ob:ob + 64, u["ych"], qcols:qcols + nq], psum[ob_][0:64, 0:nq], rc.ap[0:64, 0:nq],
                       ALU.mult, [PS[ob_], rc], u["ywr"])

        for l in range(depth):
            last = (l == depth - 1)
            act_tiles = list(range(16)) if last else list(range(NT))
            if l == 0:
                compute_mod_fm(0)
            build_ht(l, 0, list(range(NT)))
            SS = [[A.alloc("S%d_%d" % (sl, i), 2048, F32) for i in range(3)] for sl in range(2)]
            stats = [A.alloc("stat%d" % sl, 256, F32) for sl in range(2)]
            hbuf = [A.alloc("hbuf%d" % i, 2048, BF16) for i in range(2)]
            PT = [A.alloc("PT%d" % i, 1024, BF16) for i in range(4)]
            REC = [A.alloc("REC%d" % i, 2048, F32) for i in range(2)]

            def load_ht(slot, t):
                hb = hbuf[slot]
                dma("sp", hb.ap, ht_d[t], [R_ht[t]], [hb])
                return hb, v3(hb.ap, 8)

            YT = A.alloc("YT", 4 * NT * 128 * 2, BF16)
            YTv = v3(YT.ap, 4)
            jobs = [(qc * 512, 512, list(range(NT))) for qc in range(4)]
            if not last:
                jobs.append((2048, 256, [16, 17]))

            wB, wBv = load_w("wB", w_in_d, l, 768, 1440)
            wuq, wuqv = load_w("wuq", w_uq_d, l, 0, 768)
            wukv, wukvv = load_w("wukv", w_ukv_d, l, 0, 1024)
            gqn = load_row("gqn", l, "q_nope")
            gkn = load_row("gkn", l, "k_nope")
            gqr = load_row("gqr", l, "q_rope")
            gkr = load_row("gkr", l, "k_rope")
            cosb = A.alloc("cosb", NT * 32 * 4, F32)
            sinb = A.alloc("sinb", NT * 32 * 4, F32)
            dma("sp", cosb.ap, cosb_d, [], [cosb, ROPE])
            dma("sp", sinb.ap, sinb_d, [], [sinb, ROPE])
            cosbv = v3(cosb.ap, NT)
            sinbv = v3(sinb.ap, NT)
            CQT = A.alloc("CQT", 3 * NT * 128 * 2, BF16)
            CKVT = A.alloc("CKVT", 2 * NT * 128 * 2, BF16)
            KR = A.alloc("KR", NT * 32 * 2, BF16)
            CQTv = v3(CQT.ap, 3)
            CKVTv = v3(CKVT.ap, 2)
            KRv = v3(KR.ap, NT)
            cns = [A.alloc("cqn%d" % i, 384 * 2 + 256 * 2, BF16) for i in range(2)]
            gcq = fm(l, "gcq")
            gckv = fm(l, "gckv")

            def b1_tile(t):
                def gen(slot):
                    need_q = t in act_tiles
                    S, stat, cn_ = SS[slot], stats[slot], cns[slot]
                    b0 = slot * 4
                    hb, hbv = load_ht(slot, t)
                    if need_q:
                        for kc in range(8):
                            mm(psum[b0][:, 0:384], hbv[:, kc, :], wBv[:, kc, 0:384], kc == 0, kc == 7, [hb, wB], [PS[b0]])
                    for kc in range(8):
                        mm(psum[b0 + 1][:, 0:288], hbv[:, kc, :], wBv[:, kc, 384:672], kc == 0, kc == 7, [hb, wB],
                           [PS[b0 + 1]])
                    yield
                    sv = stat.ap
                    jk = S[0].ap
                    if need_q:
                        act(jk[:, 0:384], psum[b0][:, 0:384], AF.Square, [PS[b0]], [S[0], stat], accum=sv[:, 48:49])
                        act(sv[:, 51:52], sv[:, 48:49], AF.Sqrt, [stat], [stat], bias=EPS_AP[0], scale=1.0 / 384)
                    act(jk[:, 0:256], psum[b0 + 1][:, 0:256], AF.Square, [PS[b0 + 1]], [S[0], stat], accum=sv[:, 49:50])
                    act(sv[:, 52:53], sv[:, 49:50], AF.Sqrt, [stat], [stat], bias=EPS_AP[0], scale=1.0 / 256)
                    if not need_q:
                        memset("dve", sv[:, 51:52], 1.0, [stat])
                    yield
                    recip(sv[:, 54:56], sv[:, 51:53], [stat], [stat])
                    yield
                    if need_q:
                        act(cn_.ap[:, 0:384], psum[b0][:, 0:384], AF.Copy, [PS[b0], stat], [cn_], scale=sv[:, 54:55])
                    act(cn_.ap[:, 384:640], psum[b0 + 1][:, 0:256], AF.Copy, [PS[b0 + 1], stat], [cn_], scale=sv[:, 55:56])
                    yield
                    pv = v3(psb(b0 + 2)[:, 0:640], 5)
                    for c in range(5):
                        if c < 3 and not need_q:
                            continue
                        tr(pv[:, c, :], cn_.ap[:, c * 128:(c + 1) * 128], identb.ap, [cn_, identb], [PS[b0 + 2]])
                    yield from headnorm(psum[b0 + 1][:, 256:288], 1, 32, 32, 0, gkr, cosbv[:, t, :], sinbv[:, t, :],
                                        KRv[:, t, :].unsqueeze(1), [PS[b0 + 1]], S, stat, [KR])
                    for c in range(5):
                        if c < 3 and not need_q:
                            continue
                        dst = CQTv[:, c, t * 128:(t + 1) * 128] if c < 3 else CKVTv[:, c - 3, t * 128:(t + 1) * 128]
                        gcol = gcq[:, c:c + 1] if c < 3 else gckv[:, c - 3:c - 2]
                        dbuf = CQT if c < 3 else CKVT
                        if c % 2 == 0:
                            act(dst, pv[:, c, :], AF.Copy, [PS[b0 + 2], vfm], [dbuf], scale=gcol)
                        else:
                            ts("dve", dst, pv[:, c, :], gcol, None, ALU.mult, None, [PS[b0 + 2], vfm], [dbuf])
                    yield
                return gen

            interleave([b1_tile(t) for t in range(NT)])
            wB.free()
            for b in cns:
                b.free()
            gkr.free()
            HG = 2
            KBT = A.alloc("KBT", HG * NT * 128 * 2, BF16)
            QBT = A.alloc("QBT", HG * NT * 128 * 2, BF16)
            VB = A.alloc("VB", NT * HG * 128 * 2, BF16)
            KBTv = v3(KBT.ap, HG)
            QBTv = v3(QBT.ap, HG)
            VBv = VB.ap.rearrange("p (t h c) -> p t h c", t=NT, h=HG)
            ktks = [A.alloc("ktk%d" % i, HG * 96 * 2, BF16) for i in range(2)]
            qtks = [A.alloc("qtk%d" % i, HG * 96 * 2, BF16) for i in range(2)]
            memset("pool", VB.ap, 1.0, [VB])
            for hg in range(8 // HG):
                def b2_tile(t, hg=hg):
                    def gen(slot):
                        need_q = t in act_tiles
                        S, stat = SS[slot], stats[slot]
                        ktkv = v3(ktks[slot].ap, HG)
                        qtkv = v3(qtks[slot].ap, HG)
                        ktk, qtk = ktks[slot], qtks[slot]
                        b0 = slot * 4
                        for kc in range(2):
                            mm(psum[b0][:, 0:HG * 128], CKVTv[:, kc, t * 128:(t + 1) * 128],
                               wukvv[:, kc, hg * HG * 128:(hg + 1) * HG * 128], kc == 0, kc == 1, [CKVT, wukv], [PS[b0]])
                        if need_q:
                            for kc in range(3):
                                mm(psum[b0 + 1][:, 0:HG * 96], CQTv[:, kc, t * 128:(t + 1) * 128],
                                   wuqv[:, kc, hg * HG * 96:(hg + 1) * HG * 96], kc == 0, kc == 2, [CQT, wuq], [PS[b0 + 1]])
                        yield
                        yield from headnorm(psum[b0][:], HG, 64, 128, 0, gkn, None, None, ktkv[:, :, 0:64], [PS[b0]],
                                            S, stat, [ktk])
                        cp("dve", ktkv[:, :, 64:96], KRv[:, t, :].unsqueeze(1).to_broadcast([128, HG, 32]), [KR], [ktk])
                        cp("act", VBv[:, t, :, 0:64], v3(psum[b0][:, 0:HG * 128], HG)[:, :, 64:128], [PS[b0]], [VB])
                        yield
                        pv = v3(psb(b0 + 2)[:, 0:HG * 128], HG)
                        for h in range(HG):
                            tr(pv[0:96, h, :], ktkv[:, h, :], identb.ap, [ktk, identb], [PS[b0 + 2]])
                        yield
                        cp("act", KBTv[0:96, :, t * 128:(t + 1) * 128], pv[0:96, :, :], [PS[b0 + 2]], [KBT])
                        if need_q:
                            yield from headnorm(psum[b0 + 1][:], HG, 64, 96, 0, gqn, None, None, qtkv[:, :, 0:64],
                                                [PS[b0 + 1]], S, stat, [qtk])
                            yield from headnorm(psum[b0 + 1][:], HG, 32, 96, 64, gqr, cosbv[:, t, :], sinbv[:, t, :],
                                                qtkv[:, :, 64:96], [PS[b0 + 1]], S, stat, [qtk])
                            pv2 = v3(psb(b0 + 3)[:, 0:HG * 128], HG)
                            for h in range(HG):
                                tr(pv2[0:96, h, :], qtkv[:, h, :], identb.ap, [qtk, identb], [PS[b0 + 3]])
                            yield
                            cp("dve", QBTv[0:96, :, t * 128:(t + 1) * 128], pv2[0:96, :, :], [PS[b0 + 3]], [QBT])
                        yield
                    return gen

                interleave([b2_tile(t) for t in range(NT)])
                for (qcols, nq, kts) in jobs:
                    units = []
                    for h in range(HG):
                        hh = hg * HG + h
                        units.append(dict(kT=KBTv[0:96, h, :], qT=QBTv[0:96, h, :],
                                          v=(lambda kt, h=h: VBv[:, kt, h, :]), reads=[KBT, QBT], vreads=[VB],
                                          ob=64 * (hh % 2), ych=hh // 2, ywr=[YT]))
                    attention(units, kts, qcols, nq, MLA_SCALE, YTv)
            dma("sp", y_d[1], YTv, [YT], [R_y[1]])
            for b in [wuq, wukv, gqn, gkn, gqr, cosb, sinb, CQT, CKVT, KR, KBT, QBT, VB] + ktks + qtks:
                b.free()

            wA, wAv = load_w("wA", w_in_d, l, 0, 768)
            gq = load_row("gq", l, "gqa_q")
            gk = load_row("gk", l, "gqa_k")
            cosa = A.alloc("cosa", NT * 64 * 4, F32)
            sina = A.alloc("sina", NT * 64 * 4, F32)
            dma("sp", cosa.ap, cosa_d, [], [cosa, ROPE])
            dma("sp", sina.ap, sina_d, [], [sina, ROPE])
            cosav = v3(cosa.ap, NT)
            sinav = v3(sina.ap, NT)
            QT = A.alloc("QaT", 4 * NT * 128 * 2, BF16)
            KT = A.alloc("KaT", 4 * NT * 128 * 2, BF16)
            VA = A.alloc("Va", NT * 2 * 128 * 2, BF16)
            QTv = v3(QT.ap, 4)
            KTv = KT.ap.rearrange("p (k r n) -> p k r n", k=2, r=2)
            VAv = VA.ap.rearrange("p (t k c) -> p t k c", t=NT, k=2)
            qtoks = [A.alloc("qtok%d" % i, 1024, BF16) for i in range(2)]
            ktoks = [A.alloc("ktok%d" % i, 512, BF16) for i in range(2)]
            memset("pool", VA.ap, 1.0, [VA])
            memset("pool", KT.ap, 0.0, [KT])

            def a_tile(t):
                def gen(slot):
                    need_q = t in act_tiles
                    S, stat, qtok, ktok = SS[slot], stats[slot], qtoks[slot], ktoks[slot]
                    b0 = slot * 4
                    hb, hbv = load_ht(slot, t)
                    if need_q:
                        for kc in range(8):
                            mm(psum[b0][:], hbv[:, kc, :], wAv[:, kc, 0:512], kc == 0, kc == 7, [hb, wA], [PS[b0]])
                    for kc in range(8):
                        mm(psum[b0 + 1][:, 0:256], hbv[:, kc, :], wAv[:, kc, 512:768], kc == 0, kc == 7, [hb, wA],
                           [PS[b0 + 1]])
                    yield
                    if need_q:
                        yield from headnorm(psum[b0][:], 8, 64, 64, 0, gq, cosav[:, t, :], sinav[:, t, :],
                                            v3(qtok.ap, 8), [PS[b0]], S, stat, [qtok])
                        pv = v3(psb(b0 + 2)[:, 0:512], 4)
                        for c in range(4):
                            tr(pv[:, c, :], qtok.ap[:, c * 128:(c + 1) * 128], identb.ap, [qtok, identb], [PS[b0 + 2]])
                        yield
                        cp("act", QTv[:, :, t * 128:(t + 1) * 128], pv, [PS[b0 + 2]], [QT])
                    k4 = ktok.ap.rearrange("p (k r d) -> p k r d", k=2, r=2)
                    yield from headnorm(psum[b0 + 1][:, 0:128], 2, 64, 64, 0, gk, cosav[:, t, :], sinav[:, t, :],
                                        k4[:, :, 0, :], [PS[b0 + 1]], S, stat, [ktok])
                    cp("dve", k4[:, :, 1, :], k4[:, :, 0, :], [ktok], [ktok])
                    yield
                    pv = v3(psb(b0 + 3)[:, 0:256], 2)
                    for c in range(2):
                        tr(pv[:, c, :], ktok.ap[:, c * 128:(c + 1) * 128], identb.ap, [ktok, identb], [PS[b0 + 3]])
                    yield
                    cp("act", KTv[0:64, :, 0, t * 128:(t + 1) * 128], pv[0:64, :, :], [PS[b0 + 3]], [KT])
                    cp("dve", KTv[64:128, :, 1, t * 128:(t + 1) * 128], pv[64:128, :, :], [PS[b0 + 3]], [KT])
                    cp("act", VAv[:, t, :, 0:64], v3(psum[b0 + 1][:, 128:256], 2), [PS[b0 + 1]], [VA])
                    yield
                return gen

            interleave([a_tile(t) for t in range(NT)])
            wA.free()
            for b in [gq, gk, cosa, sina] + qtoks + ktoks:
                b.free()
            for ji, (qcols, nq, kts) in enumerate(jobs):
                if l + 1 < depth and ji < 4:
                    wbm = mod_load(l + 1, MOD_BLKS[ji])
                units = []
                for p in range(4):
                    kv = p // 2
                    for r in range(2):
                        units.append(dict(kT=KTv[:, kv, r, :], qT=QTv[:, p, :],
                                          v=(lambda kt, kv=kv: VAv[:, kt, kv, :]), reads=[KT, QT], vreads=[VA],
                                          ob=64 * r, ych=p, ywr=[YT]))
                attention(units, kts, qcols, nq, GQA_SCALE, YTv)
                if l + 1 < depth and ji < 4:
                    mod_compute(l + 1, MOD_BLKS[ji], wbm[0], wbm[1])
            if l + 1 < depth:
                mod_finish(l + 1)
            dma("sp", y_d[0], YTv, [YT], [R_y[0]])
            for b in (QT, KT, VA):
                b.free()
            for b in PT + REC + SS[0] + SS[1] + stats:
                b.free()

            gts = compute_gate_rows(l, 0, not last)
            wmrg = [None, None]

            def load_merge_w(half):
                c0 = half * 512
                ws = []
                for i in range(3):
                    ws.append(load_w("wg%d" % i, w_in_d, l, 1952 + i * D + c0, 1952 + i * D + c0 + 512))
                for i in range(3):
                    ws.append(load_w("wbr%d" % i, w_br_d[i], l, c0, c0 + 512))
                b = A.alloc("wo", 4 * 1024 * 2, BF16)
                bv = v3(b.ap, 4)
                src = w_out_d[l].rearrange("(kc p) n -> p kc n", p=128)
                for kc in range(4):
                    dma("pool", bv[:, kc, :], src[:, half * 4 + kc, :], [], [b])
                ws.append((b, bv))
                return ws

            pieces = [(0, 6), (6, 6), (12, 5), (17, 5)]
            def load_ffn_wo(pi):
                h0, nh = pieces[pi]
                b = A.alloc("wfo", nh * 1024 * 2, BF16)
                bv = v3(b.ap, nh)
                src = w_fo_d[l].rearrange("(kc p) n -> p kc n", p=128)
                for kc in range(nh):
                    dma("pool", bv[:, kc, :], src[:, h0 + kc, :], [], [b])
                return (b, bv)

            def load_ffn_w(pi, with_wo=True):
                h0, nh = pieces[pi]
                wg = load_w("wfg", w_fi_d, l, h0 * 128, (h0 + nh) * 128)
                wu = load_w("wfu", w_fi_d, l, FFN_H + h0 * 128, FFN_H + (h0 + nh) * 128)
                return [wg, wu, load_ffn_wo(pi) if with_wo else None]

            wf = [None] * 4
            wC, wCv = load_w("wC", w_in_d, l, 1440, 1952)
            PF = A.alloc("PF", NT * 512 * 2, BF16)
            PFv = v3(PF.ap, NT)
            for it, t in enumerate(range(NT) if not last else range(16)):
                hb, hbv = load_ht(it % 2, t)
                pb = it % 2
                for kc in range(8):
                    mm(psum[pb][:], hbv[:, kc, :], wCv[:, kc, :], kc == 0, kc == 7, [hb, wC], [PS[pb]])
                cp("act" if it % 2 == 0 else "dve", PFv[:, t, :], psum[pb][:], [PS[pb]], [PF])
            wC.free()
            wmrg[0] = load_merge_w(0)
            TB = [[A.alloc("TB%d%d" % (i, j), 16 * 256 * 2, BF16) for j in range(2)] for i in range(2)]
            PQ = [A.alloc("PQ%d" % i, 1024, BF16) for i in range(2)]

            def dft_chunk(tabs, ncs, tile0, ycol0, idx):
                tcv, tsv, tres = tabs
                for g in range(4):
                    u = idx * 4 + g
                    pa, pq_, py = (u % 2) * 3, (u % 2) * 3 + 1, (u % 2) * 3 + 2
                    for nc_ in range(ncs):
                        mm(psum[pa][:, 0:256], PFv[:, tile0 + nc_, g * 128:(g + 1) * 128], tcv[:, nc_, :], nc_ == 0,
                           nc_ == ncs - 1, [PF] + tres, [PS[pa]])
                    for nc_ in range(ncs):
                        mm(psum[pq_][:, 0:256], PFv[:, tile0 + nc_, g * 128:(g + 1) * 128], tsv[:, nc_, :], nc_ == 0,
                           nc_ == ncs - 1, [PF] + tres, [PS[pq_]])
                    pq = PQ[u % 2]
                    cp("act", pq.ap[:, 0:256], psum[pa][:, 0:256], [PS[pa]], [pq])
                    cp("dve", pq.ap[:, 256:512], psum[pq_][:, 0:256], [PS[pq_]], [pq])
                    mm(psum[py][:, 0:256], c128.ap, pq.ap[:, 0:256], True, False, [c128, pq], [PS[py]])
                    mm(psum[py][:, 0:256], ns128.ap, pq.ap[:, 256:512], False, True, [ns128, pq], [PS[py]])
                    cp("act" if g % 2 else "dve", YTv[:, g, ycol0:ycol0 + 256], psum[py][:, 0:256], [PS[py]], [YT])

            for kc2 in range(8):
                tb = TB[kc2 % 2]
                dma("sp", tb[0].ap, cn_d[kc2], [], [tb[0]])
                dma("sp", tb[1].ap, sn_d[kc2], [], [tb[1]])
                dft_chunk((v3(tb[0].ap, 16), v3(tb[1].ap, 16), [tb[0], tb[1]]), 16, 0, kc2 * 256, kc2)
            if not last:
                dft_chunk((v3(c256.ap, 2), v3(s256.ap, 2), [c256, s256]), 2, 16, 2048, 8)
            dma("sp", y_d[2], YTv, [YT], [R_y[2]])
            for b in (PF, TB[0][0], TB[0][1], TB[1][0], TB[1][1], PQ[0], PQ[1], YT):
                b.free()

            ybuf = [A.alloc("ybuf%d" % i, 3 * 4 * 128 * 2, BF16) for i in range(2)]
            sgs = [A.alloc("sg%d" % i, 2048, F32) for i in range(2)]
            macs = [A.alloc("mac%d" % i, 2048, F32) for i in range(2)]
            mtoks = [A.alloc("mtok%d" % i, 1024, BF16) for i in range(2)]
            mT = [A.alloc("mT%d" % i, 1024, BF16) for i in range(2)]
            bg = A.alloc("bg", 3 * 512 * 4, F32)
            for half in range(2):
                if half == 0:
                    wmrg[1] = load_merge_w(1)
                if half == 1:
                    wf[0] = load_ffn_w(0, with_wo=False)
                ws = wmrg[half]
                c0 = half * 512
                for i in range(3):
                    o_ = RV["b_gate"][0] + i * D + half * 512
                    dma("sp", bg.ap[:, i * 512:(i + 1) * 512], vrow_d[l, o_:o_ + 512].partition_broadcast(128), [], [bg])

                def m_tile(t, ws=ws):
                    def gen(slot):
                        j = 1 if t >= 16 else 0
                        b0 = slot * 4
                        s_, mac, mtok, m_ = sgs[slot], macs[slot], mtoks[slot], mT[slot]
                        hb, hbv = load_ht(slot, t)
                        yb = ybuf[slot]
                        ybv = yb.ap.rearrange("p (i k c) -> p i k c", i=3, k=4)
                        for i in range(3):
                            dma("sp", ybv[:, i, :, :], y_d[i][:, :, t * 128:(t + 1) * 128], [R_y[i]], [yb])
                        for i in range(3):
                            wg, wgv = ws[i]
                            wb_, wbv_ = ws[3 + i]
                            for kc in range(8):
                                mm(psum[b0][:], hbv[:, kc, :], wgv[:, kc, :], kc == 0, kc == 7, [hb, wg], [PS[b0]])
                            for kc in range(4):
                                mm(psum[b0 + 1][:], ybv[:, i, kc, :], wbv_[:, kc, :], kc == 0, kc == 3, [yb, wb_],
                                   [PS[b0 + 1]])
                            yield
                            tt("dve", s_.ap, psum[b0][:], bg.ap[:, i * 512:(i + 1) * 512], ALU.add, [PS[b0], bg], [s_])
                            yield
                            act(s_.ap, s_.ap, AF.Sigmoid, [s_], [s_])
                            yield
                            if i == 0:
                                tt("dve", mac.ap, s_.ap, psum[b0 + 1][:], ALU.mult, [s_, PS[b0 + 1]], [mac])
                            else:
                                tt("dve", s_.ap, s_.ap, psum[b0 + 1][:], ALU.mult, [s_, PS[b0 + 1]], [s_])
                                if i == 1:
                                    tt("dve", mac.ap, mac.ap, s_.ap, ALU.add, [mac, s_], [mac])
                                else:
                                    tt("dve", mtok.ap, mac.ap, s_.ap, ALU.add, [mac, s_], [mtok])
                            yield
                        pv = v3(psb(b0 + 2)[:, 0:512], 4)
                        for c in range(4):
                            tr(pv[:, c, :], mtok.ap[:, c * 128:(c + 1) * 128], identb.ap, [mtok, identb], [PS[b0 + 2]])
                        yield
                        cp("act", v3(m_.ap, 4), pv, [PS[b0 + 2]], [m_])
                        yield
                        wo, wov = ws[6]
                        for nch in range(2):
                            pb = b0 + 3
                            for kc in range(4):
                                mm(psum[pb][:], v3(m_.ap, 4)[:, kc, :], wov[:, kc, nch * 512:(nch + 1) * 512], kc == 0,
                                   kc == 3, [m_, wo], [PS[pb]])
                            yield
                            tx = s_ if nch == 0 else mac
                            tt("dve", tx.ap, psum[pb][:], gts[j].ap[:, nch * 512:(nch + 1) * 512],
                               ALU.mult, [PS[pb], gts[j]], [tx])
                            tt("pool", Xv[:, t, nch * 512:(nch + 1) * 512], Xv[:, t, nch * 512:(nch + 1) * 512], tx.ap,
                               ALU.add, [XR[t], tx], [XR[t]])
                            yield
                    return gen

                interleave([m_tile(t) for t in act_tiles])
                for (b, _) in ws:
                    b.free()
            for b in gts + [bg] + ybuf + sgs + macs + mtoks + mT + hbuf:
                b.free()

            build_ht(l, 1, act_tiles)
            gts = compute_gate_rows(l, 1, not last)
            hb4 = [A.alloc("hb4_%d" % i, 8 * 512 * 2, BF16) for i in range(2)]
            aT = [A.alloc("aT%d" % i, 6 * 512 * 2, BF16) for i in range(2)]
            sl = [A.alloc("sl%d" % i, 2048, F32) for i in range(2)]
            tmpx = [A.alloc("tmpy%d" % i, 4096, F32) for i in range(2)]

            chunks = [list(range(c * 4, c * 4 + 4)) for c in range(4)]
            if not last:
                chunks.append([16, 17])
            cnt = 0
            for pi in range(4):
                if wf[pi][2] is None:
                    wf[pi][2] = load_ffn_wo(pi)
                if pi + 1 < 4:
                    wf[pi + 1] = load_ffn_w(pi + 1)
                (wg, wgv), (wu, wuv), (wo, wov) = wf[pi]
                h0, nh = pieces[pi]
                for ci, tiles in enumerate(chunks):
                    ntk = len(tiles) * 128
                    hb = hb4[cnt % 2]
                    a_ = aT[cnt % 2]
                    cnt += 1
                    hbv = hb.ap.rearrange("p (k t c) -> p k t c", k=8, t=4)
                    for ti, t in enumerate(tiles):
                        dma("sp", hbv[:, :, ti, :], ht_d[t].rearrange("p (k c) -> p k c", k=8), [R_ht[t]], [hb])
                    hbk = hb.ap.rearrange("p (k n) -> p k n", k=8)
                    av = v3(a_.ap, 6)
                    for hc in range(nh):
                        pg, pu = (hc % 2) * 2, (hc % 2) * 2 + 1
                        for kc in range(8):
                            mm(psum[pg][:, 0:ntk], wgv[:, kc, hc * 128:(hc + 1) * 128], hbk[:, kc, 0:ntk], kc == 0, kc == 7,
                               [wg, hb], [PS[pg]])
                        for kc in range(8):
                            mm(psum[pu][:, 0:ntk], wuv[:, kc, hc * 128:(hc + 1) * 128], hbk[:, kc, 0:ntk], kc == 0, kc == 7,
                               [wu, hb], [PS[pu]])
                        s_ = sl[hc % 2]
                        act(s_.ap[:, 0:ntk], psum[pg][:, 0:ntk], AF.Silu, [PS[pg]], [s_])
                        tt("dve", av[:, hc, 0:ntk], s_.ap[:, 0:ntk], psum[pu][:, 0:ntk], ALU.mult, [s_, PS[pu]], [a_])
                    for ti, t in enumerate(tiles):
                        j = 1 if t >= 16 else 0
                        tx = tmpx[ti % 2]
                        for nch in range(2):
                            pb = 4 + (ti % 2) * 2 + nch
                            for hc in range(nh):
                                mm(psum[pb][:], av[:, hc, ti * 128:(ti + 1) * 128], wov[:, hc, nch * 512:(nch + 1) * 512],
                                   hc == 0, hc == nh - 1, [a_, wo], [PS[pb]])
                            tt("dve", tx.ap[:, nch * 512:(nch + 1) * 512], psum[pb][:],
                               gts[j].ap[:, nch * 512:(nch + 1) * 512], ALU.mult, [PS[pb], gts[j]], [tx])
                        tt("pool", Xv[:, t, :], Xv[:, t, :], tx.ap, ALU.add, [XR[t], tx], [XR[t]])
                for (b, _) in (wf[pi][0], wf[pi][1], wf[pi][2]):
                    b.free()
            for b in gts + hb4 + aT + sl + tmpx:
                b.free()

        for i in range(16):
            dma("sp", out_d[i * 128:(i + 1) * 128, :], Xv[:, i, :], [XR[i]], [R_out])
        P.emit()
        _CACHE["peak"] = A.peak
        _CACHE["nops"] = len(P.ops)
    return nc


def _host_consts():
    if "consts" in _CACHE:
        return _CACHE["consts"]
    cosa, sina = _rope_tables(64)
    cosb, sinb = _rope_tables(32)
    cn, sn, c256, s256, c128, ns128 = _dft_tables()
    c = dict(identf=np.eye(128, dtype=np.float32),
             cosa=cosa.reshape(128, -1), sina=sina.reshape(128, -1),
             cosb=cosb.reshape(128, -1), sinb=sinb.reshape(128, -1),
             cn=cn.reshape(8, 128, -1), sn=sn.reshape(8, 128, -1),
             c256=c256.reshape(128, -1), s256=s256.reshape(128, -1), c128=c128, ns128=ns128)
    _CACHE["consts"] = c
    return c


def kernel(x, c, ctx, c_ctx, w_mod, b_mod, g_norm1, g_norm2, w_in, g_q_gqa, g_k_gqa, g_cq, g_ckv, w_uq, w_ukv,
           g_q_nope, g_k_nope, g_q_rope, g_k_rope, b_gate, w_br_a, w_br_b, w_br_c, w_out, w_ffn_in, w_ffn_out):
    f = lambda a: np.ascontiguousarray(np.asarray(a, dtype=np.float32))
    x, c, ctx, c_ctx = f(x), f(c), f(ctx), f(c_ctx)
    B = x.shape[0]
    if "nc" not in _CACHE:
        _CACHE["nc"] = build_nc()
    nc = _CACHE["nc"]
    consts = _host_consts()
    b_mod = f(b_mod)
    vrow = np.zeros((DEPTH, RV_W), np.float32)
    for name, arr in (("gqa_q", g_q_gqa), ("gqa_k", g_k_gqa), ("q_nope", g_q_nope), ("k_nope", g_k_nope),
                      ("q_rope", g_q_rope), ("k_rope", g_k_rope), ("b_gate", b_gate)):
        o, w = RV[name]
        vrow[:, o:o + w] = f(arr)
    vrow[:, RV["b_gt1"][0]:RV["b_gt1"][0] + D] = b_mod[:, 2 * D:3 * D]
    vrow[:, RV["b_gt2"][0]:RV["b_gt2"][0] + D] = b_mod[:, 5 * D:6 * D]
    vfm = np.zeros((128, DEPTH, FM_W), np.float32)

    def fmaj(v):
        L = v.shape[0]
        return f(v).reshape(L, -1, 128).transpose(2, 0, 1)

    for name, arr in (("g1", g_norm1), ("g2", g_norm2), ("bmod", b_mod), ("gcq", g_cq), ("gckv", g_ckv)):
        o, w = FM[name]
        vfm[:, :, o:o + w] = fmaj(arr)
    vfm = np.ascontiguousarray(vfm.reshape(128, DEPTH * FM_W))
    shared = dict(w_mod=f(w_mod), w_in=f(w_in), w_uq=f(w_uq), w_ukv=f(w_ukv), w_br_a=f(w_br_a), w_br_b=f(w_br_b),
                  w_br_c=f(w_br_c), w_out=f(w_out), w_ffn_in=f(w_ffn_in), w_ffn_out=f(w_ffn_out), vrow=vrow, vfm=vfm)
    shared.update(consts)
    in_maps = []
    for b in range(B):
        c2 = np.stack([c[b], c_ctx], axis=-1).reshape(8, 128, 2).transpose(1, 0, 2).reshape(128, 16)
        m = dict(shared)
        m.update(x=x[b], ctx=ctx[b], c2=np.ascontiguousarray(c2))
        in_maps.append(m)
    res = run_bass_kernel_spmd(nc, in_maps, core_ids=list(range(B)))
    return np.stack([np.asarray(r["out"], dtype=np.float32) for r in res.results], axis=0)
```
